# Optimizing a Trainium2 kernel written in Bass

```python
import jax, jax.numpy as jnp
from jax import lax
import numpy as np

D_MODEL = 1024
BATCH = 16
SEQ = 4096
DEPTH = 2

CTX_LEN = 256
GRID_W = 64
HEAD_DIM = 64
ATTN_W = D_MODEL // 2
N_HEADS = ATTN_W // HEAD_DIM
Q_PER_KV = 4
KV_HEADS = N_HEADS // Q_PER_KV
KV_W = KV_HEADS * HEAD_DIM
POOL_W = D_MODEL // 2
POOL_WINDOWS = (2, 4, 8, 16)
POOL_GROUPS = len(POOL_WINDOWS)
POOL_GW = POOL_W // POOL_GROUPS
N_BRANCH = 3
BRANCH_W = 512
WINDOW = 128
BLOCK = 128
D_FF = 128 * ((8 * D_MODEL // 3 + 127) // 128)
ROPE_THETA = 10000.0
EPS = 1e-6
NEG = -1e30
SM_SCALE = HEAD_DIM ** -0.5
IN_SPLITS = (POOL_W, ATTN_W, KV_W, KV_W, ATTN_W, KV_W, KV_W)
IN_W = sum(IN_SPLITS)

kernel_name = "hybrid_pool_window_global_gqa_dit"


def rms_norm(x, g):
    xf = x.astype(jnp.float32)
    y = xf * lax.rsqrt(jnp.mean(xf * xf, axis=-1, keepdims=True) + EPS)
    return (y * g.astype(jnp.float32)).astype(x.dtype)


def modulate(x, g, shift, scale):
    return rms_norm(x, g) * (1 + scale) + shift


def heads(t, n):
    return t.reshape(t.shape[:-1] + (n, HEAD_DIM))


def split_in(z):
    offs = np.cumsum((0,) + IN_SPLITS)
    return [z[..., int(offs[i]):int(offs[i + 1])] for i in range(len(IN_SPLITS))]


def axial_rope_tables(n_tok):
    rows = n_tok // GRID_W
    row = jnp.repeat(jnp.arange(rows), GRID_W).astype(jnp.float32)
    col = jnp.tile(jnp.arange(GRID_W), rows).astype(jnp.float32)
    half = HEAD_DIM // 2
    inv = ROPE_THETA ** (-jnp.arange(0, half, 2, dtype=jnp.float32) / half)
    ang = jnp.stack([row[:, None] * inv, col[:, None] * inv], axis=1)
    return jnp.cos(ang)[:, None], jnp.sin(ang)[:, None]


def apply_rope(x, cos, sin):
    shp = x.shape
    xs = x.astype(jnp.float32).reshape(shp[:-1] + (2, 2, HEAD_DIM // 4))
    x1, x2 = xs[..., 0, :], xs[..., 1, :]
    out = jnp.stack([x1 * cos - x2 * sin, x2 * cos + x1 * sin], axis=-2)
    return out.reshape(shp).astype(x.dtype)


def multiscale_pool(u, w_grp, scale):
    Bn, L, _ = u.shape
    uf = u.astype(jnp.float32)
    cs = jnp.concatenate([jnp.zeros_like(uf[:, :1]), jnp.cumsum(uf, axis=1)], axis=1)
    t = jnp.arange(L)
    outs = []
    for gi, w in enumerate(POOL_WINDOWS):
        sl = slice(gi * POOL_GW, (gi + 1) * POOL_GW)
        lo = jnp.clip(t - w // 2, 0, L)
        hi = jnp.clip(t + w // 2, 0, L)
        csg = cs[..., sl]
        s = jnp.take(csg, hi, axis=1) - jnp.take(csg, lo, axis=1)
        cnt = (hi - lo).astype(jnp.float32)[None, :, None]
        outs.append(s / cnt - uf[..., sl])
    p = jnp.stack(outs, axis=2).astype(u.dtype)
    y = jnp.einsum('blgc,gcd->blgd', p, w_grp).reshape(Bn, L, POOL_W)
    return y * scale


def window_attention(q, k, v, kc, vc, sink):
    Bn, L = q.shape[0], q.shape[1]
    nb = L // BLOCK
    pad = ((0, 0), (WINDOW, WINDOW), (0, 0), (0, 0))
    kp = jnp.pad(k, pad).reshape(Bn, nb + 2, BLOCK, KV_HEADS, HEAD_DIM)
    vp = jnp.pad(v, pad).reshape(Bn, nb + 2, BLOCK, KV_HEADS, HEAD_DIM)
    kwin = jnp.concatenate([kp[:, :-2], kp[:, 1:-1], kp[:, 2:]], axis=2).swapaxes(0, 1)
    vwin = jnp.concatenate([vp[:, :-2], vp[:, 1:-1], vp[:, 2:]], axis=2).swapaxes(0, 1)
    qb = q.reshape(Bn, nb, BLOCK, KV_HEADS, Q_PER_KV, HEAD_DIM).swapaxes(0, 1)
    rel = (jnp.arange(3 * BLOCK)[None, :] - BLOCK) - jnp.arange(BLOCK)[:, None]
    band = jnp.abs(rel) <= WINDOW
    sink_l = sink.reshape(1, KV_HEADS, Q_PER_KV, 1, 1).astype(jnp.float32)
    n_win = 3 * BLOCK
    n_ctx = kc.shape[1]

    def step(args):
        bi, qi, ki, vi = args
        kpos = bi * BLOCK - BLOCK + jnp.arange(n_win)
        valid = band & ((kpos >= 0) & (kpos < L))[None, :]
        s_lat = jnp.einsum('bqkgd,bskd->bkgqs', qi, ki).astype(jnp.float32) * SM_SCALE
        s_lat = jnp.where(valid, s_lat, NEG)
        s_ctx = jnp.einsum('bqkgd,bskd->bkgqs', qi, kc).astype(jnp.float32) * SM_SCALE
        s_snk = jnp.broadcast_to(sink_l, s_lat.shape[:-1] + (1,))
        p = jax.nn.softmax(jnp.concatenate([s_lat, s_ctx, s_snk], axis=-1), axis=-1).astype(v.dtype)
        return (jnp.einsum('bkgqs,bskd->bqkgd', p[..., :n_win], vi)
                + jnp.einsum('bkgqs,bskd->bqkgd', p[..., n_win:n_win + n_ctx], vc))

    o = lax.map(step, (jnp.arange(nb), qb, kwin, vwin))
    return o.swapaxes(0, 1).reshape(Bn, L, ATTN_W)


def global_attention(q, k, v, kc, vc):
    Bn, L = q.shape[0], q.shape[1]
    nb = L // BLOCK
    kall = jnp.concatenate([k, kc], axis=1)
    vall = jnp.concatenate([v, vc], axis=1)
    qb = q.reshape(Bn, nb, BLOCK, KV_HEADS, Q_PER_KV, HEAD_DIM).swapaxes(0, 1)

    def step(qi):
        s = jnp.einsum('bqkgd,bskd->bkgqs', qi, kall).astype(jnp.float32) * SM_SCALE
        p = jax.nn.softmax(s, axis=-1).astype(vall.dtype)
        return jnp.einsum('bkgqs,bskd->bqkgd', p, vall)

    o = lax.map(step, qb)
    return o.swapaxes(0, 1).reshape(Bn, L, ATTN_W)


def context_attention(qc, kc, vc, sink):
    s = jnp.einsum('bqkgd,bskd->bkgqs', qc, kc).astype(jnp.float32) * SM_SCALE
    if sink is not None:
        sk = jnp.broadcast_to(sink.reshape(1, KV_HEADS, Q_PER_KV, 1, 1).astype(jnp.float32), s.shape[:-1] + (1,))
        s = jnp.concatenate([s, sk], axis=-1)
    p = jax.nn.softmax(s, axis=-1)[..., :kc.shape[1]].astype(vc.dtype)
    o = jnp.einsum('bkgqs,bskd->bqkgd', p, vc)
    return o.reshape(qc.shape[0], qc.shape[1], ATTN_W)


def merge_branches(h, ys, w_branch, w_gate, b_gate, w_out):
    merged = jax.nn.sigmoid(h @ w_gate[0] + b_gate[0]) * (ys[0] @ w_branch[0])
    for i in range(1, N_BRANCH):
        merged = merged + jax.nn.sigmoid(h @ w_gate[i] + b_gate[i]) * (ys[i] @ w_branch[i])
    return merged @ w_out


def dwconv3(u, w, b):
    up = jnp.pad(u, ((0, 0), (1, 1), (0, 0)))
    return up[:, :-2] * w[0] + up[:, 1:-1] * w[1] + up[:, 2:] * w[2] + b


def conv_glu(h, w_g, w_v, cw, cb, w_d):
    a = dwconv3(h @ w_g, cw, cb)
    return (jax.nn.silu(a) * (h @ w_v)) @ w_d


def setup_inputs(seed: int = 0) -> dict:
    key = jax.random.key(seed)
    ks = jax.random.split(key, 24)
    f32 = jnp.float32
    nrm = lambda k, shp, s: jax.random.normal(k, shp, f32) * s
    D = D_MODEL
    return {
        "x": nrm(ks[0], (BATCH, SEQ, D), 1.0),
        "c": nrm(ks[1], (BATCH, D), 1.0),
        "ctx": nrm(ks[2], (BATCH, CTX_LEN, D), 1.0),
        "c_ctx": nrm(ks[3], (D,), 1.0),
        "w_mod": nrm(ks[4], (DEPTH, D, 6 * D), 0.5 * D ** -0.5),
        "b_mod": nrm(ks[5], (DEPTH, 6 * D), 0.02),
        "norm1_g": 1.0 + nrm(ks[6], (DEPTH, D), 0.02),
        "norm2_g": 1.0 + nrm(ks[7], (DEPTH, D), 0.02),
        "w_in": nrm(ks[8], (DEPTH, D, IN_W), D ** -0.5),
        "w_pool_grp": nrm(ks[9], (DEPTH, POOL_GROUPS, POOL_GW, POOL_GW), POOL_GW ** -0.5),
        "pool_scale": 1.0 + nrm(ks[10], (DEPTH, POOL_W), 0.02),
        "win_sink": nrm(ks[11], (DEPTH, N_HEADS), 0.5),
        "q_norm_g": 1.0 + nrm(ks[12], (DEPTH, HEAD_DIM), 0.02),
        "k_norm_g": 1.0 + nrm(ks[13], (DEPTH, HEAD_DIM), 0.02),
        "w_branch": nrm(ks[14], (DEPTH, N_BRANCH, BRANCH_W, D), BRANCH_W ** -0.5),
        "w_gate": nrm(ks[15], (DEPTH, N_BRANCH, D, D), D ** -0.5),
        "b_gate": nrm(ks[16], (DEPTH, N_BRANCH, D), 0.02),
        "w_out": nrm(ks[17], (DEPTH, D, D), D ** -0.5),
        "w_ff_gate": nrm(ks[18], (DEPTH, D, D_FF), D ** -0.5),
        "w_ff_val": nrm(ks[19], (DEPTH, D, D_FF), D ** -0.5),
        "conv_w": nrm(ks[20], (DEPTH, 3, D_FF), 3 ** -0.5),
        "conv_b": nrm(ks[21], (DEPTH, D_FF), 0.02),
        "w_ff_down": nrm(ks[22], (DEPTH, D_FF, D), D_FF ** -0.5),
        "final_g": 1.0 + nrm(ks[23], (D,), 0.02),
    }


def reference(x, c, ctx, c_ctx, w_mod, b_mod, norm1_g, norm2_g, w_in, w_pool_grp, pool_scale,
              win_sink, q_norm_g, k_norm_g, w_branch, w_gate, b_gate, w_out,
              w_ff_gate, w_ff_val, conv_w, conv_b, w_ff_down, final_g):
    Bn, L, _ = x.shape
    C = ctx.shape[1]
    cos, sin = axial_rope_tables(L)
    for l in range(DEPTH):
        last = l == DEPTH - 1
        mod = (jax.nn.silu(c) @ w_mod[l] + b_mod[l]).reshape(Bn, 6, 1, D_MODEL)
        mod_c = (jax.nn.silu(c_ctx) @ w_mod[l] + b_mod[l]).reshape(6, 1, 1, D_MODEL)

        h = modulate(x, norm1_g[l], mod[:, 0], mod[:, 1])
        hc = modulate(ctx, norm1_g[l], mod_c[0], mod_c[1])
        p_l, wq_l, wk_l, wv_l, gq_l, gk_l, gv_l = split_in(h @ w_in[l])
        p_c, wq_c, wk_c, wv_c, gq_c, gk_c, gv_c = split_in(hc @ w_in[l])

        wkc = heads(wk_c, KV_HEADS)
        wvc = heads(wv_c, KV_HEADS)
        gkc = rms_norm(heads(gk_c, KV_HEADS), k_norm_g[l])
        gvc = heads(gv_c, KV_HEADS)

        wq = apply_rope(heads(wq_l, N_HEADS), cos, sin).reshape(Bn, L, KV_HEADS, Q_PER_KV, HEAD_DIM)
        wk = apply_rope(heads(wk_l, KV_HEADS), cos, sin)
        gq = apply_rope(rms_norm(heads(gq_l, N_HEADS), q_norm_g[l]), cos, sin).reshape(Bn, L, KV_HEADS, Q_PER_KV, HEAD_DIM)
        gk = apply_rope(rms_norm(heads(gk_l, KV_HEADS), k_norm_g[l]), cos, sin)

        y_pool = multiscale_pool(p_l, w_pool_grp[l], pool_scale[l])
        y_win = window_attention(wq, wk, heads(wv_l, KV_HEADS), wkc, wvc, win_sink[l])
        y_glob = global_attention(gq, gk, heads(gv_l, KV_HEADS), gkc, gvc)
        x = x + mod[:, 2] * merge_branches(h, (y_pool, y_win, y_glob), w_branch[l], w_gate[l], b_gate[l], w_out[l])

        if not last:
            qcw = heads(wq_c, N_HEADS).reshape(Bn, C, KV_HEADS, Q_PER_KV, HEAD_DIM)
            qcg = rms_norm(heads(gq_c, N_HEADS), q_norm_g[l]).reshape(Bn, C, KV_HEADS, Q_PER_KV, HEAD_DIM)
            yc_pool = multiscale_pool(p_c, w_pool_grp[l], pool_scale[l])
            yc_win = context_attention(qcw, wkc, wvc, win_sink[l])
            yc_glob = context_attention(qcg, gkc, gvc, None)
            ctx = ctx + mod_c[2] * merge_branches(hc, (yc_pool, yc_win, yc_glob), w_branch[l], w_gate[l], b_gate[l], w_out[l])
            h2c = modulate(ctx, norm2_g[l], mod_c[3], mod_c[4])
            ctx = ctx + mod_c[5] * conv_glu(h2c, w_ff_gate[l], w_ff_val[l], conv_w[l], conv_b[l], w_ff_down[l])

        h2 = modulate(x, norm2_g[l], mod[:, 3], mod[:, 4])
        x = x + mod[:, 5] * conv_glu(h2, w_ff_gate[l], w_ff_val[l], conv_w[l], conv_b[l], w_ff_down[l])
    return rms_norm(x, final_g)
```

```python
import contextlib
import numpy as np
import concourse.bass as bass
import concourse.mybir as mybir
from concourse.bass_utils import run_bass_kernel_spmd

F32 = mybir.dt.float32
BF16 = mybir.dt.bfloat16
AF = mybir.ActivationFunctionType
ALU = mybir.AluOpType

D = 1024
KC = 8
CTX = 256
HD = 64
DFF = 2816
FC = 22
NL = 2
NSEQ = 2
GRID_W = 64
EPS = 1e-6
SM_SCALE = HD ** -0.5
N_CORES = 8


class Buf:
    __slots__ = ("name", "writers", "readers", "dsem", "dcount", "excl", "full")

    def __init__(self, name, excl=False):
        self.name = name
        self.writers = []
        self.readers = []
        self.dsem = None
        self.dcount = 0
        self.excl = excl
        self.full = []


class Eng:
    def __init__(self, name):
        self.name = name
        self.ops = []
        self.count = 0
        self.seen = {}


class Sched:
    def __init__(self, nc):
        self.nc = nc
        self.eng = {n: Eng(n) for n in ("tensor", "vector", "scalar", "gpsimd", "sync")}
        self.semkeys = ["e_" + n for n in self.eng]
        self.dma_latest = {}
        self.n_dsem = 0
        self.dsem_pool = {}

    def _deps(self, reads, writes, partial, own=None):
        ev = []
        for b in reads:
            ev.extend(b.writers)
            if b.excl:
                ev.extend(r for r in b.readers if r[0] != own)
        for b in writes:
            ev.extend(b.readers)
            if not (partial and not b.readers):
                ev.extend(b.writers)
            else:
                ev.extend(b.full)
        return ev

    def _commit(self, reads, writes, partial, event):
        for b in writes:
            if partial and not b.readers:
                b.writers.append(event)
            else:
                b.writers = [event]
                b.readers = []
                b.full = [] if partial else [event]
        for b in reads:
            b.readers.append(event)
            if len(b.readers) > 64:
                best = {}
                for (k, v) in b.readers:
                    if best.get(k, 0) < v:
                        best[k] = v
                b.readers = list(best.items())

    def _emit_waits(self, e, events, skip_own):
        need = {}
        for (k, v) in events:
            if k[0] == "d":
                v = self.dma_latest[k]
            if skip_own and k == "e_" + e.name:
                continue
            if need.get(k, 0) < v:
                need[k] = v
        for k, v in need.items():
            if e.seen.get(k, 0) >= v:
                continue
            e.seen[k] = v
            e.ops.append(("wait", k, v))

    def op(self, engine, fn, reads=(), writes=(), partial=False):
        e = self.eng[engine]
        ev = self._deps(reads, writes, partial, "e_" + engine)
        self._emit_waits(e, ev, engine == "tensor")
        e.count += 1
        event = ("e_" + engine, e.count)
        e.ops.append(("op", fn, "e_" + engine))
        self._commit(reads, writes, partial, event)
        return event

    def dma(self, queue, fn, sb, reads=(), writes=(), partial=False):
        e = self.eng[queue]
        if sb.dsem is None:
            sb.dsem = {}
            sb.dcount = {}
        if queue not in sb.dsem:
            key = "d_%d" % self.n_dsem
            self.n_dsem += 1
            sb.dsem[queue] = key
            sb.dcount[queue] = 0
            self.semkeys.append(key)
            self.dma_latest[key] = 0
        key = sb.dsem[queue]
        ev = self._deps(reads, writes, partial)
        self._emit_waits(e, ev, False)
        sb.dcount[queue] += 16
        self.dma_latest[key] = sb.dcount[queue]
        event = (key, sb.dcount[queue])
        e.ops.append(("dma", fn, key))
        self._commit(reads, writes, partial, event)
        return event

    def barrier(self):
        targets = {}
        for n, e in self.eng.items():
            if e.count:
                targets["e_" + n] = e.count
        for k, v in self.dma_latest.items():
            if v:
                targets[k] = v
        for n, e in self.eng.items():
            for k, v in targets.items():
                if e.seen.get(k, 0) >= v:
                    continue
                if k == "e_" + n and n == "tensor":
                    pass
                e.seen[k] = v
                e.ops.append(("wait", k, v))

    def run(self, sems):
        nc = self.nc

        def replay(ename):
            def body(h):
                for item in self.eng[ename].ops:
                    if item[0] == "wait":
                        h.wait_ge(sems[item[1]], item[2])
                    elif item[0] == "op":
                        item[1](h).then_inc(sems[item[2]], 1)
                    else:
                        item[1](h).then_inc(sems[item[2]], 16)
            return body

        with nc.Block() as block:
            block.sync(replay("sync"))
            block.tensor(replay("tensor"))
            block.vector(replay("vector"))
            block.scalar(replay("scalar"))
            block.gpsimd(replay("gpsimd"))


def f_act(out, in_, func, bias=None, scale=None, accum=None):
    def fn(e):
        kw = {}
        if bias is not None:
            kw["bias"] = bias
        if scale is not None:
            kw["scale"] = scale
        if accum is not None:
            kw["accum_out"] = accum
        return e.activation(out=out, in_=in_, func=func, **kw)
    return fn


def f_tt(out, a, b, op):
    return lambda e: e.tensor_tensor(out=out, in0=a, in1=b, op=op)


def f_ts(out, a, s1, s2, op0, op1=None):
    if op1 is None:
        return lambda e: e.tensor_scalar(out=out, in0=a, scalar1=s1, scalar2=None, op0=op0)
    return lambda e: e.tensor_scalar(out=out, in0=a, scalar1=s1, scalar2=s2, op0=op0, op1=op1)


def f_stt(out, in0, scalar, in1, op0, op1):
    return lambda e: e.scalar_tensor_tensor(out=out, in0=in0, scalar=scalar, in1=in1, op0=op0, op1=op1)


def f_copy(out, in_):
    return lambda e: e.tensor_copy(out=out, in_=in_)


def f_recip(out, in_):
    return lambda e: e.reciprocal(out=out, in_=in_)


def f_memset(ap, v):
    return lambda e: e.memset(ap, v)


def f_dma(out, in_):
    return lambda e: e.dma_start(out=out, in_=in_)


def f_mms(lst):
    def fn(e):
        ins = None
        for (o, l, r, st, sp) in lst:
            ins = e.matmul(o, lhsT=l, rhs=r, start=st, stop=sp)
        return ins
    return fn


def f_transposes(lst):
    def fn(e):
        ins = None
        for (o, i, idn) in lst:
            ins = e.transpose(out=o, in_=i, identity=idn)
        return ins
    return fn


def host_consts(L):
    rows = L // GRID_W
    t = np.arange(L)
    row = (t // GRID_W).astype(np.float32)
    col = (t % GRID_W).astype(np.float32)
    half = HD // 2
    inv = (np.float32(10000.0) ** (-np.arange(0, half, 2, dtype=np.float32) / np.float32(half))).astype(np.float32)
    cosT = np.zeros((128, L), np.float32)
    sinT = np.zeros((128, L), np.float32)
    for p in range(128):
        d = p % 64
        axis = d // 32
        hf = (d % 32) // 16
        f = d % 16
        pos = row if axis == 0 else col
        ang = (pos * inv[f]).astype(np.float32)
        cosT[p] = np.cos(ang).astype(np.float32)
        s = np.sin(ang).astype(np.float32)
        sinT[p] = -s if hf == 0 else s
    ident = np.eye(128, dtype=np.float32)
    pswap = np.zeros((128, 128), np.float32)
    for p in range(128):
        q = p + 16 if (p % 32) < 16 else p - 16
        pswap[q, p] = 1.0
    bd = np.zeros((128, 128), np.float32)
    bd[0:64, 0:64] = 1.0 / 64
    bd[64:128, 64:128] = 1.0 / 64
    onesA = np.zeros((128, 128), np.float32); onesA[:, 0:64] = 1.0
    onesB = np.zeros((128, 128), np.float32); onesB[:, 64:128] = 1.0
    kk = np.arange(128)[:, None]
    qq = np.arange(128)[None, :]
    mge = (kk >= qq).astype(np.float32)
    mle = (kk <= qq).astype(np.float32)
    masks = np.concatenate([mge, mge, mle, mle], axis=1)
    edge = np.zeros((4, 16), np.float32)
    for g, w in enumerate((2, 4, 8, 16)):
        for i in range(8):
            edge[g, i] = float(w) / min(w, i + w // 2)
            edge[g, 8 + i] = float(w) / min(w, w // 2 + 8 - i)
    edge = np.broadcast_to(edge.reshape(1, 64), (128, 64)).copy()
    cbf = np.concatenate([pswap, bd, onesA, onesB, masks], axis=1)
    sel = np.zeros((128, 128), np.float32)
    sel[64, 0:64] = 1.0
    sel[0, 64:128] = 1.0
    return {"ropec": cosT, "ropes": sinT, "ident": ident, "cbf": cbf, "edge": edge, "sel": sel}


class _Stop(Exception):
    pass


def build_nc(L, debug=False, stop=None):
    assert L % 512 == 0
    NB = L // 128
    NT = L // 512
    LK = L + CTX
    NKB = NB + 2
    nc = bass.Bass("TRN2", target_bir_lowering=False)

    def din(name, shape, dt=F32):
        return nc.dram_tensor(name, list(shape), dt, kind="ExternalInput").ap()

    x_in = din("x", [NSEQ, L, D])
    ctx_in = din("ctx", [NSEQ, CTX, D])
    c3_in = din("c3", [3, D])
    w_mod = din("w_mod", [NL, D, 6 * D])
    b_mod = din("b_mod", [NL, 6 * D])
    norm1_g = din("norm1_g", [NL, D])
    norm2_g = din("norm2_g", [NL, D])
    w_in = din("w_in", [NL, D, 2048])
    w_pool = din("w_pool_grp", [NL, 4, 128, 128])
    pool_scale = din("pool_scale", [NL, 512])
    win_sink = din("win_sink", [NL, 8])
    q_norm_g = din("q_norm_g", [NL, HD])
    k_norm_g = din("k_norm_g", [NL, HD])
    w_branch = din("w_branch", [NL, 3, 512, D])
    w_gate = din("w_gate", [NL, 3, D, D])
    b_gate = din("b_gate", [NL, 3, D])
    w_out = din("w_out", [NL, D, D])
    w_ffg = din("w_ff_gate", [NL, D, DFF])
    w_ffv = din("w_ff_val", [NL, D, DFF])
    conv_w = din("conv_w", [NL, 3, DFF])
    conv_b = din("conv_b", [NL, DFF])
    w_ffd = din("w_ff_down", [NL, DFF, D])
    final_g = din("final_g", [D])
    ropec_in = din("ropec", [128, L])
    ropes_in = din("ropes", [128, L])
    ident_in = din("ident", [128, 128])
    cbf_in = din("cbf", [128, 1024])
    edge_in = din("edge", [128, 64])
    sel_in = din("sel", [128, 128])

    y_out = nc.dram_tensor("y", [NSEQ, L, D], F32, kind="ExternalOutput").ap()
    okind = "ExternalOutput" if debug else "Internal"
    xsA = nc.dram_tensor("xsA", [NSEQ, L, D], F32, kind=okind).ap()
    xsB = nc.dram_tensor("xsB", [NSEQ, L, D], F32, kind=okind).ap()
    csA = nc.dram_tensor("csA", [NSEQ, CTX, D], F32, kind=okind).ap()
    csB = nc.dram_tensor("csB", [NSEQ, CTX, D], F32, kind=okind).ap()
    yb = nc.dram_tensor("yb", [NSEQ, 12, 128, LK], BF16, kind=okind).ap()
    modd = nc.dram_tensor("modd", [NL, 3, 6 * D], F32, kind=okind).ap()

    S = Sched(nc)
    es = contextlib.ExitStack()
    with es:
        SB_BASE = 16576
        SB_LIMIT = 229376
        state = {"persist": 0, "top": SB_BASE}

        def alloc(name, free_elems, dt, base=None):
            nbytes = free_elems * (4 if dt == F32 else 2)
            off = state["top"] if base is None else base
            off = (off + 31) // 32 * 32
            t = nc.alloc_sbuf_tensor_at(name, [128, free_elems], dt, offset=off)
            if base is None:
                state["top"] = off + nbytes
                assert state["top"] <= SB_LIMIT, (name, state["top"])
            return t

        cnt = [0]

        def palloc(name, free_elems, dt):
            cnt[0] += 1
            return alloc("%s_%d" % (name, cnt[0]), free_elems, dt)

        ps = es.enter_context(nc.psum_tensor("ps", [128, 4096], F32))
        bank = [Buf("bank%d" % i, excl=True) for i in range(8)]

        def pbank(b, n=512, off=0):
            return ps[:, b * 512 + off: b * 512 + off + n]

        ident = alloc("ident", 128, F32); b_const = Buf("const")
        cbf = alloc("cbf", 1024, BF16)
        pswap = cbf[:, 0:128]
        bdm = cbf[:, 128:256]
        onesA = cbf[:, 256:384]
        onesB = cbf[:, 384:512]
        maskLR = cbf[:, 512:1024]
        edge = alloc("edge", 64, F32)
        selT = alloc("selT", 128, F32)
        epst = alloc("epst", 1, F32)
        modT = alloc("modT", NL * 3 * 4 * 8, F32); b_modT = Buf("modT")
        gT = alloc("gT", 4 * 8, F32)
        bgT = alloc("bgT", NL * 3 * 8, F32)
        pscT = alloc("pscT", NL * 4, F32)
        cvT = alloc("cvT", NL * 4 * FC, F32)
        qkg = alloc("qkg", NL * 2, F32)
        sinkT = alloc("sinkT", NL * 4, F32)
        stat = alloc("stat", 16, F32); b_stat = [Buf("stat%d" % i) for i in range(8)]
        identb_t = alloc("identb", 128, BF16)
        identb = identb_t[:]
        P_END = state["top"]

        def modT_ap(l, v, k, c):
            i = ((l * 3 + v) * 4 + k) * 8 + c
            return modT[:, i:i + 1]

        S.dma("sync", f_dma(ident[:], ident_in), b_const, writes=[b_const], partial=True)
        S.dma("gpsimd", f_dma(cbf[:], cbf_in), b_const, writes=[b_const], partial=True)
        S.dma("sync", f_dma(edge[:], edge_in), b_const, writes=[b_const], partial=True)
        S.dma("sync", f_dma(selT[:], sel_in), b_const, writes=[b_const], partial=True)
        S.op("vector", f_memset(epst[:], EPS), writes=[b_const], partial=True)
        S.op("vector", f_copy(identb, ident[:]), reads=[b_const], writes=[b_const])

        def small_T(dst, src_ap):
            def fn(e):
                with nc.allow_non_contiguous_dma(reason="tiny per-feature vectors, loaded once"):
                    return e.dma_start(out=dst, in_=src_ap.rearrange("(c p) -> p c", p=128))
            S.dma("sync", fn, b_const, writes=[b_const], partial=True)

        for l in range(NL):
            small_T(gT[:, l * 8:(l + 1) * 8], norm1_g[l])
            small_T(gT[:, 16 + l * 8:16 + (l + 1) * 8], norm2_g[l])
            for i in range(3):
                small_T(bgT[:, (l * 3 + i) * 8:(l * 3 + i + 1) * 8], b_gate[l, i])
                small_T(cvT[:, (l * 4 + i) * FC:(l * 4 + i + 1) * FC], conv_w[l, i])
            small_T(cvT[:, (l * 4 + 3) * FC:(l * 4 + 4) * FC], conv_b[l])
            small_T(pscT[:, l * 4:(l + 1) * 4], pool_scale[l])

            for (p0, col, src) in ((0, 2 * l, q_norm_g[l]), (64, 2 * l, q_norm_g[l]), (0, 2 * l + 1, k_norm_g[l]), (64, 2 * l + 1, k_norm_g[l])):
                def fq(e, p0=p0, col=col, src=src):
                    with nc.allow_non_contiguous_dma(reason="tiny"):
                        return e.dma_start(out=qkg[p0:p0 + 64, col:col + 1], in_=src.rearrange("(p o) -> p o", o=1))
                S.dma("sync", fq, b_const, writes=[b_const], partial=True)
            for j in range(4):
                for h in range(2):
                    def fs(e, l=l, j=j, h=h):
                        with nc.allow_non_contiguous_dma(reason="tiny"):
                            return e.dma_start(out=sinkT[h * 64:(h + 1) * 64, l * 4 + j:l * 4 + j + 1],
                                               in_=win_sink[l:l + 1, h * 4 + j:h * 4 + j + 1].broadcast_to([64, 1]))
                    S.dma("sync", fs, b_const, writes=[b_const], partial=True)
        S.op("scalar", f_act(sinkT[:], sinkT[:], AF.Exp), reads=[b_const], writes=[b_const])

        state["top"] = P_END
        if stop == "consts":
            S.barrier()
            sems = {k: es.enter_context(nc.semaphore(k)) for k in S.semkeys}
            S.run(sems)
            return nc
        s3 = alloc("s3", D, F32)
        sT = alloc("sT", KC * 128, F32)
        wm = [alloc("wm%d" % i, 8 * 512, F32) for i in range(2)]; b_wm = [Buf("wm0"), Buf("wm1")]
        bm3 = alloc("bm3", 6 * D, F32); b_bm3 = Buf("bm3")
        mrow = alloc("mrow", 6 * D, F32); b_mrow = Buf("mrow")
        b_s3 = Buf("s3"); b_sT = Buf("sT")
        S.op("gpsimd", f_memset(s3[:], 0.0), writes=[b_s3])
        S.op("gpsimd", f_memset(mrow[:], 0.0), writes=[b_mrow])
        S.dma("sync", f_dma(s3[0:3, :], c3_in), b_s3, writes=[b_s3])
        S.op("scalar", f_act(s3[0:3, :], s3[0:3, :], AF.Silu), reads=[b_s3], writes=[b_s3])
        for hh in range(2):
            S.op("tensor", f_transposes([(pbank(hh, 128, 128 * c4), s3[:, (hh * 4 + c4) * 128:(hh * 4 + c4 + 1) * 128], ident[:]) for c4 in range(4)]),
                 reads=[b_s3, b_const], writes=[bank[hh]])
            S.op("vector", f_copy(sT[:, hh * 512:(hh + 1) * 512], pbank(hh)), reads=[bank[hh]], writes=[b_sT], partial=True)
        for l in range(NL):
            S.dma("sync", f_dma(bm3[0:3, :], b_mod[l:l + 1, :].broadcast_to([3, 6 * D])), b_bm3, writes=[b_bm3])
            for pc in range(12):
                wb = wm[pc % 2]; bw = b_wm[pc % 2]
                S.dma("sync", f_dma(wb[:].rearrange("p (c n) -> p c n", c=KC),
                                    w_mod[l][:, pc * 512:(pc + 1) * 512].rearrange("(c p) n -> p c n", p=128)),
                      bw, writes=[bw])
                pb = 1 + pc % 2
                S.op("tensor", f_mms([(pbank(pb), sT[:, c * 128:(c + 1) * 128], wb[:, c * 512:(c + 1) * 512], c == 0, c == KC - 1)
                                      for c in range(KC)]), reads=[b_sT, bw], writes=[bank[pb]])
                S.op("vector", f_tt(mrow[0:3, pc * 512:(pc + 1) * 512], pbank(pb)[0:3, :], bm3[0:3, pc * 512:(pc + 1) * 512], ALU.add),
                     reads=[bank[pb], b_bm3], writes=[b_mrow], partial=True)
            S.dma("sync", f_dma(modd[l], mrow[0:3, :]), b_mrow, reads=[b_mrow])
            for k, mi in enumerate((0, 1, 3, 4)):
                for hh in range(2):
                    S.op("tensor", f_transposes([(pbank(3 + hh, 128, 128 * c4), mrow[:, mi * D + (hh * 4 + c4) * 128: mi * D + (hh * 4 + c4 + 1) * 128], ident[:])
                                                 for c4 in range(4)]), reads=[b_mrow, b_const], writes=[bank[3 + hh]])
                for v in range(3):
                    i0 = ((l * 3 + v) * 4 + k) * 8
                    for hh in range(2):
                        src = pbank(3 + hh).rearrange("p (c v) -> p c v", v=128)[:, :, v]
                        S.op("vector", f_copy(modT[:, i0 + hh * 4:i0 + hh * 4 + 4], src), reads=[bank[3 + hh]], writes=[b_modT], partial=True)
        for l in range(NL):
            for v in range(3):
                for k, goff in ((1, l * 8), (3, 16 + l * 8)):
                    i0 = ((l * 3 + v) * 4 + k) * 8
                    S.op("vector", f_stt(modT[:, i0:i0 + 8], modT[:, i0:i0 + 8], 1.0, gT[:, goff:goff + 8], ALU.add, ALU.mult),
                         reads=[b_modT, b_const], writes=[b_modT])
        S.barrier()

        def load_gate_tiles(dst_list):
            for (t, b, l, v, mi) in dst_list:
                S.dma("sync", f_dma(t[:], modd[l, v:v + 1, mi * D:(mi + 1) * D].broadcast_to([128, D])), b, writes=[b])

        def norm_block(src_rows, xslot, b_x, junk, b_junk, hT, b_hT, col0, l, v, k_shift, k_gmod, tb, si, mode="all"):
            if mode in ("all", "load"):
                S.dma("sync", f_dma(xslot, src_rows), b_x, writes=[b_x])
            if mode == "load":
                return
            _nb = "9"
            ms = stat[:, 2 * si:2 * si + 1]
            rs = stat[:, 2 * si + 1:2 * si + 2]
            S.op("scalar", f_act(junk, xslot, AF.Square, scale=1.0 / 32.0, accum=ms), reads=[b_x], writes=[b_junk, b_stat[si]])
            S.op("scalar", f_act(rs, ms, AF.Sqrt, bias=epst[:, 0:1], scale=1.0), reads=[b_stat[si], b_const], writes=[b_stat[si]])
            S.op("vector", f_recip(rs, rs), reads=[b_stat[si]], writes=[b_stat[si]])
            S.op("vector", f_ts(junk, xslot, rs, None, ALU.mult), reads=[b_x, b_stat[si]], writes=[b_junk])
            if _nb == "2":
                return
            pt = pbank(tb).bitcast(BF16)
            S.op("tensor", f_transposes([(pt[:, c * 128:(c + 1) * 128], junk[:, c * 128:(c + 1) * 128], identb) for c in range(KC)]),
                 reads=[b_junk, b_const], writes=[bank[tb]])
            if _nb == "3":
                return
            for c in range(KC):
                o = hT[:, c * hT_w + col0: c * hT_w + col0 + 128]
                i = pt[:, c * 128:(c + 1) * 128]
                if False:
                    S.op("scalar", f_act(o, i, AF.Identity, bias=modT_ap(l, v, k_shift, c), scale=modT_ap(l, v, k_gmod, c)),
                         reads=[bank[tb], b_modT], writes=[b_hT], partial=True)
                else:
                    S.op("vector", f_ts(o, i, modT_ap(l, v, k_gmod, c), modT_ap(l, v, k_shift, c), ALU.mult, ALU.add),
                         reads=[bank[tb], b_modT], writes=[b_hT], partial=True)

        hT_w = 512

        def finalize():
            S.barrier()
            sems = {k: es.enter_context(nc.semaphore(k)) for k in S.semkeys}
            S.run(sems)
            return nc
        if stop == "p0":
            return finalize()
        for l in range(NL):
            last = (l == NL - 1)
            src_x = x_in if l == 0 else xsA
            src_c = ctx_in if l == 0 else csA

            state["top"] = P_END
            WIN = alloc("WIN", KC * 2048, BF16); b_WIN = Buf("WIN")
            WPL = alloc("WPL", 4 * 128, BF16); b_WPL = Buf("WPL")
            KTw = alloc("KTw", LK, BF16); b_KTw = Buf("KTw")
            KTg = alloc("KTg", LK, BF16); b_KTg = Buf("KTg")
            Vw = alloc("Vw", NKB * 192, BF16); b_Vw = Buf("Vw")
            Vg = alloc("Vg", NKB * 192, BF16); b_Vg = Buf("Vg")
            PW = 8 + L + 8
            pT = alloc("pT", 4 * PW, BF16); b_pT = Buf("pT")
            PWc = 8 + CTX + 8
            pTc = alloc("pTc", 4 * PWc, BF16); b_pTc = Buf("pTc")
            ropec_t = [alloc("ropec%d" % i, 512, F32) for i in range(2)]
            ropes_t = [alloc("ropes%d" % i, 512, F32) for i in range(2)]
            b_rope_t = [Buf("rope0"), Buf("rope1")]
            rope_state = {"i": 0}
            xs_ = [alloc("xslot%d" % i, D, F32) for i in range(2)]; b_xs = [Buf("xs0"), Buf("xs1")]
            junk = [alloc("junk%d" % i, D, BF16) for i in range(2)]; b_junk = [Buf("junk0"), Buf("junk1")]
            hT = alloc("hT", KC * 512, BF16); b_hT = Buf("hT")
            zb = [alloc("zb%d" % i, 512, BF16) for i in range(2)]; b_zb = [Buf("zb0"), Buf("zb1")]
            sq = [alloc("sq%d" % i, 512, BF16) for i in range(2)]; b_sq = [Buf("sq0"), Buf("sq1")]
            rsd = [alloc("rsd%d" % i, 512, F32) for i in range(2)]; b_rsd = [Buf("rsd0"), Buf("rsd1")]
            t1 = [alloc("t1_%d" % i, 512, F32) for i in range(2)]; b_t1 = [Buf("t1a"), Buf("t1b")]
            t2 = [alloc("t2_%d" % i, 512, F32) for i in range(2)]; b_t2 = [Buf("t2a"), Buf("t2b")]
            QA = alloc("QA", 8 * 512, BF16); b_QA = Buf("QA")
            QB = alloc("QB", 8 * 512, BF16); b_QB = Buf("QB")
            Pb = [alloc("Pb%d" % i, 1280, BF16) for i in range(2)]; b_Pb = [Buf("Pb0"), Buf("Pb1")]
            yst = alloc("yst", 12 * 512, BF16); b_yst = Buf("yst")
            pl = [alloc("pl%d" % i, 528, F32) for i in range(2)]; b_pl = [Buf("pl0"), Buf("pl1")]
            pld = alloc("pld", 4 * 512, BF16); b_pld = Buf("pld")
            rcp = alloc("rcp", 512, F32); b_rcp = Buf("rcp")
            Dsb = alloc("Dsb", 512, F32); b_Dsb = Buf("Dsb")
            S.op("gpsimd", f_memset(Dsb[:], 0.0), writes=[b_Dsb])

            def wcols(dst0, src0, n, l=l):
                S.dma("gpsimd", f_dma(WIN[:].rearrange("p (c n) -> p c n", c=KC)[:, :, dst0:dst0 + n],
                                      w_in[l][:, src0:src0 + n].rearrange("(c p) n -> p c n", p=128)),
                      b_WIN, writes=[b_WIN], partial=True)
            wcols(0, 0, 512)
            wcols(512, 1024, 128)
            wcols(640, 1792, 128)
            wcols(768, 1152, 128)
            wcols(896, 1920, 128)
            for a, base in ((0, 512), (1, 1280)):
                for j in range(4):
                    for h in range(2):
                        wcols(1024 + a * 512 + j * 128 + h * 64, base + (h * 4 + j) * 64, 64)
            S.dma("gpsimd", f_dma(WPL[:].rearrange("p (g d) -> p g d", g=4), w_pool[l].rearrange("g c d -> c g d")),
                  b_WPL, writes=[b_WPL])

            def load_rope(t0, n):
                i = rope_state["i"] = 1 - rope_state["i"]
                S.dma("sync", f_dma(ropec_t[i][:, 0:n], ropec_in[:, t0:t0 + n]), b_rope_t[i], writes=[b_rope_t[i]], partial=True)
                S.dma("sync", f_dma(ropes_t[i][:, 0:n], ropes_in[:, t0:t0 + n]), b_rope_t[i], writes=[b_rope_t[i]], partial=True)
            S.op("gpsimd", f_memset(Vw[:].rearrange("p (k w) -> p k w", w=192)[:, :, 65:128], 0.0), writes=[b_Vw], partial=True)
            S.op("gpsimd", f_memset(Vg[:].rearrange("p (k w) -> p k w", w=192)[:, :, 65:128], 0.0), writes=[b_Vg], partial=True)
            S.op("gpsimd", f_memset(Vw[:].rearrange("p (k w) -> p k w", w=192)[:, :, 64:65], 1.0), writes=[b_Vw], partial=True)
            S.op("gpsimd", f_memset(Vg[:].rearrange("p (k w) -> p k w", w=192)[:, :, 64:65], 1.0), writes=[b_Vg], partial=True)
            S.op("gpsimd", f_memset(pT[:].rearrange("p (g w) -> p g w", g=4)[:, :, 0:8], 0.0), writes=[b_pT], partial=True)
            S.op("gpsimd", f_memset(pT[:].rearrange("p (g w) -> p g w", g=4)[:, :, 8 + L:PW], 0.0), writes=[b_pT], partial=True)
            S.op("gpsimd", f_memset(pTc[:].rearrange("p (g w) -> p g w", g=4)[:, :, 0:8], 0.0), writes=[b_pTc], partial=True)
            S.op("gpsimd", f_memset(pTc[:].rearrange("p (g w) -> p g w", g=4)[:, :, 8 + CTX:PWc], 0.0), writes=[b_pTc], partial=True)
            S.op("gpsimd", f_memset(QA[64:128, :], 0.0), writes=[b_QA])
            S.op("gpsimd", f_memset(QB[0:64, :], 0.0), writes=[b_QB])

            WINc = lambda c, a, n: WIN[:, c * 2048 + a: c * 2048 + a + n]

            def rope_stages(zsrc, b_zsrc, n, tok0, outs, pbs, bi, normed, gcol, is_ctx=False):
                z16 = zb[bi]; bz = b_zb[bi]
                zf = t1[bi][:, 0:n]; bzf = b_t1[bi]

                def s1():
                    if normed:
                        S.op("scalar", f_act(sq[bi][:, 0:n], zsrc, AF.Square), reads=[b_zsrc], writes=[b_sq[bi]])
                    S.op("scalar", f_act(zf, zsrc, AF.Identity), reads=[b_zsrc], writes=[bzf])

                def s2():
                    if normed:
                        S.op("tensor", f_mms([(pbank(pbs[0], n), bdm, sq[bi][:, 0:n], True, True)]), reads=[b_sq[bi], b_const], writes=[bank[pbs[0]]])
                        S.op("scalar", f_act(rsd[bi][:, 0:n], pbank(pbs[0], n), AF.Sqrt, bias=epst[:, 0:1], scale=1.0),
                             reads=[bank[pbs[0]], b_const], writes=[b_rsd[bi]])
                        S.op("vector", f_recip(rsd[bi][:, 0:n], rsd[bi][:, 0:n]), reads=[b_rsd[bi]], writes=[b_rsd[bi]])
                        S.op("vector", f_stt(zf, zf, qkg[:, gcol:gcol + 1], rsd[bi][:, 0:n], ALU.mult, ALU.mult),
                             reads=[bzf, b_rsd[bi], b_const], writes=[bzf])
                    if is_ctx:
                        for (o, bo, p0, p1) in outs:
                            S.op("vector", f_copy(o[p0:p1], zf[p0:p1]), reads=[bzf], writes=[bo], partial=True)
                        return
                    S.op("scalar", f_act(z16[:, 0:n], zf, AF.Identity), reads=[bzf], writes=[bz])

                def s3():
                    if is_ctx:
                        return
                    S.op("tensor", f_mms([(pbank(pbs[1], n), pswap, z16[:, 0:n], True, True)]), reads=[bz, b_const], writes=[bank[pbs[1]]])
                    ri = rope_state["i"]
                    S.op("vector", f_tt(t2[bi][:, 0:n], pbank(pbs[1], n), ropes_t[ri][:, 0:n], ALU.mult),
                         reads=[bank[pbs[1]], b_rope_t[ri]], writes=[b_t2[bi]])
                    S.op("vector", f_tt(zf, zf, ropec_t[ri][:, 0:n], ALU.mult), reads=[bzf, b_rope_t[ri]], writes=[bzf])
                    for (o, bo, p0, p1) in outs:
                        S.op("vector", f_tt(o[p0:p1], zf[p0:p1], t2[bi][p0:p1, 0:n], ALU.add),
                             reads=[bzf, b_t2[bi]], writes=[bo], partial=True)
                return s1, s2, s3

            def rope_chunk(*args, **kw):
                s1, s2, s3 = rope_stages(*args, **kw)
                s1(); s2(); s3()

            def pool_tile(pbuf, b_pbuf, W, n, t0, Lseq, ycol0):
                o = 8 + t0
                first = (t0 == 0)
                lastt = (t0 + n == Lseq)
                for g, w in enumerate((2, 4, 8, 16)):
                    base = g * W
                    wd = n + w - 2
                    a0 = base + o - w // 2
                    S.op("gpsimd", f_tt(pl[0][:, 0:wd], pbuf[:, a0:a0 + wd], pbuf[:, a0 + 1:a0 + 1 + wd], ALU.add),
                         reads=[b_pbuf], writes=[b_pl[0]])
                    cur = 0
                    sh = 2
                    while sh < w:
                        wd2 = wd - sh
                        S.op("gpsimd", f_tt(pl[1 - cur][:, 0:wd2], pl[cur][:, 0:wd2], pl[cur][:, sh:sh + wd2], ALU.add),
                             reads=[b_pl[cur]], writes=[b_pl[1 - cur]])
                        cur = 1 - cur
                        wd = wd2
                        sh *= 2
                    assert wd == n
                    dst = pld[:, g * 512: g * 512 + n]
                    S.op("gpsimd", f_ts(pl[cur][:, 0:n], pl[cur][:, 0:n], 1.0 / w, None, ALU.mult), reads=[b_pl[cur]], writes=[b_pl[cur]])
                    S.op("gpsimd", f_tt(dst, pl[cur][:, 0:n], pbuf[:, base + o: base + o + n], ALU.subtract),
                         reads=[b_pl[cur], b_pbuf], writes=[b_pld], partial=True)
                    if first:
                        S.op("gpsimd", f_tt(pl[cur][:, 0:8], pl[cur][:, 0:8], edge[:, g * 16: g * 16 + 8], ALU.mult),
                             reads=[b_pl[cur], b_const, b_pld], writes=[b_pl[cur]])
                        S.op("gpsimd", f_tt(dst[:, 0:8], pl[cur][:, 0:8], pbuf[:, base + o: base + o + 8], ALU.subtract),
                             reads=[b_pl[cur], b_pbuf], writes=[b_pld], partial=True)
                    if lastt:
                        S.op("gpsimd", f_tt(pl[cur][:, n - 8:n], pl[cur][:, n - 8:n], edge[:, g * 16 + 8: g * 16 + 16], ALU.mult),
                             reads=[b_pl[cur], b_const, b_pld], writes=[b_pl[cur]])
                        S.op("gpsimd", f_tt(dst[:, n - 8:n], pl[cur][:, n - 8:n], pbuf[:, base + o + n - 8: base + o + n], ALU.subtract),
                             reads=[b_pl[cur], b_pbuf], writes=[b_pld], partial=True)
                    pb = 4 + g % 2
                    S.op("tensor", f_mms([(pbank(pb, n), WPL[:, g * 128:(g + 1) * 128], dst, True, True)]),
                         reads=[b_pld, b_WPL], writes=[bank[pb]])
                    S.op("scalar", f_act(yst[:, g * 512 + ycol0: g * 512 + ycol0 + n], pbank(pb, n), AF.Identity, scale=pscT[:, l * 4 + g: l * 4 + g + 1]),
                         reads=[bank[pb], b_const], writes=[b_yst], partial=True)

            def attention(n_q, KT, b_KT, Vb, b_Vb, kbs, jq, sink_col, ydst, Sbanks_list, mask_fn=None, qcols=0, ob=6, db=7, dbc=5):
                nk = len(kbs)
                qa = QA[:, jq * 512 + qcols: jq * 512 + qcols + n_q]
                qb_ = QB[:, jq * 512 + qcols: jq * 512 + qcols + n_q]

                def stage_qk(i):
                    kb = kbs[i]
                    sbk = Sbanks_list[i % 2]
                    pbuf = Pb[i % 2]; bp = b_Pb[i % 2]
                    kcol = KT[:, kb * 128:(kb + 1) * 128]
                    S.op("tensor", f_mms([(pbank(sbk[0], n_q), kcol, qa, True, True), (pbank(sbk[1], n_q), kcol, qb_, True, True)]),
                         reads=[b_KT, b_QA, b_QB], writes=[bank[sbk[0]], bank[sbk[1]]])
                    if n_q == 512:
                        S.op("scalar", f_act(pbuf[:, 0:1024], ps[:, sbk[0] * 512: sbk[0] * 512 + 1024], AF.Exp, scale=SM_SCALE),
                             reads=[bank[sbk[0]], bank[sbk[1]]], writes=[bp])
                    else:
                        S.op("scalar", f_act(pbuf[:, 0:n_q], pbank(sbk[0], n_q), AF.Exp, scale=SM_SCALE), reads=[bank[sbk[0]]], writes=[bp], partial=True)
                        S.op("scalar", f_act(pbuf[:, 512:512 + n_q], pbank(sbk[1], n_q), AF.Exp, scale=SM_SCALE), reads=[bank[sbk[1]]], writes=[bp], partial=True)

                def stage_pv(i):
                    kb = kbs[i]
                    pbuf = Pb[i % 2]; bp = b_Pb[i % 2]
                    pa = pbuf[:, 0:n_q]; pb2 = pbuf[:, 512:512 + n_q]
                    st = (i == 0); sp = (i == nk - 1)
                    S.op("tensor", f_mms([
                        (pbank(ob, n_q), Vb[:, kb * 192: kb * 192 + 128], pa, st, sp),
                        (pbank(db, n_q), Vb[:, kb * 192 + 64: kb * 192 + 192], pb2, st, sp)]),
                        reads=[bp, b_Vb, b_const], writes=[bank[ob], bank[db]], partial=(i > 0))

                stage_qk(0)
                for i in range(nk):
                    if i + 1 < nk:
                        stage_qk(i + 1)
                    stage_pv(i)
                finish_attn(n_q, sink_col, ydst, ob=ob, db=db, dbc=dbc)

            def finish_attn(n_q, sink_col, ydst, col0=0, ob=6, db=7, dbc=5):
                S.op("vector", f_copy(Dsb[64:65, 0:n_q], pbank(ob, n_q)[64:65]), reads=[bank[ob]], writes=[b_Dsb], partial=True)
                S.op("vector", f_copy(Dsb[0:1, 0:n_q], pbank(db, n_q)[0:1]), reads=[bank[db]], writes=[b_Dsb], partial=True)
                S.op("tensor", f_mms([(pbank(dbc, n_q), selT[:], Dsb[:, 0:n_q], True, True)]), reads=[b_Dsb, b_const], writes=[bank[dbc]])
                if sink_col is not None:
                    S.op("vector", f_ts(rcp[:, 0:n_q], pbank(dbc, n_q), sinkT[:, sink_col:sink_col + 1], None, ALU.add), reads=[bank[dbc], b_const], writes=[b_rcp])
                    S.op("vector", f_recip(rcp[:, 0:n_q], rcp[:, 0:n_q]), reads=[b_rcp], writes=[b_rcp])
                else:
                    S.op("vector", f_recip(rcp[:, 0:n_q], pbank(dbc, n_q)), reads=[bank[dbc]], writes=[b_rcp])
                S.op("vector", f_tt(ydst[0:64], pbank(ob, n_q)[0:64], rcp[0:64, 0:n_q], ALU.mult), reads=[bank[ob], b_rcp], writes=[b_yst], partial=True)
                S.op("vector", f_tt(ydst[64:128], pbank(db, n_q)[64:128], rcp[64:128, 0:n_q], ALU.mult), reads=[bank[db], b_rcp], writes=[b_yst], partial=True)

            def window_attention(jq, t0, sink_col, ydst):
                info = {}

                def stage1(qb):
                    gq = t0 // 128 + qb
                    kbs = []
                    if gq >= 1:
                        kbs.append((gq - 1, "L"))
                    kbs.append((gq, "C"))
                    if gq + 1 < NB:
                        kbs.append((gq + 1, "R"))
                    kbs.append((NB, "X"))
                    kbs.append((NB + 1, "X"))
                    info[qb] = kbs
                    npc = len(kbs)
                    sel = qb % 2
                    sb3 = (0, 1, 2) if sel == 0 else (2, 3, 4)
                    scol0 = 0 if sel == 0 else 1280
                    pbuf = Pb[sel]; bp = b_Pb[sel]
                    qa = QA[:, jq * 512 + qb * 128: jq * 512 + (qb + 1) * 128]
                    qb_ = QB[:, jq * 512 + qb * 128: jq * 512 + (qb + 1) * 128]
                    mm = []
                    for i, (kb, kind) in enumerate(kbs):
                        kcol = KTw[:, kb * 128:(kb + 1) * 128]
                        mm.append((ps[:, scol0 + (2 * i) * 128: scol0 + (2 * i + 1) * 128], kcol, qa, True, True))
                        mm.append((ps[:, scol0 + (2 * i + 1) * 128: scol0 + (2 * i + 2) * 128], kcol, qb_, True, True))
                    bks = [bank[b] for b in sb3]
                    S.op("tensor", f_mms(mm), reads=[b_KTw, b_QA, b_QB], writes=bks)
                    S.op("scalar", f_act(pbuf[:, 0:npc * 256], ps[:, scol0: scol0 + npc * 256], AF.Exp, scale=SM_SCALE), reads=bks, writes=[bp])
                    for i, (kb, kind) in enumerate(kbs):
                        if kind == "L":
                            S.op("gpsimd", f_tt(pbuf[:, i * 256:(i + 1) * 256], pbuf[:, i * 256:(i + 1) * 256], maskLR[:, 0:256], ALU.mult),
                                 reads=[bp, b_const], writes=[bp])
                        elif kind == "R":
                            S.op("gpsimd", f_tt(pbuf[:, i * 256:(i + 1) * 256], pbuf[:, i * 256:(i + 1) * 256], maskLR[:, 256:512], ALU.mult),
                                 reads=[bp, b_const], writes=[bp])

                def stage2(qb):
                    kbs = info[qb]
                    npc = len(kbs)
                    sel = qb % 2
                    pbuf = Pb[sel]; bp = b_Pb[sel]
                    mm = []
                    oc = qb * 128
                    for i, (kb, kind) in enumerate(kbs):
                        st = (i == 0); sp = (i == npc - 1)
                        pa = pbuf[:, (2 * i) * 128:(2 * i + 1) * 128]
                        pb2 = pbuf[:, (2 * i + 1) * 128:(2 * i + 2) * 128]
                        mm.append((pbank(6, 128, oc), Vw[:, kb * 192: kb * 192 + 128], pa, st, sp))
                        mm.append((pbank(7, 128, oc), Vw[:, kb * 192 + 64: kb * 192 + 192], pb2, st, sp))
                    S.op("tensor", f_mms(mm), reads=[bp, b_Vw, b_const], writes=[bank[6], bank[7]], partial=(qb > 0))

                stage1(0)
                for qb in range(4):
                    if qb + 1 < 4:
                        stage1(qb + 1)
                    stage2(qb)
                finish_attn(512, sink_col, ydst, ob=6, db=7, dbc=5)

            SG = [(0, 1), (2, 3)]

            if stop == "A0":
                return finalize()
            for s in range(NSEQ):
                import os as _os
                _dbg = _os.environ.get("KDBG", "")
                if stop == "A0b":
                    return finalize()
                tiles = [("ctx", 0, CTX)] + [("lat", t * 512, 512) for t in range(NT)]
                blkc = 0
                for (kind, t0, n) in tiles:
                    is_ctx = (kind == "ctx")
                    v = 2 if is_ctx else s
                    src = src_c[s] if is_ctx else src_x[s]
                    nb = n // 128
                    if not is_ctx:
                        load_rope(t0, n)
                    for b in range(nb):
                        si = blkc % 2
                        norm_block(src[t0 + b * 128: t0 + (b + 1) * 128, :], xs_[si][:], b_xs[si], junk[si][:], b_junk[si],
                                   hT, b_hT, b * 128, l, v, 0, 1, si, si)
                        blkc += 1
                        if stop == "A1":
                            return finalize()
                    kbase = (L + t0) if is_ctx else t0
                    if not (is_ctx and last):
                        for g in range(4):
                            pb = 2 + g % 2
                            S.op("tensor", f_mms([(pbank(pb, n), WINc(c, g * 128, 128), hT[:, c * 512: c * 512 + n], c == 0, c == KC - 1) for c in range(KC)]),
                                 reads=[b_hT, b_WIN], writes=[bank[pb]])
                            if is_ctx:
                                dst = pTc[:, g * PWc + 8: g * PWc + 8 + n]; bd_ = b_pTc
                            else:
                                dst = pT[:, g * PW + 8 + t0: g * PW + 8 + t0 + n]; bd_ = b_pT
                            S.op("scalar", f_act(dst, pbank(pb, n), AF.Identity), reads=[bank[pb]], writes=[bd_], partial=True)
                    if stop == "A2":
                        return finalize()
                    for a, (wcol, KT, bKT, normed) in enumerate(((512, KTw, b_KTw, False), (640, KTg, b_KTg, True))):
                        pb = 2 + a
                        S.op("tensor", f_mms([(pbank(pb, n), WINc(c, wcol, 128), hT[:, c * 512: c * 512 + n], c == 0, c == KC - 1) for c in range(KC)]),
                             reads=[b_hT, b_WIN], writes=[bank[pb]])
                        rope_chunk(pbank(pb, n), bank[pb], n, t0, [(KT[:, kbase:kbase + n], bKT, 0, 128)], (4, 5), a, normed, 2 * l + 1, is_ctx=is_ctx)
                    if stop == "A3" or (stop == "A3b" and not is_ctx):
                        return finalize()
                    for b in range(nb):
                        pb = 6 + b % 2
                        S.op("tensor", f_mms([(pbank(pb, 256), hT[:, c * 512 + b * 128: c * 512 + (b + 1) * 128], WINc(c, 768, 256), c == 0, c == KC - 1) for c in range(KC)]),
                             reads=[b_hT, b_WIN], writes=[bank[pb]])
                        kblk = kbase // 128 + b
                        S.op("scalar", f_act(Vw[:, kblk * 192: kblk * 192 + 64], pbank(pb, 64, 0), AF.Identity), reads=[bank[pb]], writes=[b_Vw], partial=True)
                        S.op("scalar", f_act(Vw[:, kblk * 192 + 128: kblk * 192 + 192], pbank(pb, 64, 64), AF.Identity), reads=[bank[pb]], writes=[b_Vw], partial=True)
                        S.op("scalar", f_act(Vg[:, kblk * 192: kblk * 192 + 64], pbank(pb, 64, 128), AF.Identity), reads=[bank[pb]], writes=[b_Vg], partial=True)
                        S.op("scalar", f_act(Vg[:, kblk * 192 + 128: kblk * 192 + 192], pbank(pb, 64, 192), AF.Identity), reads=[bank[pb]], writes=[b_Vg], partial=True)
                        if stop is not None and stop.startswith("Vb") and not is_ctx and int(stop[2:]) == b:
                            return finalize()
                    if stop is not None and stop.startswith("At") and int(stop[2:]) * 512 == t0 + (0 if is_ctx else 512):
                        return finalize()

                if stop == "A":
                    return finalize()
                tilesB = ([] if last else [("ctx", 0, CTX)]) + [("lat", t * 512, 512) for t in range(NT)]

                def prepB(tile):
                    nonlocal blkc
                    (kind, t0, n) = tile
                    is_ctx = (kind == "ctx")
                    v = 2 if is_ctx else s
                    src = src_c[s] if is_ctx else src_x[s]
                    for b in range(n // 128):
                        si = blkc % 2
                        norm_block(src[t0 + b * 128: t0 + (b + 1) * 128, :], xs_[si][:], b_xs[si], junk[si][:], b_junk[si],
                                   hT, b_hT, b * 128, l, v, 0, 1, si, si)
                        blkc += 1

                prepB(tilesB[0])
                for ti, (kind, t0, n) in enumerate(tilesB):
                    is_ctx = (kind == "ctx")
                    if not is_ctx:
                        load_rope(t0, n)
                    stg = {}
                    for step in range(8 + 2):
                        if step - 2 >= 0:
                            stg[step - 2][2]()
                        if 0 <= step - 1 < 8:
                            stg[step - 1][1]()
                        if step < 8:
                            jq = step
                            pb = 2 + jq % 2
                            S.op("tensor", f_mms([(pbank(pb, n), WINc(c, 1024 + jq * 128, 128), hT[:, c * 512: c * 512 + n], c == 0, c == KC - 1) for c in range(KC)]),
                                 reads=[b_hT, b_WIN], writes=[bank[pb]])
                            outs = [(QA[:, jq * 512: jq * 512 + n], b_QA, 0, 64), (QB[:, jq * 512: jq * 512 + n], b_QB, 64, 128)]
                            stg[jq] = rope_stages(pbank(pb, n), bank[pb], n, t0, outs, (4, 5), jq % 2, jq >= 4, 2 * l, is_ctx=is_ctx)
                            stg[jq][0]()
                    if is_ctx:
                        pool_tile(pTc, b_pTc, PWc, n, 0, CTX, 0)
                    else:
                        pool_tile(pT, b_pT, PW, n, t0, L, 0)
                    for j in range(4):
                        if is_ctx:
                            attention(n, KTw, b_KTw, Vw, b_Vw, [NB, NB + 1], j, l * 4 + j, yst[:, (4 + j) * 512:(4 + j) * 512 + n], SG, ob=6, db=7, dbc=5)
                            attention(n, KTg, b_KTg, Vg, b_Vg, [NB, NB + 1], 4 + j, None, yst[:, (8 + j) * 512:(8 + j) * 512 + n], SG, ob=4, db=5, dbc=6)
                        else:
                            window_attention(j, t0, l * 4 + j, yst[:, (4 + j) * 512:(4 + j) * 512 + 512])
                            attention(512, KTg, b_KTg, Vg, b_Vg, list(range(NKB)), 4 + j, None, yst[:, (8 + j) * 512:(8 + j) * 512 + 512], SG, ob=4, db=5, dbc=6)
                        if j == 0 and ti + 1 < len(tilesB):
                            prepB(tilesB[ti + 1])
                    ycol = (L + t0) if is_ctx else t0
                    S.dma("sync", f_dma(yb[s].rearrange("k p t -> p k t")[:, :, ycol:ycol + n],
                                        yst[:].rearrange("p (k t) -> p k t", k=12)[:, :, 0:n]), b_yst, reads=[b_yst])
            S.barrier()
            if stop == "B":
                return finalize()

            state["top"] = P_END
            WBR = alloc("WBR", 3 * 4 * D, BF16); b_WBR = Buf("WBR")
            WGT = alloc("WGT", 3 * KC * D, BF16); b_WGT = Buf("WGT")
            WOU = alloc("WOU", KC * D, BF16); b_WOU = Buf("WOU")
            xq = [alloc("xq%d" % i, D, F32) for i in range(8)]; b_xq = [Buf("xq%d" % i) for i in range(8)]
            junkm = [alloc("junkm%d" % i, D, BF16) for i in range(2)]; b_junkm = [Buf("jm0"), Buf("jm1")]
            hTm = alloc("hTm", KC * 512, BF16); b_hTm = Buf("hTm")
            yT = [alloc("yT%d" % i, 12 * 512, BF16) for i in range(2)]; b_yT = [Buf("yT0"), Buf("yT1")]
            mT = alloc("mT", KC * 512, BF16); b_mT = Buf("mT")
            sg = [alloc("sg%d" % i, 512, F32) for i in range(2)]; b_sg = [Buf("sg0"), Buf("sg1")]
            macc = alloc("macc", 512, F32); b_macc = Buf("macc")
            mtmp = alloc("mtmp", 512, F32); b_mtmp = Buf("mtmp")
            gt1 = [alloc("gt1_%d" % i, D, F32) for i in range(2)]; b_gt1 = [Buf("gt1s"), Buf("gt1c")]
            otmp = [alloc("otmp%d" % i, 512, F32) for i in range(2)]; b_otmp = [Buf("ot0"), Buf("ot1")]

            S.dma("gpsimd", f_dma(WBR[:, 0:4 * D].rearrange("p (c n) -> p c n", c=4), w_branch[l, 0].rearrange("(c p) n -> p c n", p=128)),
                  b_WBR, writes=[b_WBR], partial=True)
            for i in (1, 2):
                for j in range(4):
                    for h in range(2):
                        r0 = (h * 4 + j) * 64
                        S.dma("gpsimd", f_dma(WBR[h * 64:(h + 1) * 64, (i * 4 + j) * D:(i * 4 + j + 1) * D], w_branch[l, i, r0:r0 + 64, :]),
                              b_WBR, writes=[b_WBR], partial=True)
            for i in range(3):
                S.dma("gpsimd", f_dma(WGT[:, i * KC * D:(i + 1) * KC * D].rearrange("p (c n) -> p c n", c=KC), w_gate[l, i].rearrange("(c p) n -> p c n", p=128)),
                      b_WGT, writes=[b_WGT], partial=True)
            S.dma("gpsimd", f_dma(WOU[:].rearrange("p (c n) -> p c n", c=KC), w_out[l].rearrange("(c p) n -> p c n", p=128)), b_WOU, writes=[b_WOU])

            dst_x = xsB
            dst_c = csB
            for s in range(NSEQ):
                load_gate_tiles([(gt1[0], b_gt1[0], l, s, 2)] + ([] if last else [(gt1[1], b_gt1[1], l, 2, 2)]))
                tilesM = ([] if last else [("ctx", 0, CTX)]) + [("lat", t * 512, 512) for t in range(NT)]

                def prepM_y(ti):
                    (kind, t0, n) = tilesM[ti]
                    ycol = (L + t0) if kind == "ctx" else t0
                    yt = yT[ti % 2]; byt = b_yT[ti % 2]
                    S.dma("sync", f_dma(yt[:].rearrange("p (k t) -> p k t", k=12)[:, :, 0:n], yb[s].rearrange("k p t -> p k t")[:, :, ycol:ycol + n]),
                          byt, writes=[byt])

                def prepM_blk(ti, b, mode="all"):
                    (kind, t0, n) = tilesM[ti]
                    is_ctx = (kind == "ctx")
                    v = 2 if is_ctx else s
                    src = src_c[s] if is_ctx else src_x[s]
                    q = (ti % 2) * 4 + b
                    norm_block(src[t0 + b * 128: t0 + (b + 1) * 128, :], xq[q][:], b_xq[q], junkm[b % 2][:], b_junkm[b % 2],
                               hTm, b_hTm, b * 128, l, v, 0, 1, b % 2, b % 2, mode=mode)

                prepM_y(0)
                for b in range(tilesM[0][2] // 128):
                    prepM_blk(0, b)
                for ti, (kind, t0, n) in enumerate(tilesM):
                    is_ctx = (kind == "ctx")
                    dstd = dst_c[s] if is_ctx else dst_x[s]
                    gtile = gt1[1] if is_ctx else gt1[0]
                    bgt = b_gt1[1] if is_ctx else b_gt1[0]
                    nb = n // 128
                    yt = yT[ti % 2]; byt = b_yT[ti % 2]
                    for oc in range(KC):
                        for i in range(3):
                            pg = 2 + i % 2
                            S.op("tensor", f_mms([(pbank(pg, n), WGT[:, (i * KC + c) * D + oc * 128:(i * KC + c) * D + (oc + 1) * 128], hTm[:, c * 512: c * 512 + n], c == 0, c == KC - 1)
                                                  for c in range(KC)]), reads=[b_hTm, b_WGT], writes=[bank[pg]])
                            S.op("scalar", f_act(sg[i % 2][:, 0:n], pbank(pg, n), AF.Sigmoid, bias=bgT[:, (l * 3 + i) * 8 + oc:(l * 3 + i) * 8 + oc + 1], scale=1.0),
                                 reads=[bank[pg], b_const], writes=[b_sg[i % 2]])
                            pbx = 4 + i % 2
                            S.op("tensor", f_mms([(pbank(pbx, n), WBR[:, (i * 4 + c) * D + oc * 128:(i * 4 + c) * D + (oc + 1) * 128], yt[:, (i * 4 + c) * 512:(i * 4 + c) * 512 + n], c == 0, c == 3)
                                                  for c in range(4)]), reads=[byt, b_WBR], writes=[bank[pbx]])
                            if i == 0:
                                S.op("vector", f_tt(macc[:, 0:n], pbank(pbx, n), sg[i % 2][:, 0:n], ALU.mult), reads=[bank[pbx], b_sg[i % 2]], writes=[b_macc])
                            elif i == 1:
                                S.op("vector", f_tt(mtmp[:, 0:n], pbank(pbx, n), sg[i % 2][:, 0:n], ALU.mult), reads=[bank[pbx], b_sg[i % 2]], writes=[b_mtmp])
                                S.op("gpsimd", f_tt(macc[:, 0:n], macc[:, 0:n], mtmp[:, 0:n], ALU.add), reads=[b_macc, b_mtmp], writes=[b_macc])
                            else:
                                S.op("vector", f_tt(mtmp[:, 0:n], pbank(pbx, n), sg[i % 2][:, 0:n], ALU.mult), reads=[bank[pbx], b_sg[i % 2]], writes=[b_mtmp])
                                S.op("gpsimd", f_tt(mT[:, oc * 512: oc * 512 + n], macc[:, 0:n], mtmp[:, 0:n], ALU.add), reads=[b_macc, b_mtmp], writes=[b_mT], partial=True)
                    has_next = ti + 1 < len(tilesM)
                    pending = list(range(tilesM[ti + 1][2] // 128)) if has_next else []
                    if has_next:
                        prepM_y(ti + 1)
                        for b2 in pending:
                            prepM_blk(ti + 1, b2, mode="load")
                    for b in range(nb):
                        q = (ti % 2) * 4 + b
                        for hf in range(2):
                            po = 6 + hf
                            S.op("tensor", f_mms([(pbank(po), mT[:, c * 512 + b * 128: c * 512 + (b + 1) * 128], WOU[:, c * D + hf * 512: c * D + (hf + 1) * 512], c == 0, c == KC - 1)
                                                  for c in range(KC)]), reads=[b_mT, b_WOU], writes=[bank[po]])
                            S.op("vector", f_tt(otmp[hf][:], pbank(po), gtile[:, hf * 512:(hf + 1) * 512], ALU.mult), reads=[bank[po], bgt], writes=[b_otmp[hf]])
                            S.op("vector", f_tt(xq[q][:, hf * 512:(hf + 1) * 512], xq[q][:, hf * 512:(hf + 1) * 512], otmp[hf][:], ALU.add),
                                 reads=[b_xq[q], b_otmp[hf]], writes=[b_xq[q]])
                        S.dma("sync", f_dma(dstd[t0 + b * 128: t0 + (b + 1) * 128, :], xq[q][:]), b_xq[q], reads=[b_xq[q]])
                        if pending:
                            prepM_blk(ti + 1, pending.pop(0), mode="compute")
                    while pending:
                        prepM_blk(ti + 1, pending.pop(0), mode="compute")
            S.barrier()
            if stop == "M":
                return finalize()

            state["top"] = P_END
            WG = alloc("WG", KC * DFF, BF16); b_WG = Buf("WG")
            WV = alloc("WV", KC * DFF, BF16); b_WV = Buf("WV")
            WD = alloc("WD", FC * D, BF16); b_WD = Buf("WD")
            xc = [alloc("xc%d" % i, D, F32) for i in range(4)]; b_xc = [Buf("xc%d" % i) for i in range(4)]
            xh = alloc("xh", D, F32); b_xh = Buf("xh")
            junkc = [alloc("junkc%d" % i, D, BF16) for i in range(2)]; b_junkc = [Buf("jc0"), Buf("jc1")]
            hTc = alloc("hTc", KC * 512, BF16); b_hTc = Buf("hTc")
            hTh = alloc("hTh", KC * 2, BF16); b_hTh = Buf("hTh")
            ghs = alloc("ghs", 2 * FC, F32); b_ghs = Buf("ghs")
            junkh = alloc("junkh", D, BF16); b_junkh = Buf("junkh")
            S.op("gpsimd", f_memset(junkh[:], 0.0), writes=[b_junkh])
            uT = alloc("uT", FC * 512, BF16); b_uT = Buf("uT")
            av = [alloc("av%d" % i, 512, F32) for i in range(2)]; b_av = [Buf("av0"), Buf("av1")]
            sa = [alloc("sa%d" % i, 512, BF16) for i in range(2)]; b_sa = [Buf("sa0"), Buf("sa1")]
            gt2 = [alloc("gt2_%d" % i, D, F32) for i in range(2)]; b_gt2 = [Buf("gt2s"), Buf("gt2c")]
            fgt = gt2[1]; b_fgt = b_gt2[1]
            oc2 = av; b_oc2 = b_av

            S.dma("gpsimd", f_dma(WG[:].rearrange("p (c n) -> p c n", c=KC), w_ffg[l].rearrange("(c p) n -> p c n", p=128)), b_WG, writes=[b_WG])
            S.dma("gpsimd", f_dma(WV[:].rearrange("p (c n) -> p c n", c=KC), w_ffv[l].rearrange("(c p) n -> p c n", p=128)), b_WV, writes=[b_WV])
            S.dma("gpsimd", f_dma(WD[:].rearrange("p (c n) -> p c n", c=FC), w_ffd[l].rearrange("(c p) n -> p c n", p=128)), b_WD, writes=[b_WD])
            if last:
                S.dma("sync", f_dma(fgt[:], final_g.rearrange("(o n) -> o n", o=1).broadcast_to([128, D])), b_fgt, writes=[b_fgt])

            def cv(k, f):
                i = (l * 4 + k) * FC + f
                return cvT[:, i:i + 1]

            srcC_x = xsB
            srcC_c = csB
            dstC_x = y_out if last else xsA
            dstC_c = csA
            for s in range(NSEQ):
                load_gate_tiles([(gt2[0], b_gt2[0], l, s, 5)] + ([] if last else [(gt2[1], b_gt2[1], l, 2, 5)]))
                tilesC = ([] if last else [("ctx", 0, CTX)]) + [("lat", t * 512, 512) for t in range(NT)]

                def tinfo(ti):
                    (kind, t0, n) = tilesC[ti]
                    is_ctx = (kind == "ctx")
                    Lseq = CTX if is_ctx else L
                    src = srcC_c[s] if is_ctx else srcC_x[s]
                    return is_ctx, t0, n, (2 if is_ctx else s), src, (t0 > 0), (t0 + n < Lseq)

                def prepC_blk(ti, b, mode="all"):
                    is_ctx, t0, n, v, src, has_l, has_r = tinfo(ti)
                    norm_block(src[t0 + b * 128: t0 + (b + 1) * 128, :], xc[b][:], b_xc[b], junkc[b % 2][:], b_junkc[b % 2],
                               hTc, b_hTc, b * 128, l, v, 2, 3, b % 2, b % 2, mode=mode)

                def prepC_halo(ti):
                    is_ctx, t0, n, v, src, has_l, has_r = tinfo(ti)
                    if not (has_l or has_r):
                        return
                    if has_l:
                        S.dma("sync", f_dma(xh[0:1, :], src[t0 - 1:t0, :]), b_xh, writes=[b_xh], partial=True)
                    if has_r:
                        S.dma("sync", f_dma(xh[1:2, :], src[t0 + n:t0 + n + 1, :]), b_xh, writes=[b_xh], partial=True)
                    if not (has_l and has_r):
                        if has_l:
                            S.dma("sync", f_dma(xh[1:2, :], src[t0 - 1:t0, :]), b_xh, writes=[b_xh], partial=True)
                        else:
                            S.dma("sync", f_dma(xh[0:1, :], src[t0 + n:t0 + n + 1, :]), b_xh, writes=[b_xh], partial=True)
                    msh = stat[0:2, 8:9]; rsh = stat[0:2, 9:10]
                    S.op("scalar", f_act(junkh[0:2, :], xh[0:2, :], AF.Square, scale=1.0 / 32.0, accum=msh), reads=[b_xh], writes=[b_junkh, b_stat[4]])
                    S.op("scalar", f_act(rsh, msh, AF.Sqrt, bias=epst[0:2, 0:1], scale=1.0), reads=[b_stat[4], b_const], writes=[b_stat[4]])
                    S.op("vector", f_recip(rsh, rsh), reads=[b_stat[4]], writes=[b_stat[4]])
                    S.op("vector", f_ts(junkh[0:2, :], xh[0:2, :], rsh, None, ALU.mult), reads=[b_xh, b_stat[4]], writes=[b_junkh])
                    pt = pbank(0).bitcast(BF16)
                    S.op("tensor", f_transposes([(pt[:, c * 128:(c + 1) * 128], junkh[:, c * 128:(c + 1) * 128], identb) for c in range(KC)]),
                         reads=[b_junkh, b_const], writes=[bank[0]])
                    for c in range(KC):
                        S.op("vector", f_ts(hTh[:, 2 * c:2 * c + 2], pt[:, c * 128:c * 128 + 2], modT_ap(l, v, 3, c), modT_ap(l, v, 2, c), ALU.mult, ALU.add),
                             reads=[bank[0], b_modT], writes=[b_hTh], partial=True)

                def mainC1(ti):
                    is_ctx, t0, n, v, src, has_l, has_r = tinfo(ti)
                    if has_l or has_r:
                        for f in range(FC):
                            S.op("tensor", f_mms([(pbank(5, 2, 2 * f), WG[:, c * DFF + f * 128: c * DFF + (f + 1) * 128], hTh[:, 2 * c:2 * c + 2], c == 0, c == KC - 1) for c in range(KC)]),
                                 reads=[b_hTh, b_WG], writes=[bank[5]], partial=(f > 0))
                        S.op("vector", f_copy(ghs[:], pbank(5, 2 * FC)), reads=[bank[5]], writes=[b_ghs])
                    for f in range(FC):
                        pg = 1 + f % 2
                        pv = 3 + f % 2
                        S.op("tensor", f_mms([(pbank(pg, n), WG[:, c * DFF + f * 128: c * DFF + (f + 1) * 128], hTc[:, c * 512: c * 512 + n], c == 0, c == KC - 1) for c in range(KC)]),
                             reads=[b_hTc, b_WG], writes=[bank[pg]])
                        S.op("tensor", f_mms([(pbank(pv, n), WV[:, c * DFF + f * 128: c * DFF + (f + 1) * 128], hTc[:, c * 512: c * 512 + n], c == 0, c == KC - 1) for c in range(KC)]),
                             reads=[b_hTc, b_WV], writes=[bank[pv]])
                        a_ = av[f % 2]; ba = b_av[f % 2]
                        S.op("vector", f_ts(a_[:, 0:n], pbank(pg, n), cv(1, f), cv(3, f), ALU.mult, ALU.add), reads=[bank[pg], b_const], writes=[ba])
                        S.op("vector", f_stt(a_[:, 1:n], pbank(pg, n - 1), cv(0, f), a_[:, 1:n], ALU.mult, ALU.add), reads=[bank[pg], ba, b_const], writes=[ba])
                        S.op("vector", f_stt(a_[:, 0:n - 1], pbank(pg, n - 1, 1), cv(2, f), a_[:, 0:n - 1], ALU.mult, ALU.add), reads=[bank[pg], ba, b_const], writes=[ba])
                        if has_l:
                            S.op("vector", f_stt(a_[:, 0:1], ghs[:, 2 * f:2 * f + 1], cv(0, f), a_[:, 0:1], ALU.mult, ALU.add), reads=[b_ghs, ba, b_const], writes=[ba])
                        if has_r:
                            S.op("vector", f_stt(a_[:, n - 1:n], ghs[:, 2 * f + 1:2 * f + 2], cv(2, f), a_[:, n - 1:n], ALU.mult, ALU.add), reads=[b_ghs, ba, b_const], writes=[ba])
                        s_ = sa[f % 2]; bs = b_sa[f % 2]
                        S.op("scalar", f_act(s_[:, 0:n], a_[:, 0:n], AF.Silu), reads=[ba], writes=[bs])
                        S.op("vector", f_tt(uT[:, f * 512: f * 512 + n], pbank(pv, n), s_[:, 0:n], ALU.mult), reads=[bank[pv], bs], writes=[b_uT], partial=True)

                def mainC2_blk(ti, b):
                    is_ctx, t0, n, v, src, has_l, has_r = tinfo(ti)
                    dstd = dstC_c[s] if is_ctx else dstC_x[s]
                    gtile = gt2[1] if is_ctx else gt2[0]
                    bgt = b_gt2[1] if is_ctx else b_gt2[0]
                    for hf in range(2):
                        po = 6 + hf
                        S.op("tensor", f_mms([(pbank(po), uT[:, f * 512 + b * 128: f * 512 + (b + 1) * 128], WD[:, f * D + hf * 512: f * D + (hf + 1) * 512], f == 0, f == FC - 1)
                                              for f in range(FC)]), reads=[b_uT, b_WD], writes=[bank[po]])
                        S.op("vector", f_tt(oc2[hf][:], pbank(po), gtile[:, hf * 512:(hf + 1) * 512], ALU.mult), reads=[bank[po], bgt], writes=[b_oc2[hf]])
                        S.op("gpsimd", f_tt(xc[b][:, hf * 512:(hf + 1) * 512], xc[b][:, hf * 512:(hf + 1) * 512], oc2[hf][:], ALU.add),
                             reads=[b_xc[b], b_oc2[hf]], writes=[b_xc[b]])
                    if last and not is_ctx:
                        msf = stat[:, 12:13]; rsf = stat[:, 13:14]
                        S.op("scalar", f_act(junkc[b % 2][:], xc[b][:], AF.Square, scale=1.0 / 32.0, accum=msf), reads=[b_xc[b]], writes=[b_junkc[b % 2], b_stat[6]])
                        S.op("scalar", f_act(rsf, msf, AF.Sqrt, bias=epst[:, 0:1], scale=1.0), reads=[b_stat[6], b_const], writes=[b_stat[6]])
                        S.op("vector", f_recip(rsf, rsf), reads=[b_stat[6]], writes=[b_stat[6]])
                        S.op("vector", f_stt(xc[b][:], xc[b][:], rsf, fgt[:], ALU.mult, ALU.mult), reads=[b_xc[b], b_stat[6], b_fgt], writes=[b_xc[b]])
                    S.dma("sync", f_dma(dstd[t0 + b * 128: t0 + (b + 1) * 128, :], xc[b][:]), b_xc[b], reads=[b_xc[b]])

                for b in range(tilesC[0][2] // 128):
                    prepC_blk(0, b)
                prepC_halo(0)
                for ti in range(len(tilesC)):
                    nb = tilesC[ti][2] // 128
                    mainC1(ti)
                    has_next = ti + 1 < len(tilesC)
                    nbn = tilesC[ti + 1][2] // 128 if has_next else 0
                    pending = list(range(nbn))
                    loaded = set()
                    for b in range(nb):
                        mainC2_blk(ti, b)
                        if b < nbn:
                            prepC_blk(ti + 1, b, mode="load")
                            loaded.add(b)
                        if b >= 2 and pending and pending[0] in loaded:
                            prepC_blk(ti + 1, pending.pop(0), mode="compute")
                    while pending:
                        b2 = pending.pop(0)
                        if b2 not in loaded:
                            prepC_blk(ti + 1, b2, mode="load")
                            loaded.add(b2)
                        prepC_blk(ti + 1, b2, mode="compute")
                    if has_next:
                        prepC_halo(ti + 1)
            S.barrier()

        S.barrier()
        sems = {k: es.enter_context(nc.semaphore(k)) for k in S.semkeys}
        S.run(sems)
    return nc


_WEIGHT_NAMES = ["w_mod", "b_mod", "norm1_g", "norm2_g", "w_in", "w_pool_grp", "pool_scale", "win_sink",
                 "q_norm_g", "k_norm_g", "w_branch", "w_gate", "b_gate", "w_out", "w_ff_gate", "w_ff_val",
                 "conv_w", "conv_b", "w_ff_down", "final_g"]

_NC_CACHE = {}


def make_in_maps(inputs, L, n_cores):
    consts = host_consts(L)
    x = np.ascontiguousarray(np.asarray(inputs["x"], dtype=np.float32))
    c = np.asarray(inputs["c"], dtype=np.float32)
    ctx = np.ascontiguousarray(np.asarray(inputs["ctx"], dtype=np.float32))
    c_ctx = np.asarray(inputs["c_ctx"], dtype=np.float32)
    shared = {k: np.ascontiguousarray(np.asarray(inputs[k], dtype=np.float32)) for k in _WEIGHT_NAMES}
    shared.update(consts)
    maps = []
    for i in range(n_cores):
        m = dict(shared)
        m["x"] = x[NSEQ * i: NSEQ * (i + 1)]
        m["ctx"] = ctx[NSEQ * i: NSEQ * (i + 1)]
        m["c3"] = np.ascontiguousarray(np.concatenate([c[NSEQ * i: NSEQ * (i + 1)], c_ctx[None, :]], axis=0))
        maps.append(m)
    return maps


def kernel(**inputs):
    x = inputs["x"]
    B, L, _ = x.shape
    n_cores = B // NSEQ
    if L not in _NC_CACHE:
        _NC_CACHE[L] = build_nc(L)
    nc = _NC_CACHE[L]
    in_maps = make_in_maps(inputs, L, n_cores)
    res = run_bass_kernel_spmd(nc, in_maps, core_ids=list(range(n_cores)))
    out = np.concatenate([np.asarray(r["y"]) for r in res.results], axis=0)
    return out.astype(np.float32)
```

```python
import contextlib
import numpy as np
import concourse.bass as bass
import concourse.mybir as mybir
from concourse.bass_utils import run_bass_kernel_spmd

F32 = mybir.dt.float32
BF16 = mybir.dt.bfloat16
AF = mybir.ActivationFunctionType
ALU = mybir.AluOpType

D = 1024
KC = 8
CTX = 256
HD = 64
DFF = 2816
FC = 22
NL = 2
NSEQ = 2
GRID_W = 64
EPS = 1e-6
SM_SCALE = HD ** -0.5
N_CORES = 8


class Buf:
    __slots__ = ("name", "writers", "readers", "dsem", "dcount", "excl", "full")

    def __init__(self, name, excl=False):
        self.name = name
        self.writers = []
        self.readers = []
        self.dsem = None
        self.dcount = 0
        self.excl = excl
        self.full = []


class Eng:
    def __init__(self, name):
        self.name = name
        self.ops = []
        self.count = 0
        self.seen = {}


class Sched:
    def __init__(self, nc):
        self.nc = nc
        self.eng = {n: Eng(n) for n in ("tensor", "vector", "scalar", "gpsimd", "sync")}
        self.semkeys = ["e_" + n for n in self.eng]
        self.dma_latest = {}
        self.n_dsem = 0
        self.dsem_pool = {}

    def _deps(self, reads, writes, partial, own=None):
        ev = []
        for b in reads:
            ev.extend(b.writers)
            if b.excl:
                ev.extend(r for r in b.readers if r[0] != own)
        for b in writes:
            ev.extend(b.readers)
            if not (partial and not b.readers):
                ev.extend(b.writers)
            else:
                ev.extend(b.full)
        return ev

    def _commit(self, reads, writes, partial, event):
        for b in writes:
            if partial and not b.readers:
                b.writers.append(event)
            else:
                b.writers = [event]
                b.readers = []
                b.full = [] if partial else [event]
        for b in reads:
            b.readers.append(event)
            if len(b.readers) > 64:
                best = {}
                for (k, v) in b.readers:
                    if best.get(k, 0) < v:
                        best[k] = v
                b.readers = list(best.items())

    def _emit_waits(self, e, events, skip_own):
        need = {}
        for (k, v) in events:
            if k[0] == "d":
                v = self.dma_latest[k]
            if skip_own and k == "e_" + e.name:
                continue
            if need.get(k, 0) < v:
                need[k] = v
        for k, v in need.items():
            if e.seen.get(k, 0) >= v:
                continue
            e.seen[k] = v
            e.ops.append(("wait", k, v))

    def op(self, engine, fn, reads=(), writes=(), partial=False):
        e = self.eng[engine]
        ev = self._deps(reads, writes, partial, "e_" + engine)
        self._emit_waits(e, ev, engine == "tensor")
        e.count += 1
        event = ("e_" + engine, e.count)
        e.ops.append(("op", fn, "e_" + engine))
        self._commit(reads, writes, partial, event)
        return event

    def dma(self, queue, fn, sb, reads=(), writes=(), partial=False):
        e = self.eng[queue]
        if sb.dsem is None:
            sb.dsem = {}
            sb.dcount = {}
        if queue not in sb.dsem:
            key = "d_%d" % self.n_dsem
            self.n_dsem += 1
            sb.dsem[queue] = key
            sb.dcount[queue] = 0
            self.semkeys.append(key)
            self.dma_latest[key] = 0
        key = sb.dsem[queue]
        ev = self._deps(reads, writes, partial)
        self._emit_waits(e, ev, False)
        sb.dcount[queue] += 16
        self.dma_latest[key] = sb.dcount[queue]
        event = (key, sb.dcount[queue])
        e.ops.append(("dma", fn, key))
        self._commit(reads, writes, partial, event)
        return event

    def barrier(self):
        targets = {}
        for n, e in self.eng.items():
            if e.count:
                targets["e_" + n] = e.count
        for k, v in self.dma_latest.items():
            if v:
                targets[k] = v
        for n, e in self.eng.items():
            for k, v in targets.items():
                if e.seen.get(k, 0) >= v:
                    continue
                if k == "e_" + n and n == "tensor":
                    pass
                e.seen[k] = v
                e.ops.append(("wait", k, v))

    def run(self, sems):
        nc = self.nc

        def replay(ename):
            def body(h):
                for item in self.eng[ename].ops:
                    if item[0] == "wait":
                        h.wait_ge(sems[item[1]], item[2])
                    elif item[0] == "op":
                        item[1](h).then_inc(sems[item[2]], 1)
                    else:
                        item[1](h).then_inc(sems[item[2]], 16)
            return body

        with nc.Block() as block:
            block.sync(replay("sync"))
            block.tensor(replay("tensor"))
            block.vector(replay("vector"))
            block.scalar(replay("scalar"))
            block.gpsimd(replay("gpsimd"))


def f_act(out, in_, func, bias=None, scale=None, accum=None):
    def fn(e):
        kw = {}
        if bias is not None:
            kw["bias"] = bias
        if scale is not None:
            kw["scale"] = scale
        if accum is not None:
            kw["accum_out"] = accum
        return e.activation(out=out, in_=in_, func=func, **kw)
    return fn


def f_tt(out, a, b, op):
    return lambda e: e.tensor_tensor(out=out, in0=a, in1=b, op=op)


def f_ts(out, a, s1, s2, op0, op1=None):
    if op1 is None:
        return lambda e: e.tensor_scalar(out=out, in0=a, scalar1=s1, scalar2=None, op0=op0)
    return lambda e: e.tensor_scalar(out=out, in0=a, scalar1=s1, scalar2=s2, op0=op0, op1=op1)


def f_stt(out, in0, scalar, in1, op0, op1):
    return lambda e: e.scalar_tensor_tensor(out=out, in0=in0, scalar=scalar, in1=in1, op0=op0, op1=op1)


def f_copy(out, in_):
    return lambda e: e.tensor_copy(out=out, in_=in_)


def f_recip(out, in_):
    return lambda e: e.reciprocal(out=out, in_=in_)


def f_memset(ap, v):
    return lambda e: e.memset(ap, v)


def f_dma(out, in_):
    return lambda e: e.dma_start(out=out, in_=in_)


def f_mms(lst):
    def fn(e):
        ins = None
        for (o, l, r, st, sp) in lst:
            ins = e.matmul(o, lhsT=l, rhs=r, start=st, stop=sp)
        return ins
    return fn


def f_transposes(lst):
    def fn(e):
        ins = None
        for (o, i, idn) in lst:
            ins = e.transpose(out=o, in_=i, identity=idn)
        return ins
    return fn


def host_consts(L):
    rows = L // GRID_W
    t = np.arange(L)
    row = (t // GRID_W).astype(np.float32)
    col = (t % GRID_W).astype(np.float32)
    half = HD // 2
    inv = (np.float32(10000.0) ** (-np.arange(0, half, 2, dtype=np.float32) / np.float32(half))).astype(np.float32)
    cosT = np.zeros((128, L), np.float32)
    sinT = np.zeros((128, L), np.float32)
    for p in range(128):
        d = p % 64
        axis = d // 32
        hf = (d % 32) // 16
        f = d % 16
        pos = row if axis == 0 else col
        ang = (pos * inv[f]).astype(np.float32)
        cosT[p] = np.cos(ang).astype(np.float32)
        s = np.sin(ang).astype(np.float32)
        sinT[p] = -s if hf == 0 else s
    ident = np.eye(128, dtype=np.float32)
    pswap = np.zeros((128, 128), np.float32)
    for p in range(128):
        q = p + 16 if (p % 32) < 16 else p - 16
        pswap[q, p] = 1.0
    bd = np.zeros((128, 128), np.float32)
    bd[0:64, 0:64] = 1.0 / 64
    bd[64:128, 64:128] = 1.0 / 64
    onesA = np.zeros((128, 128), np.float32); onesA[:, 0:64] = 1.0
    onesB = np.zeros((128, 128), np.float32); onesB[:, 64:128] = 1.0
    kk = np.arange(128)[:, None]
    qq = np.arange(128)[None, :]
    mge = (kk >= qq).astype(np.float32)
    mle = (kk <= qq).astype(np.float32)
    NEGM = np.float32(-30000.0)
    masks = np.concatenate([(1 - mge) * NEGM, (1 - mge) * NEGM, (1 - mle) * NEGM, (1 - mle) * NEGM], axis=1)
    edge = np.zeros((4, 16), np.float32)
    for g, w in enumerate((2, 4, 8, 16)):
        for i in range(8):
            edge[g, i] = float(w) / min(w, i + w // 2)
            edge[g, 8 + i] = float(w) / min(w, w // 2 + 8 - i)
    edge = np.broadcast_to(edge.reshape(1, 64), (128, 64)).copy()
    cbf = np.concatenate([pswap, bd, onesA, onesB, masks], axis=1)
    sel = np.zeros((128, 128), np.float32)
    sel[64, 0:64] = 1.0
    sel[0, 64:128] = 1.0
    return {"ropec": cosT, "ropes": sinT, "ident": ident, "cbf": cbf, "edge": edge, "sel": sel}


class _Stop(Exception):
    pass


def build_nc(L, debug=False, stop=None):
    assert L % 512 == 0
    NB = L // 128
    NT = L // 512
    LK = L + CTX
    NKB = NB + 2
    nc = bass.Bass("TRN2", target_bir_lowering=False)

    def din(name, shape, dt=F32):
        return nc.dram_tensor(name, list(shape), dt, kind="ExternalInput").ap()

    x_in = din("x", [NSEQ, L, D])
    ctx_in = din("ctx", [NSEQ, CTX, D])
    c3_in = din("c3", [3, D])
    w_mod = din("w_mod", [NL, D, 6 * D])
    b_mod = din("b_mod", [NL, 6 * D])
    norm1_g = din("norm1_g", [NL, D])
    norm2_g = din("norm2_g", [NL, D])
    w_in = din("w_in", [NL, D, 2048])
    w_pool = din("w_pool_grp", [NL, 4, 128, 128])
    pool_scale = din("pool_scale", [NL, 512])
    win_sink = din("win_sink", [NL, 8])
    q_norm_g = din("q_norm_g", [NL, HD])
    k_norm_g = din("k_norm_g", [NL, HD])
    w_branch = din("w_branch", [NL, 3, 512, D])
    w_gate = din("w_gate", [NL, 3, D, D])
    b_gate = din("b_gate", [NL, 3, D])
    w_out = din("w_out", [NL, D, D])
    w_ffg = din("w_ff_gate", [NL, D, DFF])
    w_ffv = din("w_ff_val", [NL, D, DFF])
    conv_w = din("conv_w", [NL, 3, DFF])
    conv_b = din("conv_b", [NL, DFF])
    w_ffd = din("w_ff_down", [NL, DFF, D])
    final_g = din("final_g", [D])
    ropec_in = din("ropec", [128, L])
    ropes_in = din("ropes", [128, L])
    ident_in = din("ident", [128, 128])
    cbf_in = din("cbf", [128, 1024])
    edge_in = din("edge", [128, 64])
    sel_in = din("sel", [128, 128])

    y_out = nc.dram_tensor("y", [NSEQ, L, D], F32, kind="ExternalOutput").ap()
    okind = "ExternalOutput" if debug else "Internal"
    xsA = nc.dram_tensor("xsA", [NSEQ, L, D], F32, kind=okind).ap()
    xsB = nc.dram_tensor("xsB", [NSEQ, L, D], F32, kind=okind).ap()
    csA = nc.dram_tensor("csA", [NSEQ, CTX, D], F32, kind=okind).ap()
    csB = nc.dram_tensor("csB", [NSEQ, CTX, D], F32, kind=okind).ap()
    yb = nc.dram_tensor("yb", [NSEQ, 12, 128, LK], BF16, kind=okind).ap()
    modd = nc.dram_tensor("modd", [NL, 3, 6 * D], F32, kind=okind).ap()

    S = Sched(nc)
    es = contextlib.ExitStack()
    with es:
        SB_BASE = 16576
        SB_LIMIT = 229376
        state = {"persist": 0, "top": SB_BASE}

        def alloc(name, free_elems, dt, base=None):
            nbytes = free_elems * (4 if dt == F32 else 2)
            off = state["top"] if base is None else base
            off = (off + 31) // 32 * 32
            t = nc.alloc_sbuf_tensor_at(name, [128, free_elems], dt, offset=off)
            if base is None:
                state["top"] = off + nbytes
                assert state["top"] <= SB_LIMIT, (name, state["top"])
            return t

        cnt = [0]

        def palloc(name, free_elems, dt):
            cnt[0] += 1
            return alloc("%s_%d" % (name, cnt[0]), free_elems, dt)

        ps = es.enter_context(nc.psum_tensor("ps", [128, 4096], F32))
        bank = [Buf("bank%d" % i, excl=True) for i in range(8)]

        def pbank(b, n=512, off=0):
            return ps[:, b * 512 + off: b * 512 + off + n]

        ident = alloc("ident", 128, F32); b_const = Buf("const")
        cbf = alloc("cbf", 1024, BF16)
        pswap = cbf[:, 0:128]
        bdm = cbf[:, 128:256]
        onesA = cbf[:, 256:384]
        onesB = cbf[:, 384:512]
        maskLR = cbf[:, 512:1024]
        edge = alloc("edge", 64, F32)
        selT = alloc("selT", 128, F32)
        epst = alloc("epst", 1, F32)
        modT = alloc("modT", NL * 3 * 4 * 8, F32); b_modT = Buf("modT")
        gT = alloc("gT", 4 * 8, F32)
        bgT = alloc("bgT", NL * 3 * 8, F32)
        pscT = alloc("pscT", NL * 4, F32)
        cvT = alloc("cvT", NL * 4 * FC, F32)
        qkg = alloc("qkg", NL * 2, F32)
        sinkT = alloc("sinkT", NL * 4, F32)
        stat = alloc("stat", 16, F32); b_stat = [Buf("stat%d" % i) for i in range(8)]
        identb_t = alloc("identb", 128, BF16)
        identb = identb_t[:]
        P_END = state["top"]

        def modT_ap(l, v, k, c):
            i = ((l * 3 + v) * 4 + k) * 8 + c
            return modT[:, i:i + 1]

        S.dma("sync", f_dma(ident[:], ident_in), b_const, writes=[b_const], partial=True)
        S.dma("gpsimd", f_dma(cbf[:], cbf_in), b_const, writes=[b_const], partial=True)
        S.dma("sync", f_dma(edge[:], edge_in), b_const, writes=[b_const], partial=True)
        S.dma("sync", f_dma(selT[:], sel_in), b_const, writes=[b_const], partial=True)
        S.op("vector", f_memset(epst[:], EPS), writes=[b_const], partial=True)
        S.op("vector", f_copy(identb, ident[:]), reads=[b_const], writes=[b_const])

        def small_T(dst, src_ap):
            def fn(e):
                with nc.allow_non_contiguous_dma(reason="tiny per-feature vectors, loaded once"):
                    return e.dma_start(out=dst, in_=src_ap.rearrange("(c p) -> p c", p=128))
            S.dma("sync", fn, b_const, writes=[b_const], partial=True)

        for l in range(NL):
            small_T(gT[:, l * 8:(l + 1) * 8], norm1_g[l])
            small_T(gT[:, 16 + l * 8:16 + (l + 1) * 8], norm2_g[l])
            for i in range(3):
                small_T(bgT[:, (l * 3 + i) * 8:(l * 3 + i + 1) * 8], b_gate[l, i])
                small_T(cvT[:, (l * 4 + i) * FC:(l * 4 + i + 1) * FC], conv_w[l, i])
            small_T(cvT[:, (l * 4 + 3) * FC:(l * 4 + 4) * FC], conv_b[l])
            small_T(pscT[:, l * 4:(l + 1) * 4], pool_scale[l])

            for (p0, col, src) in ((0, 2 * l, q_norm_g[l]), (64, 2 * l, q_norm_g[l]), (0, 2 * l + 1, k_norm_g[l]), (64, 2 * l + 1, k_norm_g[l])):
                def fq(e, p0=p0, col=col, src=src):
                    with nc.allow_non_contiguous_dma(reason="tiny"):
                        return e.dma_start(out=qkg[p0:p0 + 64, col:col + 1], in_=src.rearrange("(p o) -> p o", o=1))
                S.dma("sync", fq, b_const, writes=[b_const], partial=True)
            for j in range(4):
                for h in range(2):
                    def fs(e, l=l, j=j, h=h):
                        with nc.allow_non_contiguous_dma(reason="tiny"):
                            return e.dma_start(out=sinkT[h * 64:(h + 1) * 64, l * 4 + j:l * 4 + j + 1],
                                               in_=win_sink[l:l + 1, h * 4 + j:h * 4 + j + 1].broadcast_to([64, 1]))
                    S.dma("sync", fs, b_const, writes=[b_const], partial=True)
        S.op("scalar", f_act(sinkT[:], sinkT[:], AF.Exp), reads=[b_const], writes=[b_const])

        state["top"] = P_END
        if stop == "consts":
            S.barrier()
            sems = {k: es.enter_context(nc.semaphore(k)) for k in S.semkeys}
            S.run(sems)
            return nc
        s3 = alloc("s3", D, F32)
        sT = alloc("sT", KC * 128, F32)
        wm = [alloc("wm%d" % i, 8 * 512, F32) for i in range(2)]; b_wm = [Buf("wm0"), Buf("wm1")]
        bm3 = alloc("bm3", 6 * D, F32); b_bm3 = Buf("bm3")
        mrow = alloc("mrow", 6 * D, F32); b_mrow = Buf("mrow")
        b_s3 = Buf("s3"); b_sT = Buf("sT")
        S.op("gpsimd", f_memset(s3[:], 0.0), writes=[b_s3])
        S.op("gpsimd", f_memset(mrow[:], 0.0), writes=[b_mrow])
        S.dma("sync", f_dma(s3[0:3, :], c3_in), b_s3, writes=[b_s3])
        S.op("scalar", f_act(s3[0:3, :], s3[0:3, :], AF.Silu), reads=[b_s3], writes=[b_s3])
        for hh in range(2):
            S.op("tensor", f_transposes([(pbank(hh, 128, 128 * c4), s3[:, (hh * 4 + c4) * 128:(hh * 4 + c4 + 1) * 128], ident[:]) for c4 in range(4)]),
                 reads=[b_s3, b_const], writes=[bank[hh]])
            S.op("vector", f_copy(sT[:, hh * 512:(hh + 1) * 512], pbank(hh)), reads=[bank[hh]], writes=[b_sT], partial=True)
        for l in range(NL):
            S.dma("sync", f_dma(bm3[0:3, :], b_mod[l:l + 1, :].broadcast_to([3, 6 * D])), b_bm3, writes=[b_bm3])
            for pc in range(12):
                wb = wm[pc % 2]; bw = b_wm[pc % 2]
                S.dma("sync", f_dma(wb[:].rearrange("p (c n) -> p c n", c=KC),
                                    w_mod[l][:, pc * 512:(pc + 1) * 512].rearrange("(c p) n -> p c n", p=128)),
                      bw, writes=[bw])
                pb = 1 + pc % 2
                S.op("tensor", f_mms([(pbank(pb), sT[:, c * 128:(c + 1) * 128], wb[:, c * 512:(c + 1) * 512], c == 0, c == KC - 1)
                                      for c in range(KC)]), reads=[b_sT, bw], writes=[bank[pb]])
                S.op("vector", f_tt(mrow[0:3, pc * 512:(pc + 1) * 512], pbank(pb)[0:3, :], bm3[0:3, pc * 512:(pc + 1) * 512], ALU.add),
                     reads=[bank[pb], b_bm3], writes=[b_mrow], partial=True)
            S.dma("sync", f_dma(modd[l], mrow[0:3, :]), b_mrow, reads=[b_mrow])
            for k, mi in enumerate((0, 1, 3, 4)):
                for hh in range(2):
                    S.op("tensor", f_transposes([(pbank(3 + hh, 128, 128 * c4), mrow[:, mi * D + (hh * 4 + c4) * 128: mi * D + (hh * 4 + c4 + 1) * 128], ident[:])
                                                 for c4 in range(4)]), reads=[b_mrow, b_const], writes=[bank[3 + hh]])
                for v in range(3):
                    i0 = ((l * 3 + v) * 4 + k) * 8
                    for hh in range(2):
                        src = pbank(3 + hh).rearrange("p (c v) -> p c v", v=128)[:, :, v]
                        S.op("vector", f_copy(modT[:, i0 + hh * 4:i0 + hh * 4 + 4], src), reads=[bank[3 + hh]], writes=[b_modT], partial=True)
        for l in range(NL):
            for v in range(3):
                for k, goff in ((1, l * 8), (3, 16 + l * 8)):
                    i0 = ((l * 3 + v) * 4 + k) * 8
                    S.op("vector", f_stt(modT[:, i0:i0 + 8], modT[:, i0:i0 + 8], 1.0, gT[:, goff:goff + 8], ALU.add, ALU.mult),
                         reads=[b_modT, b_const], writes=[b_modT])
        S.barrier()

        def load_gate_tiles(dst_list):
            for (t, b, l, v, mi) in dst_list:
                S.dma("sync", f_dma(t[:], modd[l, v:v + 1, mi * D:(mi + 1) * D].broadcast_to([128, D])), b, writes=[b])

        def norm_block(src_rows, xslot, b_x, junk, b_junk, hT, b_hT, col0, l, v, k_shift, k_gmod, tb, si, mode="all"):
            if mode in ("all", "load"):
                S.dma("sync", f_dma(xslot, src_rows), b_x, writes=[b_x])
            if mode == "load":
                return
            _nb = "9"
            do_pre = mode in ("all", "compute", "pre")
            do_post = mode in ("all", "compute", "post")
            ms = stat[:, 2 * si:2 * si + 1]
            rs = stat[:, 2 * si + 1:2 * si + 2]
            if do_pre:
                S.op("scalar", f_act(junk, xslot, AF.Square, scale=1.0 / 32.0, accum=ms), reads=[b_x], writes=[b_junk, b_stat[si]])
                S.op("scalar", f_act(rs, ms, AF.Sqrt, bias=epst[:, 0:1], scale=1.0), reads=[b_stat[si], b_const], writes=[b_stat[si]])
                S.op("vector", f_recip(rs, rs), reads=[b_stat[si]], writes=[b_stat[si]])
                S.op("vector", f_ts(junk, xslot, rs, None, ALU.mult), reads=[b_x, b_stat[si]], writes=[b_junk])
            if not do_post:
                return
            if _nb == "2":
                return
            pt = pbank(tb).bitcast(BF16)
            S.op("tensor", f_transposes([(pt[:, c * 128:(c + 1) * 128], junk[:, c * 128:(c + 1) * 128], identb) for c in range(KC)]),
                 reads=[b_junk, b_const], writes=[bank[tb]])
            if _nb == "3":
                return
            for c in range(KC):
                o = hT[:, c * hT_w + col0: c * hT_w + col0 + 128]
                i = pt[:, c * 128:(c + 1) * 128]
                if False:
                    S.op("scalar", f_act(o, i, AF.Identity, bias=modT_ap(l, v, k_shift, c), scale=modT_ap(l, v, k_gmod, c)),
                         reads=[bank[tb], b_modT], writes=[b_hT], partial=True)
                else:
                    S.op("vector", f_ts(o, i, modT_ap(l, v, k_gmod, c), modT_ap(l, v, k_shift, c), ALU.mult, ALU.add),
                         reads=[bank[tb], b_modT], writes=[b_hT], partial=True)

        hT_w = 512

        def finalize():
            S.barrier()
            sems = {k: es.enter_context(nc.semaphore(k)) for k in S.semkeys}
            S.run(sems)
            return nc
        if stop == "p0":
            return finalize()
        for l in range(NL):
            last = (l == NL - 1)
            src_x = x_in if l == 0 else xsA
            src_c = ctx_in if l == 0 else csA

            state["top"] = P_END
            WIN = alloc("WIN", KC * 2048, BF16); b_WIN = Buf("WIN")
            WPL = alloc("WPL", 4 * 128, BF16); b_WPL = Buf("WPL")
            KTw = alloc("KTw", LK, BF16); b_KTw = Buf("KTw")
            KTg = alloc("KTg", LK, BF16); b_KTg = Buf("KTg")
            Vw = alloc("Vw", NKB * 192, BF16); b_Vw = Buf("Vw")
            Vg = alloc("Vg", NKB * 192, BF16); b_Vg = Buf("Vg")
            PW = 8 + L + 8
            pT = alloc("pT", 4 * PW, BF16); b_pT = Buf("pT")
            PWc = 8 + CTX + 8
            pTc = alloc("pTc", 4 * PWc, BF16); b_pTc = Buf("pTc")
            ropec_t = [alloc("ropec%d" % i, 512, F32) for i in range(2)]
            ropes_t = [alloc("ropes%d" % i, 512, F32) for i in range(2)]
            b_rope_t = [Buf("rope0"), Buf("rope1")]
            rope_state = {"i": 0}
            xs_ = [alloc("xslot%d" % i, D, F32) for i in range(2)]; b_xs = [Buf("xs0"), Buf("xs1")]
            junk = [alloc("junk%d" % i, D, BF16) for i in range(2)]; b_junk = [Buf("junk0"), Buf("junk1")]
            hT = alloc("hT", KC * 512, BF16); b_hT = Buf("hT")
            zb = [alloc("zb%d" % i, 512, BF16) for i in range(2)]; b_zb = [Buf("zb0"), Buf("zb1")]
            sq = [alloc("sq%d" % i, 512, BF16) for i in range(2)]; b_sq = [Buf("sq0"), Buf("sq1")]
            rsd = [alloc("rsd%d" % i, 512, F32) for i in range(2)]; b_rsd = [Buf("rsd0"), Buf("rsd1")]
            t1 = [alloc("t1_%d" % i, 512, F32) for i in range(2)]; b_t1 = [Buf("t1a"), Buf("t1b")]
            t2 = [alloc("t2_%d" % i, 512, F32) for i in range(2)]; b_t2 = [Buf("t2a"), Buf("t2b")]
            QA = alloc("QA", 8 * 512, BF16); b_QAc = [Buf("QA%d" % i) for i in range(8)]
            QB = alloc("QB", 8 * 512, BF16); b_QBc = [Buf("QB%d" % i) for i in range(8)]
            Pb = [alloc("Pb%d" % i, 1280, BF16) for i in range(2)]; b_Pb = [Buf("Pb0"), Buf("Pb1")]
            yst = alloc("yst", 12 * 512, BF16); b_yst = Buf("yst")
            pl = [alloc("pl%d" % i, 528, F32) for i in range(2)]; b_pl = [Buf("pl0"), Buf("pl1")]
            pld = alloc("pld", 4 * 512, BF16); b_pld = Buf("pld")
            rcp = alloc("rcp", 512, F32); b_rcp = Buf("rcp")
            Dsb = alloc("Dsb", 512, F32); b_Dsb = Buf("Dsb")
            S.op("gpsimd", f_memset(Dsb[:], 0.0), writes=[b_Dsb])

            def wcols(dst0, src0, n, l=l):
                S.dma("gpsimd", f_dma(WIN[:].rearrange("p (c n) -> p c n", c=KC)[:, :, dst0:dst0 + n],
                                      w_in[l][:, src0:src0 + n].rearrange("(c p) n -> p c n", p=128)),
                      b_WIN, writes=[b_WIN], partial=True)
            wcols(0, 0, 512)
            wcols(512, 1024, 128)
            wcols(640, 1792, 128)
            wcols(768, 1152, 128)
            wcols(896, 1920, 128)
            for a, base in ((0, 512), (1, 1280)):
                for j in range(4):
                    for h in range(2):
                        wcols(1024 + a * 512 + j * 128 + h * 64, base + (h * 4 + j) * 64, 64)
            S.dma("gpsimd", f_dma(WPL[:].rearrange("p (g d) -> p g d", g=4), w_pool[l].rearrange("g c d -> c g d")),
                  b_WPL, writes=[b_WPL])

            def load_rope(t0, n):
                i = rope_state["i"] = 1 - rope_state["i"]
                S.dma("sync", f_dma(ropec_t[i][:, 0:n], ropec_in[:, t0:t0 + n]), b_rope_t[i], writes=[b_rope_t[i]], partial=True)
                S.dma("sync", f_dma(ropes_t[i][:, 0:n], ropes_in[:, t0:t0 + n]), b_rope_t[i], writes=[b_rope_t[i]], partial=True)
            S.op("gpsimd", f_memset(Vw[:].rearrange("p (k w) -> p k w", w=192)[:, :, 65:128], 0.0), writes=[b_Vw], partial=True)
            S.op("gpsimd", f_memset(Vg[:].rearrange("p (k w) -> p k w", w=192)[:, :, 65:128], 0.0), writes=[b_Vg], partial=True)
            S.op("gpsimd", f_memset(Vw[:].rearrange("p (k w) -> p k w", w=192)[:, :, 64:65], 1.0), writes=[b_Vw], partial=True)
            S.op("gpsimd", f_memset(Vg[:].rearrange("p (k w) -> p k w", w=192)[:, :, 64:65], 1.0), writes=[b_Vg], partial=True)
            S.op("gpsimd", f_memset(pT[:].rearrange("p (g w) -> p g w", g=4)[:, :, 0:8], 0.0), writes=[b_pT], partial=True)
            S.op("gpsimd", f_memset(pT[:].rearrange("p (g w) -> p g w", g=4)[:, :, 8 + L:PW], 0.0), writes=[b_pT], partial=True)
            S.op("gpsimd", f_memset(pTc[:].rearrange("p (g w) -> p g w", g=4)[:, :, 0:8], 0.0), writes=[b_pTc], partial=True)
            S.op("gpsimd", f_memset(pTc[:].rearrange("p (g w) -> p g w", g=4)[:, :, 8 + CTX:PWc], 0.0), writes=[b_pTc], partial=True)
            S.op("gpsimd", f_memset(QA[64:128, :], 0.0), writes=b_QAc)
            S.op("gpsimd", f_memset(QB[0:64, :], 0.0), writes=b_QBc)

            WINc = lambda c, a, n: WIN[:, c * 2048 + a: c * 2048 + a + n]

            def rope_stages(zsrc, b_zsrc, n, tok0, outs, pbs, bi, normed, gcol, is_ctx=False):
                z16 = zb[bi]; bz = b_zb[bi]
                zf = t1[bi][:, 0:n]; bzf = b_t1[bi]

                def s1():
                    if normed:
                        S.op("scalar", f_act(sq[bi][:, 0:n], zsrc, AF.Square), reads=[b_zsrc], writes=[b_sq[bi]])
                    S.op("scalar", f_act(zf, zsrc, AF.Identity), reads=[b_zsrc], writes=[bzf])

                def s2():
                    if normed:
                        S.op("tensor", f_mms([(pbank(pbs[0], n), bdm, sq[bi][:, 0:n], True, True)]), reads=[b_sq[bi], b_const], writes=[bank[pbs[0]]])
                        S.op("scalar", f_act(rsd[bi][:, 0:n], pbank(pbs[0], n), AF.Sqrt, bias=epst[:, 0:1], scale=1.0),
                             reads=[bank[pbs[0]], b_const], writes=[b_rsd[bi]])
                        S.op("vector", f_recip(rsd[bi][:, 0:n], rsd[bi][:, 0:n]), reads=[b_rsd[bi]], writes=[b_rsd[bi]])
                        S.op("vector", f_stt(zf, zf, qkg[:, gcol:gcol + 1], rsd[bi][:, 0:n], ALU.mult, ALU.mult),
                             reads=[bzf, b_rsd[bi], b_const], writes=[bzf])
                    if is_ctx:
                        for (o, bo, p0, p1) in outs:
                            S.op("vector", f_copy(o[p0:p1], zf[p0:p1]), reads=[bzf], writes=[bo], partial=True)
                        return
                    S.op("scalar", f_act(z16[:, 0:n], zf, AF.Identity), reads=[bzf], writes=[bz])

                def s3():
                    if is_ctx:
                        return
                    S.op("tensor", f_mms([(pbank(pbs[1], n), pswap, z16[:, 0:n], True, True)]), reads=[bz, b_const], writes=[bank[pbs[1]]])
                    ri = rope_state["i"]
                    S.op("vector", f_tt(t2[bi][:, 0:n], pbank(pbs[1], n), ropes_t[ri][:, 0:n], ALU.mult),
                         reads=[bank[pbs[1]], b_rope_t[ri]], writes=[b_t2[bi]])
                    S.op("vector", f_tt(zf, zf, ropec_t[ri][:, 0:n], ALU.mult), reads=[bzf, b_rope_t[ri]], writes=[bzf])
                    for (o, bo, p0, p1) in outs:
                        S.op("vector", f_tt(o[p0:p1], zf[p0:p1], t2[bi][p0:p1, 0:n], ALU.add),
                             reads=[bzf, b_t2[bi]], writes=[bo], partial=True)
                return s1, s2, s3

            def rope_chunk(*args, **kw):
                s1, s2, s3 = rope_stages(*args, **kw)
                s1(); s2(); s3()

            def pool_tile(pbuf, b_pbuf, W, n, t0, Lseq, ycol0, mode="all", groups=(0, 1, 2, 3), pbk=None):
                o = 8 + t0
                first = (t0 == 0)
                lastt = (t0 + n == Lseq)
                for g, w in enumerate((2, 4, 8, 16)):
                    if g not in groups:
                        continue
                    base = g * W
                    dst = pld[:, g * 512: g * 512 + n]
                    if mode == "project":
                        pb = pbk if pbk is not None else 4 + g % 2
                        S.op("tensor", f_mms([(pbank(pb, n), WPL[:, g * 128:(g + 1) * 128], dst, True, True)]),
                             reads=[b_pld, b_WPL], writes=[bank[pb]])
                        S.op("scalar", f_act(yst[:, g * 512 + ycol0: g * 512 + ycol0 + n], pbank(pb, n), AF.Identity, scale=pscT[:, l * 4 + g: l * 4 + g + 1]),
                             reads=[bank[pb], b_const], writes=[b_yst], partial=True)
                        continue
                    wd = n + w - 2
                    a0 = base + o - w // 2
                    S.op("gpsimd", f_tt(pl[0][:, 0:wd], pbuf[:, a0:a0 + wd], pbuf[:, a0 + 1:a0 + 1 + wd], ALU.add),
                         reads=[b_pbuf], writes=[b_pl[0]])
                    cur = 0
                    sh = 2
                    while sh < w:
                        wd2 = wd - sh
                        S.op("gpsimd", f_tt(pl[1 - cur][:, 0:wd2], pl[cur][:, 0:wd2], pl[cur][:, sh:sh + wd2], ALU.add),
                             reads=[b_pl[cur]], writes=[b_pl[1 - cur]])
                        cur = 1 - cur
                        wd = wd2
                        sh *= 2
                    assert wd == n
                    dst = pld[:, g * 512: g * 512 + n]
                    S.op("gpsimd", f_ts(pl[cur][:, 0:n], pl[cur][:, 0:n], 1.0 / w, None, ALU.mult), reads=[b_pl[cur]], writes=[b_pl[cur]])
                    S.op("gpsimd", f_tt(dst, pl[cur][:, 0:n], pbuf[:, base + o: base + o + n], ALU.subtract),
                         reads=[b_pl[cur], b_pbuf], writes=[b_pld], partial=True)
                    if first:
                        S.op("gpsimd", f_tt(pl[cur][:, 0:8], pl[cur][:, 0:8], edge[:, g * 16: g * 16 + 8], ALU.mult),
                             reads=[b_pl[cur], b_const, b_pld], writes=[b_pl[cur]])
                        S.op("gpsimd", f_tt(dst[:, 0:8], pl[cur][:, 0:8], pbuf[:, base + o: base + o + 8], ALU.subtract),
                             reads=[b_pl[cur], b_pbuf], writes=[b_pld], partial=True)
                    if lastt:
                        S.op("gpsimd", f_tt(pl[cur][:, n - 8:n], pl[cur][:, n - 8:n], edge[:, g * 16 + 8: g * 16 + 16], ALU.mult),
                             reads=[b_pl[cur], b_const, b_pld], writes=[b_pl[cur]])
                        S.op("gpsimd", f_tt(dst[:, n - 8:n], pl[cur][:, n - 8:n], pbuf[:, base + o + n - 8: base + o + n], ALU.subtract),
                             reads=[b_pl[cur], b_pbuf], writes=[b_pld], partial=True)
                    if mode == "compute":
                        continue
                    pb = 4 + g % 2
                    S.op("tensor", f_mms([(pbank(pb, n), WPL[:, g * 128:(g + 1) * 128], dst, True, True)]),
                         reads=[b_pld, b_WPL], writes=[bank[pb]])
                    S.op("scalar", f_act(yst[:, g * 512 + ycol0: g * 512 + ycol0 + n], pbank(pb, n), AF.Identity, scale=pscT[:, l * 4 + g: l * 4 + g + 1]),
                         reads=[bank[pb], b_const], writes=[b_yst], partial=True)

            def attention(n_q, KT, b_KT, Vb, b_Vb, kbs, jq, sink_col, ydst, Sbanks_list, mask_fn=None, qcols=0, ob=6, db=7, dbc=5, hooks=None):
                nk = len(kbs)
                qa = QA[:, jq * 512 + qcols: jq * 512 + qcols + n_q]
                qb_ = QB[:, jq * 512 + qcols: jq * 512 + qcols + n_q]

                def stage_qk(i):
                    kb = kbs[i]
                    sbk = Sbanks_list[i % 2]
                    pbuf = Pb[i % 2]; bp = b_Pb[i % 2]
                    kcol = KT[:, kb * 128:(kb + 1) * 128]
                    S.op("tensor", f_mms([(pbank(sbk[0], n_q), kcol, qa, True, True), (pbank(sbk[1], n_q), kcol, qb_, True, True)]),
                         reads=[b_KT, b_QAc[jq], b_QBc[jq]], writes=[bank[sbk[0]], bank[sbk[1]]])
                    if n_q == 512:
                        S.op("scalar", f_act(pbuf[:, 0:1024], ps[:, sbk[0] * 512: sbk[0] * 512 + 1024], AF.Exp, scale=SM_SCALE),
                             reads=[bank[sbk[0]], bank[sbk[1]]], writes=[bp])
                    else:
                        S.op("scalar", f_act(pbuf[:, 0:n_q], pbank(sbk[0], n_q), AF.Exp, scale=SM_SCALE), reads=[bank[sbk[0]]], writes=[bp], partial=True)
                        S.op("scalar", f_act(pbuf[:, 512:512 + n_q], pbank(sbk[1], n_q), AF.Exp, scale=SM_SCALE), reads=[bank[sbk[1]]], writes=[bp], partial=True)

                def stage_pv(i):
                    kb = kbs[i]
                    pbuf = Pb[i % 2]; bp = b_Pb[i % 2]
                    pa = pbuf[:, 0:n_q]; pb2 = pbuf[:, 512:512 + n_q]
                    st = (i == 0); sp = (i == nk - 1)
                    S.op("tensor", f_mms([
                        (pbank(ob, n_q), Vb[:, kb * 192: kb * 192 + 128], pa, st, sp),
                        (pbank(db, n_q), Vb[:, kb * 192 + 64: kb * 192 + 192], pb2, st, sp)]),
                        reads=[bp, b_Vb, b_const], writes=[bank[ob], bank[db]], partial=(i > 0))

                stage_qk(0)
                for i in range(nk):
                    if i + 1 < nk:
                        stage_qk(i + 1)
                    stage_pv(i)
                    if hooks and i in hooks:
                        for fn in hooks[i]:
                            fn()
                finish_attn(n_q, sink_col, ydst, ob=ob, db=db, dbc=dbc)

            def finish_attn(n_q, sink_col, ydst, col0=0, ob=6, db=7, dbc=5):
                S.op("vector", f_copy(Dsb[64:65, 0:n_q], pbank(ob, n_q)[64:65]), reads=[bank[ob]], writes=[b_Dsb], partial=True)
                S.op("vector", f_copy(Dsb[0:1, 0:n_q], pbank(db, n_q)[0:1]), reads=[bank[db]], writes=[b_Dsb], partial=True)
                S.op("tensor", f_mms([(pbank(dbc, n_q), selT[:], Dsb[:, 0:n_q], True, True)]), reads=[b_Dsb, b_const], writes=[bank[dbc]])
                if sink_col is not None:
                    S.op("vector", f_ts(rcp[:, 0:n_q], pbank(dbc, n_q), sinkT[:, sink_col:sink_col + 1], None, ALU.add), reads=[bank[dbc], b_const], writes=[b_rcp])
                    S.op("vector", f_recip(rcp[:, 0:n_q], rcp[:, 0:n_q]), reads=[b_rcp], writes=[b_rcp])
                else:
                    S.op("vector", f_recip(rcp[:, 0:n_q], pbank(dbc, n_q)), reads=[bank[dbc]], writes=[b_rcp])
                S.op("vector", f_tt(ydst[0:64], pbank(ob, n_q)[0:64], rcp[0:64, 0:n_q], ALU.mult), reads=[bank[ob], b_rcp], writes=[b_yst], partial=True)
                S.op("vector", f_tt(ydst[64:128], pbank(db, n_q)[64:128], rcp[64:128, 0:n_q], ALU.mult), reads=[bank[db], b_rcp], writes=[b_yst], partial=True)

            def window_attention(jq, t0, sink_col, ydst):
                info = {}

                def stage1(qb):
                    gq = t0 // 128 + qb
                    kbs = []
                    if gq >= 1:
                        kbs.append((gq - 1, "L"))
                    kbs.append((gq, "C"))
                    if gq + 1 < NB:
                        kbs.append((gq + 1, "R"))
                    kbs.append((NB, "X"))
                    kbs.append((NB + 1, "X"))
                    info[qb] = kbs
                    npc = len(kbs)
                    sel = qb % 2
                    sb3 = (0, 1, 2) if sel == 0 else (2, 3, 4)
                    scol0 = 0 if sel == 0 else 1280
                    pbuf = Pb[sel]; bp = b_Pb[sel]
                    qa = QA[:, jq * 512 + qb * 128: jq * 512 + (qb + 1) * 128]
                    qb_ = QB[:, jq * 512 + qb * 128: jq * 512 + (qb + 1) * 128]
                    mm = []
                    for i, (kb, kind) in enumerate(kbs):
                        kcol = KTw[:, kb * 128:(kb + 1) * 128]
                        pa_ = ps[:, scol0 + (2 * i) * 128: scol0 + (2 * i + 1) * 128]
                        pb_ = ps[:, scol0 + (2 * i + 1) * 128: scol0 + (2 * i + 2) * 128]
                        if kind in ("L", "R"):
                            mk = maskLR[:, 0:128] if kind == "L" else maskLR[:, 256:384]
                            mm.append((pa_, kcol, qa, True, False))
                            mm.append((pa_, identb, mk, False, True))
                            mm.append((pb_, kcol, qb_, True, False))
                            mm.append((pb_, identb, mk, False, True))
                        else:
                            mm.append((pa_, kcol, qa, True, True))
                            mm.append((pb_, kcol, qb_, True, True))
                    bks = [bank[b] for b in sb3]
                    S.op("tensor", f_mms(mm), reads=[b_KTw, b_QAc[jq], b_QBc[jq], b_const], writes=bks)
                    S.op("scalar", f_act(pbuf[:, 0:npc * 256], ps[:, scol0: scol0 + npc * 256], AF.Exp, scale=SM_SCALE), reads=bks, writes=[bp])

                def stage2(qb):
                    kbs = info[qb]
                    npc = len(kbs)
                    sel = qb % 2
                    pbuf = Pb[sel]; bp = b_Pb[sel]
                    mm = []
                    oc = qb * 128
                    for i, (kb, kind) in enumerate(kbs):
                        st = (i == 0); sp = (i == npc - 1)
                        pa = pbuf[:, (2 * i) * 128:(2 * i + 1) * 128]
                        pb2 = pbuf[:, (2 * i + 1) * 128:(2 * i + 2) * 128]
                        mm.append((pbank(6, 128, oc), Vw[:, kb * 192: kb * 192 + 128], pa, st, sp))
                        mm.append((pbank(7, 128, oc), Vw[:, kb * 192 + 64: kb * 192 + 192], pb2, st, sp))
                    S.op("tensor", f_mms(mm), reads=[bp, b_Vw, b_const], writes=[bank[6], bank[7]], partial=(qb > 0))

                stage1(0)
                for qb in range(4):
                    if qb + 1 < 4:
                        stage1(qb + 1)
                    stage2(qb)
                finish_attn(512, sink_col, ydst, ob=6, db=7, dbc=5)

            SG = [(0, 1), (2, 3)]

            if stop == "A0":
                return finalize()
            for s in range(NSEQ):
                import os as _os
                _dbg = _os.environ.get("KDBG", "")
                if stop == "A0b":
                    return finalize()
                tiles = [("ctx", 0, CTX)] + [("lat", t * 512, 512) for t in range(NT)]
                blkc = 0
                for (kind, t0, n) in tiles:
                    is_ctx = (kind == "ctx")
                    v = 2 if is_ctx else s
                    src = src_c[s] if is_ctx else src_x[s]
                    nb = n // 128
                    if not is_ctx:
                        load_rope(t0, n)
                    for b in range(nb):
                        si = blkc % 2
                        norm_block(src[t0 + b * 128: t0 + (b + 1) * 128, :], xs_[si][:], b_xs[si], junk[si][:], b_junk[si],
                                   hT, b_hT, b * 128, l, v, 0, 1, si, si)
                        blkc += 1
                        if stop == "A1":
                            return finalize()
                    kbase = (L + t0) if is_ctx else t0
                    if not (is_ctx and last):
                        for g in range(4):
                            pb = 2 + g % 2
                            S.op("tensor", f_mms([(pbank(pb, n), WINc(c, g * 128, 128), hT[:, c * 512: c * 512 + n], c == 0, c == KC - 1) for c in range(KC)]),
                                 reads=[b_hT, b_WIN], writes=[bank[pb]])
                            if is_ctx:
                                dst = pTc[:, g * PWc + 8: g * PWc + 8 + n]; bd_ = b_pTc
                            else:
                                dst = pT[:, g * PW + 8 + t0: g * PW + 8 + t0 + n]; bd_ = b_pT
                            S.op("scalar", f_act(dst, pbank(pb, n), AF.Identity), reads=[bank[pb]], writes=[bd_], partial=True)
                    if stop == "A2":
                        return finalize()
                    for a, (wcol, KT, bKT, normed) in enumerate(((512, KTw, b_KTw, False), (640, KTg, b_KTg, True))):
                        pb = 2 + a
                        S.op("tensor", f_mms([(pbank(pb, n), WINc(c, wcol, 128), hT[:, c * 512: c * 512 + n], c == 0, c == KC - 1) for c in range(KC)]),
                             reads=[b_hT, b_WIN], writes=[bank[pb]])
                        rope_chunk(pbank(pb, n), bank[pb], n, t0, [(KT[:, kbase:kbase + n], bKT, 0, 128)], (4, 5), a, normed, 2 * l + 1, is_ctx=is_ctx)
                    if stop == "A3" or (stop == "A3b" and not is_ctx):
                        return finalize()
                    for b in range(nb):
                        pb = 6 + b % 2
                        S.op("tensor", f_mms([(pbank(pb, 256), hT[:, c * 512 + b * 128: c * 512 + (b + 1) * 128], WINc(c, 768, 256), c == 0, c == KC - 1) for c in range(KC)]),
                             reads=[b_hT, b_WIN], writes=[bank[pb]])
                        kblk = kbase // 128 + b
                        S.op("scalar", f_act(Vw[:, kblk * 192: kblk * 192 + 64], pbank(pb, 64, 0), AF.Identity), reads=[bank[pb]], writes=[b_Vw], partial=True)
                        S.op("scalar", f_act(Vw[:, kblk * 192 + 128: kblk * 192 + 192], pbank(pb, 64, 64), AF.Identity), reads=[bank[pb]], writes=[b_Vw], partial=True)
                        S.op("scalar", f_act(Vg[:, kblk * 192: kblk * 192 + 64], pbank(pb, 64, 128), AF.Identity), reads=[bank[pb]], writes=[b_Vg], partial=True)
                        S.op("scalar", f_act(Vg[:, kblk * 192 + 128: kblk * 192 + 192], pbank(pb, 64, 192), AF.Identity), reads=[bank[pb]], writes=[b_Vg], partial=True)
                        if stop is not None and stop.startswith("Vb") and not is_ctx and int(stop[2:]) == b:
                            return finalize()
                    if stop is not None and stop.startswith("At") and int(stop[2:]) * 512 == t0 + (0 if is_ctx else 512):
                        return finalize()

                if stop == "A":
                    return finalize()
                tilesB = ([] if last else [("ctx", 0, CTX)]) + [("lat", t * 512, 512) for t in range(NT)]

                def prepB(tile):
                    nonlocal blkc
                    (kind, t0, n) = tile
                    is_ctx = (kind == "ctx")
                    v = 2 if is_ctx else s
                    src = src_c[s] if is_ctx else src_x[s]
                    for b in range(n // 128):
                        si = blkc % 2
                        norm_block(src[t0 + b * 128: t0 + (b + 1) * 128, :], xs_[si][:], b_xs[si], junk[si][:], b_junk[si],
                                   hT, b_hT, b * 128, l, v, 0, 1, si, si)
                        blkc += 1

                def qchunk(tile, jq):
                    (kind_, t0_, n_) = tile
                    st = {}

                    def a():
                        S.op("tensor", f_mms([(pbank(7, n_), WINc(c, 1024 + jq * 128, 128), hT[:, c * 512: c * 512 + n_], c == 0, c == KC - 1) for c in range(KC)]),
                             reads=[b_hT, b_WIN], writes=[bank[7]])
                        outs = [(QA[:, jq * 512: jq * 512 + n_], b_QAc[jq], 0, 64), (QB[:, jq * 512: jq * 512 + n_], b_QBc[jq], 64, 128)]
                        st["s"] = rope_stages(pbank(7, n_), bank[7], n_, t0_, outs, (7, 7), 0 if jq < 4 else 1, jq >= 4, 2 * l, is_ctx=False)
                        st["s"][0]()
                    return [a, lambda: st["s"][1](), lambda: st["s"][2]()]

                prepB(tilesB[0])
                q0_done = False
                for ti, (kind, t0, n) in enumerate(tilesB):
                    is_ctx = (kind == "ctx")
                    nxt = tilesB[ti + 1] if ti + 1 < len(tilesB) else None
                    ycol = (L + t0) if is_ctx else t0
                    if is_ctx:
                        stg = {}
                        for step in range(8 + 2):
                            if step - 2 >= 0:
                                stg[step - 2][2]()
                            if 0 <= step - 1 < 8:
                                stg[step - 1][1]()
                            if step < 8:
                                jq = step
                                pb = 2 + jq % 2
                                S.op("tensor", f_mms([(pbank(pb, n), WINc(c, 1024 + jq * 128, 128), hT[:, c * 512: c * 512 + n], c == 0, c == KC - 1) for c in range(KC)]),
                                     reads=[b_hT, b_WIN], writes=[bank[pb]])
                                outs = [(QA[:, jq * 512: jq * 512 + n], b_QAc[jq], 0, 64), (QB[:, jq * 512: jq * 512 + n], b_QBc[jq], 64, 128)]
                                stg[jq] = rope_stages(pbank(pb, n), bank[pb], n, t0, outs, (4, 5), jq % 2, jq >= 4, 2 * l, is_ctx=True)
                                stg[jq][0]()
                        pool_tile(pTc, b_pTc, PWc, n, 0, CTX, 0)
                        for j in range(4):
                            attention(n, KTw, b_KTw, Vw, b_Vw, [NB, NB + 1], j, l * 4 + j, yst[:, (4 + j) * 512:(4 + j) * 512 + n], SG, ob=6, db=7, dbc=5)
                            attention(n, KTg, b_KTg, Vg, b_Vg, [NB, NB + 1], 4 + j, None, yst[:, (8 + j) * 512:(8 + j) * 512 + n], SG, ob=4, db=5, dbc=6)
                            if j == 0 and nxt is not None:
                                prepB(nxt)
                        S.dma("sync", f_dma(yb[s].rearrange("k p t -> p k t")[:, :, ycol:ycol + n],
                                            yst[:].rearrange("p (k t) -> p k t", k=12)[:, :, 0:n]), b_yst, reads=[b_yst])
                        continue

                    if not q0_done:
                        load_rope(t0, n)
                        for jq in (0, 4):
                            for fn in qchunk((kind, t0, n), jq):
                                fn()
                    q0_done = False
                    for j in range(4):
                        window_attention(j, t0, l * 4 + j, yst[:, (4 + j) * 512:(4 + j) * 512 + 512])
                        hooks = {}

                        def add(k, fn, hooks=hooks):
                            hooks.setdefault(min(NKB - 1, (k * NKB) // 34), []).append(fn)
                        if j < 3:
                            qw = qchunk((kind, t0, n), j + 1)
                            qg = qchunk((kind, t0, n), 4 + j + 1)
                            for k, fn in zip((1, 4, 7), qw):
                                add(k, fn)
                            for k, fn in zip((10, 14, 18), qg):
                                add(k, fn)
                            if j == 0:
                                add(20, lambda t0=t0, n=n: pool_tile(pT, b_pT, PW, n, t0, L, 0, mode="compute"))
                                for g in range(4):
                                    add(26 + 2 * g, lambda g=g, t0=t0, n=n: pool_tile(pT, b_pT, PW, n, t0, L, 0, mode="project", groups=(g,), pbk=7))
                        elif nxt is not None:
                            (kindn, t0n, nn) = nxt

                            def nb_fn(b, mode, t0n=t0n):
                                si = b % 2
                                norm_block(src_x[s][t0n + b * 128: t0n + (b + 1) * 128, :], xs_[si][:], b_xs[si], junk[si][:], b_junk[si],
                                           hT, b_hT, b * 128, l, s, 0, 1, 7, si, mode=mode)
                            for b in range(4):
                                add(2 * b, lambda b=b: nb_fn(b, "load"))
                                add(1 + 2 * b, lambda b=b: nb_fn(b, "pre"))
                                add(4 + 2 * b, lambda b=b: nb_fn(b, "post"))
                            add(12, lambda t0n=t0n, nn=nn: load_rope(t0n, nn))
                            qw = qchunk(nxt, 0)
                            qg = qchunk(nxt, 4)
                            for k, fn in zip((13, 16, 19), qw):
                                add(k, fn)
                            for k, fn in zip((22, 26, 30), qg):
                                add(k, fn)
                            q0_done = True
                        attention(512, KTg, b_KTg, Vg, b_Vg, list(range(NKB)), 4 + j, None, yst[:, (8 + j) * 512:(8 + j) * 512 + 512], SG, ob=4, db=5, dbc=6, hooks=hooks)
                    S.dma("sync", f_dma(yb[s].rearrange("k p t -> p k t")[:, :, ycol:ycol + n],
                                        yst[:].rearrange("p (k t) -> p k t", k=12)[:, :, 0:n]), b_yst, reads=[b_yst])
            S.barrier()
            if stop == "B":
                return finalize()

            state["top"] = P_END
            WBR = alloc("WBR", 3 * 4 * D, BF16); b_WBR = Buf("WBR")
            WGT = alloc("WGT", 3 * KC * D, BF16); b_WGT = Buf("WGT")
            WOU = alloc("WOU", KC * D, BF16); b_WOU = Buf("WOU")
            xq = [alloc("xq%d" % i, D, F32) for i in range(8)]; b_xq = [Buf("xq%d" % i) for i in range(8)]
            junkm = [alloc("junkm%d" % i, D, BF16) for i in range(2)]; b_junkm = [Buf("jm0"), Buf("jm1")]
            hTm = alloc("hTm", KC * 512, BF16); b_hTm = Buf("hTm")
            yT = [alloc("yT%d" % i, 12 * 512, BF16) for i in range(2)]; b_yT = [Buf("yT0"), Buf("yT1")]
            mT = alloc("mT", KC * 512, BF16); b_mT = Buf("mT")
            sg = [alloc("sg%d" % i, 512, F32) for i in range(2)]; b_sg = [Buf("sg0"), Buf("sg1")]
            macc = alloc("macc", 512, F32); b_macc = Buf("macc")
            mtmp = alloc("mtmp", 512, F32); b_mtmp = Buf("mtmp")
            gt1 = [alloc("gt1_%d" % i, D, F32) for i in range(2)]; b_gt1 = [Buf("gt1s"), Buf("gt1c")]
            otmp = [alloc("otmp%d" % i, 512, F32) for i in range(2)]; b_otmp = [Buf("ot0"), Buf("ot1")]

            S.dma("gpsimd", f_dma(WBR[:, 0:4 * D].rearrange("p (c n) -> p c n", c=4), w_branch[l, 0].rearrange("(c p) n -> p c n", p=128)),
                  b_WBR, writes=[b_WBR], partial=True)
            for i in (1, 2):
                for j in range(4):
                    for h in range(2):
                        r0 = (h * 4 + j) * 64
                        S.dma("gpsimd", f_dma(WBR[h * 64:(h + 1) * 64, (i * 4 + j) * D:(i * 4 + j + 1) * D], w_branch[l, i, r0:r0 + 64, :]),
                              b_WBR, writes=[b_WBR], partial=True)
            for i in range(3):
                S.dma("gpsimd", f_dma(WGT[:, i * KC * D:(i + 1) * KC * D].rearrange("p (c n) -> p c n", c=KC), w_gate[l, i].rearrange("(c p) n -> p c n", p=128)),
                      b_WGT, writes=[b_WGT], partial=True)
            S.dma("gpsimd", f_dma(WOU[:].rearrange("p (c n) -> p c n", c=KC), w_out[l].rearrange("(c p) n -> p c n", p=128)), b_WOU, writes=[b_WOU])

            dst_x = xsB
            dst_c = csB
            for s in range(NSEQ):
                load_gate_tiles([(gt1[0], b_gt1[0], l, s, 2)] + ([] if last else [(gt1[1], b_gt1[1], l, 2, 2)]))
                tilesM = ([] if last else [("ctx", 0, CTX)]) + [("lat", t * 512, 512) for t in range(NT)]

                def prepM_y(ti):
                    (kind, t0, n) = tilesM[ti]
                    ycol = (L + t0) if kind == "ctx" else t0
                    yt = yT[ti % 2]; byt = b_yT[ti % 2]
                    S.dma("sync", f_dma(yt[:].rearrange("p (k t) -> p k t", k=12)[:, :, 0:n], yb[s].rearrange("k p t -> p k t")[:, :, ycol:ycol + n]),
                          byt, writes=[byt])

                def prepM_blk(ti, b, mode="all"):
                    (kind, t0, n) = tilesM[ti]
                    is_ctx = (kind == "ctx")
                    v = 2 if is_ctx else s
                    src = src_c[s] if is_ctx else src_x[s]
                    q = (ti % 2) * 4 + b
                    norm_block(src[t0 + b * 128: t0 + (b + 1) * 128, :], xq[q][:], b_xq[q], junkm[b % 2][:], b_junkm[b % 2],
                               hTm, b_hTm, b * 128, l, v, 0, 1, b % 2, b % 2, mode=mode)

                prepM_y(0)
                for b in range(tilesM[0][2] // 128):
                    prepM_blk(0, b)
                for ti, (kind, t0, n) in enumerate(tilesM):
                    is_ctx = (kind == "ctx")
                    dstd = dst_c[s] if is_ctx else dst_x[s]
                    gtile = gt1[1] if is_ctx else gt1[0]
                    bgt = b_gt1[1] if is_ctx else b_gt1[0]
                    nb = n // 128
                    yt = yT[ti % 2]; byt = b_yT[ti % 2]
                    for oc in range(KC):
                        for i in range(3):
                            pg = 2 + i % 2
                            S.op("tensor", f_mms([(pbank(pg, n), WGT[:, (i * KC + c) * D + oc * 128:(i * KC + c) * D + (oc + 1) * 128], hTm[:, c * 512: c * 512 + n], c == 0, c == KC - 1)
                                                  for c in range(KC)]), reads=[b_hTm, b_WGT], writes=[bank[pg]])
                            S.op("scalar", f_act(sg[i % 2][:, 0:n], pbank(pg, n), AF.Sigmoid, bias=bgT[:, (l * 3 + i) * 8 + oc:(l * 3 + i) * 8 + oc + 1], scale=1.0),
                                 reads=[bank[pg], b_const], writes=[b_sg[i % 2]])
                            pbx = 4 + i % 2
                            S.op("tensor", f_mms([(pbank(pbx, n), WBR[:, (i * 4 + c) * D + oc * 128:(i * 4 + c) * D + (oc + 1) * 128], yt[:, (i * 4 + c) * 512:(i * 4 + c) * 512 + n], c == 0, c == 3)
                                                  for c in range(4)]), reads=[byt, b_WBR], writes=[bank[pbx]])
                            if i == 0:
                                S.op("vector", f_tt(macc[:, 0:n], pbank(pbx, n), sg[i % 2][:, 0:n], ALU.mult), reads=[bank[pbx], b_sg[i % 2]], writes=[b_macc])
                            elif i == 1:
                                S.op("vector", f_tt(mtmp[:, 0:n], pbank(pbx, n), sg[i % 2][:, 0:n], ALU.mult), reads=[bank[pbx], b_sg[i % 2]], writes=[b_mtmp])
                                S.op("gpsimd", f_tt(macc[:, 0:n], macc[:, 0:n], mtmp[:, 0:n], ALU.add), reads=[b_macc, b_mtmp], writes=[b_macc])
                            else:
                                S.op("vector", f_tt(mtmp[:, 0:n], pbank(pbx, n), sg[i % 2][:, 0:n], ALU.mult), reads=[bank[pbx], b_sg[i % 2]], writes=[b_mtmp])
                                S.op("gpsimd", f_tt(mT[:, oc * 512: oc * 512 + n], macc[:, 0:n], mtmp[:, 0:n], ALU.add), reads=[b_macc, b_mtmp], writes=[b_mT], partial=True)
                    has_next = ti + 1 < len(tilesM)
                    pending = list(range(tilesM[ti + 1][2] // 128)) if has_next else []
                    if has_next:
                        prepM_y(ti + 1)
                        for b2 in pending:
                            prepM_blk(ti + 1, b2, mode="load")
                    for b in range(nb):
                        q = (ti % 2) * 4 + b
                        for hf in range(2):
                            po = 6 + hf
                            S.op("tensor", f_mms([(pbank(po), mT[:, c * 512 + b * 128: c * 512 + (b + 1) * 128], WOU[:, c * D + hf * 512: c * D + (hf + 1) * 512], c == 0, c == KC - 1)
                                                  for c in range(KC)]), reads=[b_mT, b_WOU], writes=[bank[po]])
                            S.op("vector", f_tt(otmp[hf][:], pbank(po), gtile[:, hf * 512:(hf + 1) * 512], ALU.mult), reads=[bank[po], bgt], writes=[b_otmp[hf]])
                            S.op("vector", f_tt(xq[q][:, hf * 512:(hf + 1) * 512], xq[q][:, hf * 512:(hf + 1) * 512], otmp[hf][:], ALU.add),
                                 reads=[b_xq[q], b_otmp[hf]], writes=[b_xq[q]])
                        S.dma("sync", f_dma(dstd[t0 + b * 128: t0 + (b + 1) * 128, :], xq[q][:]), b_xq[q], reads=[b_xq[q]])
                        if pending:
                            prepM_blk(ti + 1, pending.pop(0), mode="compute")
                    while pending:
                        prepM_blk(ti + 1, pending.pop(0), mode="compute")
            S.barrier()
            if stop == "M":
                return finalize()

            state["top"] = P_END
            WG = alloc("WG", KC * DFF, BF16); b_WG = Buf("WG")
            WV = alloc("WV", KC * DFF, BF16); b_WV = Buf("WV")
            WD = alloc("WD", FC * D, BF16); b_WD = Buf("WD")
            xc = [alloc("xc%d" % i, D, F32) for i in range(4)]; b_xc = [Buf("xc%d" % i) for i in range(4)]
            xh = alloc("xh", D, F32); b_xh = Buf("xh")
            junkc = [alloc("junkc%d" % i, D, BF16) for i in range(2)]; b_junkc = [Buf("jc0"), Buf("jc1")]
            hTc = alloc("hTc", KC * 512, BF16); b_hTc = Buf("hTc")
            hTh = alloc("hTh", KC * 2, BF16); b_hTh = Buf("hTh")
            ghs = alloc("ghs", 2 * FC, F32); b_ghs = Buf("ghs")
            junkh = alloc("junkh", D, BF16); b_junkh = Buf("junkh")
            S.op("gpsimd", f_memset(junkh[:], 0.0), writes=[b_junkh])
            uT = alloc("uT", FC * 512, BF16); b_uT = Buf("uT")
            av = [alloc("av%d" % i, 512, F32) for i in range(2)]; b_av = [Buf("av0"), Buf("av1")]
            sa = [alloc("sa%d" % i, 512, BF16) for i in range(2)]; b_sa = [Buf("sa0"), Buf("sa1")]
            gt2 = [alloc("gt2_%d" % i, D, F32) for i in range(2)]; b_gt2 = [Buf("gt2s"), Buf("gt2c")]
            fgt = gt2[1]; b_fgt = b_gt2[1]
            oc2 = av; b_oc2 = b_av

            S.dma("gpsimd", f_dma(WG[:].rearrange("p (c n) -> p c n", c=KC), w_ffg[l].rearrange("(c p) n -> p c n", p=128)), b_WG, writes=[b_WG])
            S.dma("gpsimd", f_dma(WV[:].rearrange("p (c n) -> p c n", c=KC), w_ffv[l].rearrange("(c p) n -> p c n", p=128)), b_WV, writes=[b_WV])
            S.dma("gpsimd", f_dma(WD[:].rearrange("p (c n) -> p c n", c=FC), w_ffd[l].rearrange("(c p) n -> p c n", p=128)), b_WD, writes=[b_WD])
            if last:
                S.dma("sync", f_dma(fgt[:], final_g.rearrange("(o n) -> o n", o=1).broadcast_to([128, D])), b_fgt, writes=[b_fgt])

            def cv(k, f):
                i = (l * 4 + k) * FC + f
                return cvT[:, i:i + 1]

            srcC_x = xsB
            srcC_c = csB
            dstC_x = y_out if last else xsA
            dstC_c = csA
            for s in range(NSEQ):
                load_gate_tiles([(gt2[0], b_gt2[0], l, s, 5)] + ([] if last else [(gt2[1], b_gt2[1], l, 2, 5)]))
                tilesC = ([] if last else [("ctx", 0, CTX)]) + [("lat", t * 512, 512) for t in range(NT)]

                def tinfo(ti):
                    (kind, t0, n) = tilesC[ti]
                    is_ctx = (kind == "ctx")
                    Lseq = CTX if is_ctx else L
                    src = srcC_c[s] if is_ctx else srcC_x[s]
                    return is_ctx, t0, n, (2 if is_ctx else s), src, (t0 > 0), (t0 + n < Lseq)

                def prepC_blk(ti, b, mode="all"):
                    is_ctx, t0, n, v, src, has_l, has_r = tinfo(ti)
                    norm_block(src[t0 + b * 128: t0 + (b + 1) * 128, :], xc[b][:], b_xc[b], junkc[b % 2][:], b_junkc[b % 2],
                               hTc, b_hTc, b * 128, l, v, 2, 3, b % 2, b % 2, mode=mode)

                def prepC_halo(ti):
                    is_ctx, t0, n, v, src, has_l, has_r = tinfo(ti)
                    if not (has_l or has_r):
                        return
                    if has_l:
                        S.dma("sync", f_dma(xh[0:1, :], src[t0 - 1:t0, :]), b_xh, writes=[b_xh], partial=True)
                    if has_r:
                        S.dma("sync", f_dma(xh[1:2, :], src[t0 + n:t0 + n + 1, :]), b_xh, writes=[b_xh], partial=True)
                    if not (has_l and has_r):
                        if has_l:
                            S.dma("sync", f_dma(xh[1:2, :], src[t0 - 1:t0, :]), b_xh, writes=[b_xh], partial=True)
                        else:
                            S.dma("sync", f_dma(xh[0:1, :], src[t0 + n:t0 + n + 1, :]), b_xh, writes=[b_xh], partial=True)
                    msh = stat[0:2, 8:9]; rsh = stat[0:2, 9:10]
                    S.op("scalar", f_act(junkh[0:2, :], xh[0:2, :], AF.Square, scale=1.0 / 32.0, accum=msh), reads=[b_xh], writes=[b_junkh, b_stat[4]])
                    S.op("scalar", f_act(rsh, msh, AF.Sqrt, bias=epst[0:2, 0:1], scale=1.0), reads=[b_stat[4], b_const], writes=[b_stat[4]])
                    S.op("vector", f_recip(rsh, rsh), reads=[b_stat[4]], writes=[b_stat[4]])
                    S.op("vector", f_ts(junkh[0:2, :], xh[0:2, :], rsh, None, ALU.mult), reads=[b_xh, b_stat[4]], writes=[b_junkh])
                    pt = pbank(0).bitcast(BF16)
                    S.op("tensor", f_transposes([(pt[:, c * 128:(c + 1) * 128], junkh[:, c * 128:(c + 1) * 128], identb) for c in range(KC)]),
                         reads=[b_junkh, b_const], writes=[bank[0]])
                    for c in range(KC):
                        S.op("vector", f_ts(hTh[:, 2 * c:2 * c + 2], pt[:, c * 128:c * 128 + 2], modT_ap(l, v, 3, c), modT_ap(l, v, 2, c), ALU.mult, ALU.add),
                             reads=[bank[0], b_modT], writes=[b_hTh], partial=True)

                def mainC1(ti):
                    is_ctx, t0, n, v, src, has_l, has_r = tinfo(ti)
                    if has_l or has_r:
                        for f in range(FC):
                            S.op("tensor", f_mms([(pbank(5, 2, 2 * f), WG[:, c * DFF + f * 128: c * DFF + (f + 1) * 128], hTh[:, 2 * c:2 * c + 2], c == 0, c == KC - 1) for c in range(KC)]),
                                 reads=[b_hTh, b_WG], writes=[bank[5]], partial=(f > 0))
                        S.op("vector", f_copy(ghs[:], pbank(5, 2 * FC)), reads=[bank[5]], writes=[b_ghs])
                    for f in range(FC):
                        pg = 1 + f % 2
                        pv = 3 + f % 2
                        S.op("tensor", f_mms([(pbank(pg, n), WG[:, c * DFF + f * 128: c * DFF + (f + 1) * 128], hTc[:, c * 512: c * 512 + n], c == 0, c == KC - 1) for c in range(KC)]),
                             reads=[b_hTc, b_WG], writes=[bank[pg]])
                        S.op("tensor", f_mms([(pbank(pv, n), WV[:, c * DFF + f * 128: c * DFF + (f + 1) * 128], hTc[:, c * 512: c * 512 + n], c == 0, c == KC - 1) for c in range(KC)]),
                             reads=[b_hTc, b_WV], writes=[bank[pv]])
                        a_ = av[f % 2]; ba = b_av[f % 2]
                        S.op("vector", f_ts(a_[:, 0:n], pbank(pg, n), cv(1, f), cv(3, f), ALU.mult, ALU.add), reads=[bank[pg], b_const], writes=[ba])
                        S.op("vector", f_stt(a_[:, 1:n], pbank(pg, n - 1), cv(0, f), a_[:, 1:n], ALU.mult, ALU.add), reads=[bank[pg], ba, b_const], writes=[ba])
                        S.op("vector", f_stt(a_[:, 0:n - 1], pbank(pg, n - 1, 1), cv(2, f), a_[:, 0:n - 1], ALU.mult, ALU.add), reads=[bank[pg], ba, b_const], writes=[ba])
                        if has_l:
                            S.op("vector", f_stt(a_[:, 0:1], ghs[:, 2 * f:2 * f + 1], cv(0, f), a_[:, 0:1], ALU.mult, ALU.add), reads=[b_ghs, ba, b_const], writes=[ba])
                        if has_r:
                            S.op("vector", f_stt(a_[:, n - 1:n], ghs[:, 2 * f + 1:2 * f + 2], cv(2, f), a_[:, n - 1:n], ALU.mult, ALU.add), reads=[b_ghs, ba, b_const], writes=[ba])
                        s_ = sa[f % 2]; bs = b_sa[f % 2]
                        S.op("scalar", f_act(s_[:, 0:n], a_[:, 0:n], AF.Silu), reads=[ba], writes=[bs])
                        S.op("vector", f_tt(uT[:, f * 512: f * 512 + n], pbank(pv, n), s_[:, 0:n], ALU.mult), reads=[bank[pv], bs], writes=[b_uT], partial=True)

                def mainC2_blk(ti, b):
                    is_ctx, t0, n, v, src, has_l, has_r = tinfo(ti)
                    dstd = dstC_c[s] if is_ctx else dstC_x[s]
                    gtile = gt2[1] if is_ctx else gt2[0]
                    bgt = b_gt2[1] if is_ctx else b_gt2[0]
                    for hf in range(2):
                        po = 6 + hf
                        S.op("tensor", f_mms([(pbank(po), uT[:, f * 512 + b * 128: f * 512 + (b + 1) * 128], WD[:, f * D + hf * 512: f * D + (hf + 1) * 512], f == 0, f == FC - 1)
                                              for f in range(FC)]), reads=[b_uT, b_WD], writes=[bank[po]])
                        S.op("vector", f_tt(oc2[hf][:], pbank(po), gtile[:, hf * 512:(hf + 1) * 512], ALU.mult), reads=[bank[po], bgt], writes=[b_oc2[hf]])
                        S.op("gpsimd", f_tt(xc[b][:, hf * 512:(hf + 1) * 512], xc[b][:, hf * 512:(hf + 1) * 512], oc2[hf][:], ALU.add),
                             reads=[b_xc[b], b_oc2[hf]], writes=[b_xc[b]])
                    if last and not is_ctx:
                        msf = stat[:, 12:13]; rsf = stat[:, 13:14]
                        S.op("scalar", f_act(junkc[b % 2][:], xc[b][:], AF.Square, scale=1.0 / 32.0, accum=msf), reads=[b_xc[b]], writes=[b_junkc[b % 2], b_stat[6]])
                        S.op("scalar", f_act(rsf, msf, AF.Sqrt, bias=epst[:, 0:1], scale=1.0), reads=[b_stat[6], b_const], writes=[b_stat[6]])
                        S.op("vector", f_recip(rsf, rsf), reads=[b_stat[6]], writes=[b_stat[6]])
                        S.op("vector", f_stt(xc[b][:], xc[b][:], rsf, fgt[:], ALU.mult, ALU.mult), reads=[b_xc[b], b_stat[6], b_fgt], writes=[b_xc[b]])
                    S.dma("sync", f_dma(dstd[t0 + b * 128: t0 + (b + 1) * 128, :], xc[b][:]), b_xc[b], reads=[b_xc[b]])

                for b in range(tilesC[0][2] // 128):
                    prepC_blk(0, b)
                prepC_halo(0)
                for ti in range(len(tilesC)):
                    nb = tilesC[ti][2] // 128
                    mainC1(ti)
                    has_next = ti + 1 < len(tilesC)
                    nbn = tilesC[ti + 1][2] // 128 if has_next else 0
                    pending = list(range(nbn))
                    loaded = set()
                    for b in range(nb):
                        mainC2_blk(ti, b)
                        if b < nbn:
                            prepC_blk(ti + 1, b, mode="load")
                            loaded.add(b)
                        if b >= 2 and pending and pending[0] in loaded:
                            prepC_blk(ti + 1, pending.pop(0), mode="compute")
                    while pending:
                        b2 = pending.pop(0)
                        if b2 not in loaded:
                            prepC_blk(ti + 1, b2, mode="load")
                            loaded.add(b2)
                        prepC_blk(ti + 1, b2, mode="compute")
                    if has_next:
                        prepC_halo(ti + 1)
            S.barrier()

        S.barrier()
        sems = {k: es.enter_context(nc.semaphore(k)) for k in S.semkeys}
        S.run(sems)
    return nc


_WEIGHT_NAMES = ["w_mod", "b_mod", "norm1_g", "norm2_g", "w_in", "w_pool_grp", "pool_scale", "win_sink",
                 "q_norm_g", "k_norm_g", "w_branch", "w_gate", "b_gate", "w_out", "w_ff_gate", "w_ff_val",
                 "conv_w", "conv_b", "w_ff_down", "final_g"]

_NC_CACHE = {}


def make_in_maps(inputs, L, n_cores):
    consts = host_consts(L)
    x = np.ascontiguousarray(np.asarray(inputs["x"], dtype=np.float32))
    c = np.asarray(inputs["c"], dtype=np.float32)
    ctx = np.ascontiguousarray(np.asarray(inputs["ctx"], dtype=np.float32))
    c_ctx = np.asarray(inputs["c_ctx"], dtype=np.float32)
    shared = {k: np.ascontiguousarray(np.asarray(inputs[k], dtype=np.float32)) for k in _WEIGHT_NAMES}
    shared.update(consts)
    maps = []
    for i in range(n_cores):
        m = dict(shared)
        m["x"] = x[NSEQ * i: NSEQ * (i + 1)]
        m["ctx"] = ctx[NSEQ * i: NSEQ * (i + 1)]
        m["c3"] = np.ascontiguousarray(np.concatenate([c[NSEQ * i: NSEQ * (i + 1)], c_ctx[None, :]], axis=0))
        maps.append(m)
    return maps


def kernel(**inputs):
    x = inputs["x"]
    B, L, _ = x.shape
    n_cores = B // NSEQ
    if L not in _NC_CACHE:
        _NC_CACHE[L] = build_nc(L)
    nc = _NC_CACHE[L]
    in_maps = make_in_maps(inputs, L, n_cores)
    res = run_bass_kernel_spmd(nc, in_maps, core_ids=list(range(n_cores)))
    out = np.concatenate([np.asarray(r["y"]) for r in res.results], axis=0)
    return out.astype(np.float32)
```

```python
import contextlib
import numpy as np
import concourse.bass as bass
import concourse.mybir as mybir
from concourse.bass_utils import run_bass_kernel_spmd

F32 = mybir.dt.float32
BF16 = mybir.dt.bfloat16
AF = mybir.ActivationFunctionType
ALU = mybir.AluOpType

D = 1024
KC = 8
CTX = 256
HD = 64
DFF = 2816
FC = 22
NL = 2
NSEQ = 2
GRID_W = 64
EPS = 1e-6
SM_SCALE = HD ** -0.5
N_CORES = 8


class Buf:
    __slots__ = ("name", "writers", "readers", "dsem", "dcount", "excl", "full")

    def __init__(self, name, excl=False):
        self.name = name
        self.writers = []
        self.readers = []
        self.dsem = None
        self.dcount = 0
        self.excl = excl
        self.full = []


class Eng:
    def __init__(self, name):
        self.name = name
        self.ops = []
        self.count = 0
        self.seen = {}


class Sched:
    def __init__(self, nc):
        self.nc = nc
        self.eng = {n: Eng(n) for n in ("tensor", "vector", "scalar", "gpsimd", "sync")}
        self.semkeys = ["e_" + n for n in self.eng]
        self.dma_latest = {}
        self.n_dsem = 0
        self.dsem_pool = {}

    def _deps(self, reads, writes, partial, own=None):
        ev = []
        for b in reads:
            ev.extend(b.writers)
            if b.excl:
                ev.extend(r for r in b.readers if r[0] != own)
        for b in writes:
            ev.extend(b.readers)
            if not (partial and not b.readers):
                ev.extend(b.writers)
            else:
                ev.extend(b.full)
        return ev

    def _commit(self, reads, writes, partial, event):
        for b in writes:
            if partial and not b.readers:
                b.writers.append(event)
            else:
                b.writers = [event]
                b.readers = []
                b.full = [] if partial else [event]
        for b in reads:
            b.readers.append(event)
            if len(b.readers) > 64:
                best = {}
                for (k, v) in b.readers:
                    if best.get(k, 0) < v:
                        best[k] = v
                b.readers = list(best.items())

    def _emit_waits(self, e, events, skip_own):
        need = {}
        for (k, v) in events:
            if k[0] == "d":
                v = self.dma_latest[k]
            if skip_own and k == "e_" + e.name:
                continue
            if need.get(k, 0) < v:
                need[k] = v
        for k, v in need.items():
            if e.seen.get(k, 0) >= v:
                continue
            e.seen[k] = v
            e.ops.append(("wait", k, v))

    def op(self, engine, fn, reads=(), writes=(), partial=False):
        e = self.eng[engine]
        ev = self._deps(reads, writes, partial, "e_" + engine)
        self._emit_waits(e, ev, engine == "tensor")
        e.count += 1
        event = ("e_" + engine, e.count)
        e.ops.append(("op", fn, "e_" + engine))
        self._commit(reads, writes, partial, event)
        return event

    def dma(self, queue, fn, sb, reads=(), writes=(), partial=False):
        e = self.eng[queue]
        if sb.dsem is None:
            sb.dsem = {}
            sb.dcount = {}
        if queue not in sb.dsem:
            key = "d_%d" % self.n_dsem
            self.n_dsem += 1
            sb.dsem[queue] = key
            sb.dcount[queue] = 0
            self.semkeys.append(key)
            self.dma_latest[key] = 0
        key = sb.dsem[queue]
        ev = self._deps(reads, writes, partial)
        self._emit_waits(e, ev, False)
        sb.dcount[queue] += 16
        self.dma_latest[key] = sb.dcount[queue]
        event = (key, sb.dcount[queue])
        e.ops.append(("dma", fn, key))
        self._commit(reads, writes, partial, event)
        return event

    def barrier(self):
        targets = {}
        for n, e in self.eng.items():
            if e.count:
                targets["e_" + n] = e.count
        for k, v in self.dma_latest.items():
            if v:
                targets[k] = v
        for n, e in self.eng.items():
            for k, v in targets.items():
                if e.seen.get(k, 0) >= v:
                    continue
                if k == "e_" + n and n == "tensor":
                    pass
                e.seen[k] = v
                e.ops.append(("wait", k, v))

    def run(self, sems):
        nc = self.nc

        def replay(ename):
            def body(h):
                for item in self.eng[ename].ops:
                    if item[0] == "wait":
                        h.wait_ge(sems[item[1]], item[2])
                    elif item[0] == "op":
                        item[1](h).then_inc(sems[item[2]], 1)
                    else:
                        item[1](h).then_inc(sems[item[2]], 16)
            return body

        with nc.Block() as block:
            block.sync(replay("sync"))
            block.tensor(replay("tensor"))
            block.vector(replay("vector"))
            block.scalar(replay("scalar"))
            block.gpsimd(replay("gpsimd"))


def f_act(out, in_, func, bias=None, scale=None, accum=None):
    def fn(e):
        kw = {}
        if bias is not None:
            kw["bias"] = bias
        if scale is not None:
            kw["scale"] = scale
        if accum is not None:
            kw["accum_out"] = accum
        return e.activation(out=out, in_=in_, func=func, **kw)
    return fn


def f_tt(out, a, b, op):
    return lambda e: e.tensor_tensor(out=out, in0=a, in1=b, op=op)


def f_ts(out, a, s1, s2, op0, op1=None):
    if op1 is None:
        return lambda e: e.tensor_scalar(out=out, in0=a, scalar1=s1, scalar2=None, op0=op0)
    return lambda e: e.tensor_scalar(out=out, in0=a, scalar1=s1, scalar2=s2, op0=op0, op1=op1)


def f_stt(out, in0, scalar, in1, op0, op1):
    return lambda e: e.scalar_tensor_tensor(out=out, in0=in0, scalar=scalar, in1=in1, op0=op0, op1=op1)


def f_copy(out, in_):
    return lambda e: e.tensor_copy(out=out, in_=in_)


def f_recip(out, in_):
    return lambda e: e.reciprocal(out=out, in_=in_)


def f_memset(ap, v):
    return lambda e: e.memset(ap, v)


def f_dma(out, in_):
    return lambda e: e.dma_start(out=out, in_=in_)


def f_mms(lst):
    def fn(e):
        ins = None
        for (o, l, r, st, sp) in lst:
            ins = e.matmul(o, lhsT=l, rhs=r, start=st, stop=sp)
        return ins
    return fn


def f_transposes(lst):
    def fn(e):
        ins = None
        for (o, i, idn) in lst:
            ins = e.transpose(out=o, in_=i, identity=idn)
        return ins
    return fn


def host_consts(L):
    rows = L // GRID_W
    t = np.arange(L)
    row = (t // GRID_W).astype(np.float32)
    col = (t % GRID_W).astype(np.float32)
    half = HD // 2
    inv = (np.float32(10000.0) ** (-np.arange(0, half, 2, dtype=np.float32) / np.float32(half))).astype(np.float32)
    cosT = np.zeros((128, L), np.float32)
    sinT = np.zeros((128, L), np.float32)
    for p in range(128):
        d = p % 64
        axis = d // 32
        hf = (d % 32) // 16
        f = d % 16
        pos = row if axis == 0 else col
        ang = (pos * inv[f]).astype(np.float32)
        cosT[p] = np.cos(ang).astype(np.float32)
        s = np.sin(ang).astype(np.float32)
        sinT[p] = -s if hf == 0 else s
    ident = np.eye(128, dtype=np.float32)
    pswap = np.zeros((128, 128), np.float32)
    for p in range(128):
        q = p + 16 if (p % 32) < 16 else p - 16
        pswap[q, p] = 1.0
    bd = np.zeros((128, 128), np.float32)
    bd[0:64, 0:64] = 1.0 / 64
    bd[64:128, 64:128] = 1.0 / 64
    onesA = np.zeros((128, 128), np.float32); onesA[:, 0:64] = 1.0
    onesB = np.zeros((128, 128), np.float32); onesB[:, 64:128] = 1.0
    kk = np.arange(128)[:, None]
    qq = np.arange(128)[None, :]
    mge = (kk >= qq).astype(np.float32)
    mle = (kk <= qq).astype(np.float32)
    NEGM = np.float32(-30000.0)
    masks = np.concatenate([(1 - mge) * NEGM, (1 - mge) * NEGM, (1 - mle) * NEGM, (1 - mle) * NEGM], axis=1)
    edge = np.zeros((4, 16), np.float32)
    for g, w in enumerate((2, 4, 8, 16)):
        for i in range(8):
            edge[g, i] = float(w) / min(w, i + w // 2)
            edge[g, 8 + i] = float(w) / min(w, w // 2 + 8 - i)
    edge = np.broadcast_to(edge.reshape(1, 64), (128, 64)).copy()
    cbf = np.concatenate([pswap, bd, onesA, onesB, masks], axis=1)
    sel = np.zeros((128, 128), np.float32)
    sel[64, 0:64] = 1.0
    sel[0, 64:128] = 1.0
    return {"ropec": cosT, "ropes": sinT, "ident": ident, "cbf": cbf, "edge": edge, "sel": sel}


class _Stop(Exception):
    pass


def build_nc(L, debug=False, stop=None):
    assert L % 512 == 0
    NB = L // 128
    NT = L // 512
    LK = L + CTX
    NKB = NB + 2
    nc = bass.Bass("TRN2", target_bir_lowering=False)

    def din(name, shape, dt=F32):
        return nc.dram_tensor(name, list(shape), dt, kind="ExternalInput").ap()

    x_in = din("x", [NSEQ, L, D])
    ctx_in = din("ctx", [NSEQ, CTX, D])
    c3_in = din("c3", [3, D])
    w_mod = din("w_mod", [NL, D, 6 * D])
    b_mod = din("b_mod", [NL, 6 * D])
    norm1_g = din("norm1_g", [NL, D])
    norm2_g = din("norm2_g", [NL, D])
    w_in = din("w_in", [NL, D, 2048])
    w_pool = din("w_pool_grp", [NL, 4, 128, 128])
    pool_scale = din("pool_scale", [NL, 512])
    win_sink = din("win_sink", [NL, 8])
    q_norm_g = din("q_norm_g", [NL, HD])
    k_norm_g = din("k_norm_g", [NL, HD])
    w_branch = din("w_branch", [NL, 3, 512, D])
    w_gate = din("w_gate", [NL, 3, D, D])
    b_gate = din("b_gate", [NL, 3, D])
    w_out = din("w_out", [NL, D, D])
    w_ffg = din("w_ff_gate", [NL, D, DFF])
    w_ffv = din("w_ff_val", [NL, D, DFF])
    conv_w = din("conv_w", [NL, 3, DFF])
    conv_b = din("conv_b", [NL, DFF])
    w_ffd = din("w_ff_down", [NL, DFF, D])
    final_g = din("final_g", [D])
    ropec_in = din("ropec", [128, L])
    ropes_in = din("ropes", [128, L])
    ident_in = din("ident", [128, 128])
    cbf_in = din("cbf", [128, 1024])
    edge_in = din("edge", [128, 64])
    sel_in = din("sel", [128, 128])

    y_out = nc.dram_tensor("y", [NSEQ, L, D], F32, kind="ExternalOutput").ap()
    okind = "ExternalOutput" if debug else "Internal"
    xsA = nc.dram_tensor("xsA", [NSEQ, L, D], F32, kind=okind).ap()
    xsB = nc.dram_tensor("xsB", [NSEQ, L, D], F32, kind=okind).ap()
    csA = nc.dram_tensor("csA", [NSEQ, CTX, D], F32, kind=okind).ap()
    csB = nc.dram_tensor("csB", [NSEQ, CTX, D], F32, kind=okind).ap()
    yb = nc.dram_tensor("yb", [NSEQ, 12, 128, LK], BF16, kind=okind).ap()
    modd = nc.dram_tensor("modd", [NL, 3, 6 * D], F32, kind=okind).ap()

    S = Sched(nc)
    es = contextlib.ExitStack()
    with es:
        SB_BASE = 16576
        SB_LIMIT = 229376
        state = {"persist": 0, "top": SB_BASE}

        def alloc(name, free_elems, dt, base=None):
            nbytes = free_elems * (4 if dt == F32 else 2)
            off = state["top"] if base is None else base
            off = (off + 31) // 32 * 32
            t = nc.alloc_sbuf_tensor_at(name, [128, free_elems], dt, offset=off)
            if base is None:
                state["top"] = off + nbytes
                assert state["top"] <= SB_LIMIT, (name, state["top"])
            return t

        cnt = [0]

        def palloc(name, free_elems, dt):
            cnt[0] += 1
            return alloc("%s_%d" % (name, cnt[0]), free_elems, dt)

        ps = es.enter_context(nc.psum_tensor("ps", [128, 4096], F32))
        bank = [Buf("bank%d" % i, excl=True) for i in range(8)]

        def pbank(b, n=512, off=0):
            return ps[:, b * 512 + off: b * 512 + off + n]

        ident = alloc("ident", 128, F32); b_const = Buf("const")
        cbf = alloc("cbf", 1024, BF16)
        pswap = cbf[:, 0:128]
        bdm = cbf[:, 128:256]
        onesA = cbf[:, 256:384]
        onesB = cbf[:, 384:512]
        maskLR = cbf[:, 512:1024]
        edge = alloc("edge", 64, F32)
        selT = alloc("selT", 128, F32)
        epst = alloc("epst", 1, F32)
        modT = alloc("modT", NL * 3 * 4 * 8, F32); b_modT = Buf("modT")
        gT = alloc("gT", 4 * 8, F32)
        bgT = alloc("bgT", NL * 3 * 8, F32)
        pscT = alloc("pscT", NL * 4, F32)
        cvT = alloc("cvT", NL * 4 * FC, F32)
        qkg = alloc("qkg", NL * 2, F32)
        sinkT = alloc("sinkT", NL * 4, F32)
        stat = alloc("stat", 16, F32); b_stat = [Buf("stat%d" % i) for i in range(8)]
        identb_t = alloc("identb", 128, BF16)
        identb = identb_t[:]
        P_END = state["top"]

        def modT_ap(l, v, k, c):
            i = ((l * 3 + v) * 4 + k) * 8 + c
            return modT[:, i:i + 1]

        S.dma("sync", f_dma(ident[:], ident_in), b_const, writes=[b_const], partial=True)
        S.dma("gpsimd", f_dma(cbf[:], cbf_in), b_const, writes=[b_const], partial=True)
        S.dma("sync", f_dma(edge[:], edge_in), b_const, writes=[b_const], partial=True)
        S.dma("sync", f_dma(selT[:], sel_in), b_const, writes=[b_const], partial=True)
        S.op("vector", f_memset(epst[:], EPS), writes=[b_const], partial=True)
        S.op("vector", f_copy(identb, ident[:]), reads=[b_const], writes=[b_const])

        def small_T(dst, src_ap):
            def fn(e):
                with nc.allow_non_contiguous_dma(reason="tiny per-feature vectors, loaded once"):
                    return e.dma_start(out=dst, in_=src_ap.rearrange("(c p) -> p c", p=128))
            S.dma("sync", fn, b_const, writes=[b_const], partial=True)

        for l in range(NL):
            small_T(gT[:, l * 8:(l + 1) * 8], norm1_g[l])
            small_T(gT[:, 16 + l * 8:16 + (l + 1) * 8], norm2_g[l])
            for i in range(3):
                small_T(bgT[:, (l * 3 + i) * 8:(l * 3 + i + 1) * 8], b_gate[l, i])
                small_T(cvT[:, (l * 4 + i) * FC:(l * 4 + i + 1) * FC], conv_w[l, i])
            small_T(cvT[:, (l * 4 + 3) * FC:(l * 4 + 4) * FC], conv_b[l])
            small_T(pscT[:, l * 4:(l + 1) * 4], pool_scale[l])

            for (p0, col, src) in ((0, 2 * l, q_norm_g[l]), (64, 2 * l, q_norm_g[l]), (0, 2 * l + 1, k_norm_g[l]), (64, 2 * l + 1, k_norm_g[l])):
                def fq(e, p0=p0, col=col, src=src):
                    with nc.allow_non_contiguous_dma(reason="tiny"):
                        return e.dma_start(out=qkg[p0:p0 + 64, col:col + 1], in_=src.rearrange("(p o) -> p o", o=1))
                S.dma("sync", fq, b_const, writes=[b_const], partial=True)
            for j in range(4):
                for h in range(2):
                    def fs(e, l=l, j=j, h=h):
                        with nc.allow_non_contiguous_dma(reason="tiny"):
                            return e.dma_start(out=sinkT[h * 64:(h + 1) * 64, l * 4 + j:l * 4 + j + 1],
                                               in_=win_sink[l:l + 1, h * 4 + j:h * 4 + j + 1].broadcast_to([64, 1]))
                    S.dma("sync", fs, b_const, writes=[b_const], partial=True)
        S.op("scalar", f_act(sinkT[:], sinkT[:], AF.Exp), reads=[b_const], writes=[b_const])

        state["top"] = P_END
        if stop == "consts":
            S.barrier()
            sems = {k: es.enter_context(nc.semaphore(k)) for k in S.semkeys}
            S.run(sems)
            return nc
        s3 = alloc("s3", D, F32)
        sT = alloc("sT", KC * 128, F32)
        wm = [alloc("wm%d" % i, 8 * 512, F32) for i in range(2)]; b_wm = [Buf("wm0"), Buf("wm1")]
        bm3 = alloc("bm3", 6 * D, F32); b_bm3 = Buf("bm3")
        mrow = alloc("mrow", 6 * D, F32); b_mrow = Buf("mrow")
        b_s3 = Buf("s3"); b_sT = Buf("sT")
        S.op("gpsimd", f_memset(s3[:], 0.0), writes=[b_s3])
        S.op("gpsimd", f_memset(mrow[:], 0.0), writes=[b_mrow])
        S.dma("sync", f_dma(s3[0:3, :], c3_in), b_s3, writes=[b_s3])
        S.op("scalar", f_act(s3[0:3, :], s3[0:3, :], AF.Silu), reads=[b_s3], writes=[b_s3])
        for hh in range(2):
            S.op("tensor", f_transposes([(pbank(hh, 128, 128 * c4), s3[:, (hh * 4 + c4) * 128:(hh * 4 + c4 + 1) * 128], ident[:]) for c4 in range(4)]),
                 reads=[b_s3, b_const], writes=[bank[hh]])
            S.op("vector", f_copy(sT[:, hh * 512:(hh + 1) * 512], pbank(hh)), reads=[bank[hh]], writes=[b_sT], partial=True)
        for l in range(NL):
            S.dma("sync", f_dma(bm3[0:3, :], b_mod[l:l + 1, :].broadcast_to([3, 6 * D])), b_bm3, writes=[b_bm3])
            for pc in range(12):
                wb = wm[pc % 2]; bw = b_wm[pc % 2]
                S.dma("sync", f_dma(wb[:].rearrange("p (c n) -> p c n", c=KC),
                                    w_mod[l][:, pc * 512:(pc + 1) * 512].rearrange("(c p) n -> p c n", p=128)),
                      bw, writes=[bw])
                pb = 1 + pc % 2
                S.op("tensor", f_mms([(pbank(pb), sT[:, c * 128:(c + 1) * 128], wb[:, c * 512:(c + 1) * 512], c == 0, c == KC - 1)
                                      for c in range(KC)]), reads=[b_sT, bw], writes=[bank[pb]])
                S.op("vector", f_tt(mrow[0:3, pc * 512:(pc + 1) * 512], pbank(pb)[0:3, :], bm3[0:3, pc * 512:(pc + 1) * 512], ALU.add),
                     reads=[bank[pb], b_bm3], writes=[b_mrow], partial=True)
            S.dma("sync", f_dma(modd[l], mrow[0:3, :]), b_mrow, reads=[b_mrow])
            for k, mi in enumerate((0, 1, 3, 4)):
                for hh in range(2):
                    S.op("tensor", f_transposes([(pbank(3 + hh, 128, 128 * c4), mrow[:, mi * D + (hh * 4 + c4) * 128: mi * D + (hh * 4 + c4 + 1) * 128], ident[:])
                                                 for c4 in range(4)]), reads=[b_mrow, b_const], writes=[bank[3 + hh]])
                for v in range(3):
                    i0 = ((l * 3 + v) * 4 + k) * 8
                    for hh in range(2):
                        src = pbank(3 + hh).rearrange("p (c v) -> p c v", v=128)[:, :, v]
                        S.op("vector", f_copy(modT[:, i0 + hh * 4:i0 + hh * 4 + 4], src), reads=[bank[3 + hh]], writes=[b_modT], partial=True)
        for l in range(NL):
            for v in range(3):
                for k, goff in ((1, l * 8), (3, 16 + l * 8)):
                    i0 = ((l * 3 + v) * 4 + k) * 8
                    S.op("vector", f_stt(modT[:, i0:i0 + 8], modT[:, i0:i0 + 8], 1.0, gT[:, goff:goff + 8], ALU.add, ALU.mult),
                         reads=[b_modT, b_const], writes=[b_modT])
        S.barrier()

        def load_gate_tiles(dst_list):
            for (t, b, l, v, mi) in dst_list:
                S.dma("sync", f_dma(t[:], modd[l, v:v + 1, mi * D:(mi + 1) * D].broadcast_to([128, D])), b, writes=[b])

        def norm_block(src_rows, xslot, b_x, junk, b_junk, hT, b_hT, col0, l, v, k_shift, k_gmod, tb, si, mode="all"):
            if mode in ("all", "load"):
                S.dma("sync", f_dma(xslot, src_rows), b_x, writes=[b_x])
            if mode == "load":
                return
            _nb = "9"
            do_pre = mode in ("all", "compute", "pre")
            do_post = mode in ("all", "compute", "post")
            ms = stat[:, 2 * si:2 * si + 1]
            rs = stat[:, 2 * si + 1:2 * si + 2]
            if do_pre:
                S.op("scalar", f_act(junk, xslot, AF.Square, scale=1.0 / 32.0, accum=ms), reads=[b_x], writes=[b_junk, b_stat[si]])
                S.op("scalar", f_act(rs, ms, AF.Sqrt, bias=epst[:, 0:1], scale=1.0), reads=[b_stat[si], b_const], writes=[b_stat[si]])
                S.op("vector", f_recip(rs, rs), reads=[b_stat[si]], writes=[b_stat[si]])
                S.op("vector", f_ts(junk, xslot, rs, None, ALU.mult), reads=[b_x, b_stat[si]], writes=[b_junk])
            if not do_post:
                return
            if _nb == "2":
                return
            pt = pbank(tb).bitcast(BF16)
            S.op("tensor", f_transposes([(pt[:, c * 128:(c + 1) * 128], junk[:, c * 128:(c + 1) * 128], identb) for c in range(KC)]),
                 reads=[b_junk, b_const], writes=[bank[tb]])
            if _nb == "3":
                return
            for c in range(KC):
                o = hT[:, c * hT_w + col0: c * hT_w + col0 + 128]
                i = pt[:, c * 128:(c + 1) * 128]
                if False:
                    S.op("scalar", f_act(o, i, AF.Identity, bias=modT_ap(l, v, k_shift, c), scale=modT_ap(l, v, k_gmod, c)),
                         reads=[bank[tb], b_modT], writes=[b_hT], partial=True)
                else:
                    S.op("vector", f_ts(o, i, modT_ap(l, v, k_gmod, c), modT_ap(l, v, k_shift, c), ALU.mult, ALU.add),
                         reads=[bank[tb], b_modT], writes=[b_hT], partial=True)

        hT_w = 512

        def finalize():
            S.barrier()
            sems = {k: es.enter_context(nc.semaphore(k)) for k in S.semkeys}
            S.run(sems)
            return nc
        if stop == "p0":
            return finalize()
        for l in range(NL):
            last = (l == NL - 1)
            src_x = x_in if l == 0 else xsA
            src_c = ctx_in if l == 0 else csA

            state["top"] = P_END
            WIN = alloc("WIN", KC * 2048, BF16); b_WIN = Buf("WIN")
            WPL = alloc("WPL", 4 * 128, BF16); b_WPL = Buf("WPL")
            KTw = alloc("KTw", LK, BF16); b_KTw = Buf("KTw")
            KTg = alloc("KTg", LK, BF16); b_KTg = Buf("KTg")
            Vw = alloc("Vw", NKB * 192, BF16); b_Vw = Buf("Vw")
            Vg = alloc("Vg", NKB * 192, BF16); b_Vg = Buf("Vg")
            PW = 8 + L + 8
            pT = alloc("pT", 4 * PW, BF16); b_pT = Buf("pT")
            PWc = 8 + CTX + 8
            pTc = alloc("pTc", 4 * PWc, BF16); b_pTc = Buf("pTc")
            ropec_t = [alloc("ropec%d" % i, 512, F32) for i in range(2)]
            ropes_t = [alloc("ropes%d" % i, 512, F32) for i in range(2)]
            b_rope_t = [Buf("rope0"), Buf("rope1")]
            rope_state = {"i": 0}
            xs_ = [alloc("xslot%d" % i, D, F32) for i in range(2)]; b_xs = [Buf("xs0"), Buf("xs1")]
            junk = [alloc("junk%d" % i, D, BF16) for i in range(2)]; b_junk = [Buf("junk0"), Buf("junk1")]
            hT = alloc("hT", KC * 512, BF16); b_hT = Buf("hT")
            zb = [alloc("zb%d" % i, 512, BF16) for i in range(2)]; b_zb = [Buf("zb0"), Buf("zb1")]
            sq = [alloc("sq%d" % i, 512, BF16) for i in range(2)]; b_sq = [Buf("sq0"), Buf("sq1")]
            rsd = [alloc("rsd%d" % i, 512, F32) for i in range(2)]; b_rsd = [Buf("rsd0"), Buf("rsd1")]
            t1 = [alloc("t1_%d" % i, 512, F32) for i in range(2)]; b_t1 = [Buf("t1a"), Buf("t1b")]
            t2 = [alloc("t2_%d" % i, 512, F32) for i in range(2)]; b_t2 = [Buf("t2a"), Buf("t2b")]
            QA = alloc("QA", 8 * 512, BF16); b_QAc = [Buf("QA%d" % i) for i in range(8)]
            QB = alloc("QB", 8 * 512, BF16); b_QBc = [Buf("QB%d" % i) for i in range(8)]
            Pb = [alloc("Pb%d" % i, 1280, BF16) for i in range(2)]; b_Pb = [Buf("Pb0"), Buf("Pb1")]
            yst = alloc("yst", 12 * 512, BF16); b_yst = Buf("yst")
            pl = [alloc("pl%d" % i, 528, F32) for i in range(2)]; b_pl = [Buf("pl0"), Buf("pl1")]
            pld = alloc("pld", 4 * 512, BF16); b_pld = Buf("pld")
            rcp = alloc("rcp", 512, F32); b_rcp = Buf("rcp")
            Dsb = alloc("Dsb", 512, F32); b_Dsb = Buf("Dsb")
            S.op("gpsimd", f_memset(Dsb[:], 0.0), writes=[b_Dsb])

            def wcols(dst0, src0, n, l=l):
                S.dma("gpsimd", f_dma(WIN[:].rearrange("p (c n) -> p c n", c=KC)[:, :, dst0:dst0 + n],
                                      w_in[l][:, src0:src0 + n].rearrange("(c p) n -> p c n", p=128)),
                      b_WIN, writes=[b_WIN], partial=True)
            wcols(0, 0, 512)
            wcols(512, 1024, 128)
            wcols(640, 1792, 128)
            wcols(768, 1152, 128)
            wcols(896, 1920, 128)
            for a, base in ((0, 512), (1, 1280)):
                for j in range(4):
                    for h in range(2):
                        wcols(1024 + a * 512 + j * 128 + h * 64, base + (h * 4 + j) * 64, 64)
            S.dma("gpsimd", f_dma(WPL[:].rearrange("p (g d) -> p g d", g=4), w_pool[l].rearrange("g c d -> c g d")),
                  b_WPL, writes=[b_WPL])

            def load_rope(t0, n):
                i = rope_state["i"] = 1 - rope_state["i"]
                S.dma("sync", f_dma(ropec_t[i][:, 0:n], ropec_in[:, t0:t0 + n]), b_rope_t[i], writes=[b_rope_t[i]], partial=True)
                S.dma("sync", f_dma(ropes_t[i][:, 0:n], ropes_in[:, t0:t0 + n]), b_rope_t[i], writes=[b_rope_t[i]], partial=True)
            S.op("gpsimd", f_memset(Vw[:].rearrange("p (k w) -> p k w", w=192)[:, :, 65:128], 0.0), writes=[b_Vw], partial=True)
            S.op("gpsimd", f_memset(Vg[:].rearrange("p (k w) -> p k w", w=192)[:, :, 65:128], 0.0), writes=[b_Vg], partial=True)
            S.op("gpsimd", f_memset(Vw[:].rearrange("p (k w) -> p k w", w=192)[:, :, 64:65], 1.0), writes=[b_Vw], partial=True)
            S.op("gpsimd", f_memset(Vg[:].rearrange("p (k w) -> p k w", w=192)[:, :, 64:65], 1.0), writes=[b_Vg], partial=True)
            S.op("gpsimd", f_memset(pT[:].rearrange("p (g w) -> p g w", g=4)[:, :, 0:8], 0.0), writes=[b_pT], partial=True)
            S.op("gpsimd", f_memset(pT[:].rearrange("p (g w) -> p g w", g=4)[:, :, 8 + L:PW], 0.0), writes=[b_pT], partial=True)
            S.op("gpsimd", f_memset(pTc[:].rearrange("p (g w) -> p g w", g=4)[:, :, 0:8], 0.0), writes=[b_pTc], partial=True)
            S.op("gpsimd", f_memset(pTc[:].rearrange("p (g w) -> p g w", g=4)[:, :, 8 + CTX:PWc], 0.0), writes=[b_pTc], partial=True)
            S.op("gpsimd", f_memset(QA[64:128, :], 0.0), writes=b_QAc)
            S.op("gpsimd", f_memset(QB[0:64, :], 0.0), writes=b_QBc)

            WINc = lambda c, a, n: WIN[:, c * 2048 + a: c * 2048 + a + n]

            def rope_stages(zsrc, b_zsrc, n, tok0, outs, pbs, bi, normed, gcol, is_ctx=False, dve_copy=False):
                z16 = zb[bi]; bz = b_zb[bi]
                zf = t1[bi][:, 0:n]; bzf = b_t1[bi]

                def s1():
                    if normed:
                        S.op("scalar", f_act(sq[bi][:, 0:n], zsrc, AF.Square), reads=[b_zsrc], writes=[b_sq[bi]])
                    if dve_copy:
                        S.op("vector", f_copy(zf, zsrc), reads=[b_zsrc], writes=[bzf])
                    else:
                        S.op("scalar", f_act(zf, zsrc, AF.Identity), reads=[b_zsrc], writes=[bzf])

                def s2():
                    if normed:
                        S.op("tensor", f_mms([(pbank(pbs[0], n), bdm, sq[bi][:, 0:n], True, True)]), reads=[b_sq[bi], b_const], writes=[bank[pbs[0]]])
                        S.op("scalar", f_act(rsd[bi][:, 0:n], pbank(pbs[0], n), AF.Sqrt, bias=epst[:, 0:1], scale=1.0),
                             reads=[bank[pbs[0]], b_const], writes=[b_rsd[bi]])
                        S.op("vector", f_recip(rsd[bi][:, 0:n], rsd[bi][:, 0:n]), reads=[b_rsd[bi]], writes=[b_rsd[bi]])
                        S.op("vector", f_stt(zf, zf, qkg[:, gcol:gcol + 1], rsd[bi][:, 0:n], ALU.mult, ALU.mult),
                             reads=[bzf, b_rsd[bi], b_const], writes=[bzf])
                    if is_ctx:
                        for (o, bo, p0, p1) in outs:
                            S.op("vector", f_copy(o[p0:p1], zf[p0:p1]), reads=[bzf], writes=[bo], partial=True)
                        return
                    if dve_copy:
                        S.op("vector", f_copy(z16[:, 0:n], zf), reads=[bzf], writes=[bz])
                    else:
                        S.op("scalar", f_act(z16[:, 0:n], zf, AF.Identity), reads=[bzf], writes=[bz])

                def s3():
                    if is_ctx:
                        return
                    S.op("tensor", f_mms([(pbank(pbs[1], n), pswap, z16[:, 0:n], True, True)]), reads=[bz, b_const], writes=[bank[pbs[1]]])
                    ri = rope_state["i"]
                    S.op("vector", f_tt(t2[bi][:, 0:n], pbank(pbs[1], n), ropes_t[ri][:, 0:n], ALU.mult),
                         reads=[bank[pbs[1]], b_rope_t[ri]], writes=[b_t2[bi]])
                    S.op("vector", f_tt(zf, zf, ropec_t[ri][:, 0:n], ALU.mult), reads=[bzf, b_rope_t[ri]], writes=[bzf])
                    for (o, bo, p0, p1) in outs:
                        S.op("vector", f_tt(o[p0:p1], zf[p0:p1], t2[bi][p0:p1, 0:n], ALU.add),
                             reads=[bzf, b_t2[bi]], writes=[bo], partial=True)
                return s1, s2, s3

            def rope_chunk(*args, **kw):
                s1, s2, s3 = rope_stages(*args, **kw)
                s1(); s2(); s3()

            def pool_tile(pbuf, b_pbuf, W, n, t0, Lseq, ycol0, mode="all", groups=(0, 1, 2, 3), pbk=None):
                o = 8 + t0
                first = (t0 == 0)
                lastt = (t0 + n == Lseq)
                for g, w in enumerate((2, 4, 8, 16)):
                    if g not in groups:
                        continue
                    base = g * W
                    dst = pld[:, g * 512: g * 512 + n]
                    if mode == "project":
                        pb = pbk if pbk is not None else 4 + g % 2
                        S.op("tensor", f_mms([(pbank(pb, n), WPL[:, g * 128:(g + 1) * 128], dst, True, True)]),
                             reads=[b_pld, b_WPL], writes=[bank[pb]])
                        S.op("scalar", f_act(yst[:, g * 512 + ycol0: g * 512 + ycol0 + n], pbank(pb, n), AF.Identity, scale=pscT[:, l * 4 + g: l * 4 + g + 1]),
                             reads=[bank[pb], b_const], writes=[b_yst], partial=True)
                        continue
                    wd = n + w - 2
                    a0 = base + o - w // 2
                    S.op("gpsimd", f_tt(pl[0][:, 0:wd], pbuf[:, a0:a0 + wd], pbuf[:, a0 + 1:a0 + 1 + wd], ALU.add),
                         reads=[b_pbuf], writes=[b_pl[0]])
                    cur = 0
                    sh = 2
                    while sh < w:
                        wd2 = wd - sh
                        S.op("gpsimd", f_tt(pl[1 - cur][:, 0:wd2], pl[cur][:, 0:wd2], pl[cur][:, sh:sh + wd2], ALU.add),
                             reads=[b_pl[cur]], writes=[b_pl[1 - cur]])
                        cur = 1 - cur
                        wd = wd2
                        sh *= 2
                    assert wd == n
                    dst = pld[:, g * 512: g * 512 + n]
                    S.op("gpsimd", f_ts(pl[cur][:, 0:n], pl[cur][:, 0:n], 1.0 / w, None, ALU.mult), reads=[b_pl[cur]], writes=[b_pl[cur]])
                    S.op("gpsimd", f_tt(dst, pl[cur][:, 0:n], pbuf[:, base + o: base + o + n], ALU.subtract),
                         reads=[b_pl[cur], b_pbuf], writes=[b_pld], partial=True)
                    if first:
                        S.op("gpsimd", f_tt(pl[cur][:, 0:8], pl[cur][:, 0:8], edge[:, g * 16: g * 16 + 8], ALU.mult),
                             reads=[b_pl[cur], b_const, b_pld], writes=[b_pl[cur]])
                        S.op("gpsimd", f_tt(dst[:, 0:8], pl[cur][:, 0:8], pbuf[:, base + o: base + o + 8], ALU.subtract),
                             reads=[b_pl[cur], b_pbuf], writes=[b_pld], partial=True)
                    if lastt:
                        S.op("gpsimd", f_tt(pl[cur][:, n - 8:n], pl[cur][:, n - 8:n], edge[:, g * 16 + 8: g * 16 + 16], ALU.mult),
                             reads=[b_pl[cur], b_const, b_pld], writes=[b_pl[cur]])
                        S.op("gpsimd", f_tt(dst[:, n - 8:n], pl[cur][:, n - 8:n], pbuf[:, base + o + n - 8: base + o + n], ALU.subtract),
                             reads=[b_pl[cur], b_pbuf], writes=[b_pld], partial=True)
                    if mode == "compute":
                        continue
                    pb = 4 + g % 2
                    S.op("tensor", f_mms([(pbank(pb, n), WPL[:, g * 128:(g + 1) * 128], dst, True, True)]),
                         reads=[b_pld, b_WPL], writes=[bank[pb]])
                    S.op("scalar", f_act(yst[:, g * 512 + ycol0: g * 512 + ycol0 + n], pbank(pb, n), AF.Identity, scale=pscT[:, l * 4 + g: l * 4 + g + 1]),
                         reads=[bank[pb], b_const], writes=[b_yst], partial=True)

            def attention(n_q, KT, b_KT, Vb, b_Vb, kbs, jq, sink_col, ydst, Sbanks_list, mask_fn=None, qcols=0, ob=6, db=7, dbc=5, hooks=None):
                nk = len(kbs)
                qa = QA[:, jq * 512 + qcols: jq * 512 + qcols + n_q]
                qb_ = QB[:, jq * 512 + qcols: jq * 512 + qcols + n_q]

                def stage_qk(i):
                    kb = kbs[i]
                    sbk = Sbanks_list[i % 2]
                    pbuf = Pb[i % 2]; bp = b_Pb[i % 2]
                    kcol = KT[:, kb * 128:(kb + 1) * 128]
                    S.op("tensor", f_mms([(pbank(sbk[0], n_q), kcol, qa, True, True), (pbank(sbk[1], n_q), kcol, qb_, True, True)]),
                         reads=[b_KT, b_QAc[jq], b_QBc[jq]], writes=[bank[sbk[0]], bank[sbk[1]]])
                    if n_q == 512:
                        S.op("scalar", f_act(pbuf[:, 0:1024], ps[:, sbk[0] * 512: sbk[0] * 512 + 1024], AF.Exp, scale=SM_SCALE),
                             reads=[bank[sbk[0]], bank[sbk[1]]], writes=[bp])
                    else:
                        S.op("scalar", f_act(pbuf[:, 0:n_q], pbank(sbk[0], n_q), AF.Exp, scale=SM_SCALE), reads=[bank[sbk[0]]], writes=[bp], partial=True)
                        S.op("scalar", f_act(pbuf[:, 512:512 + n_q], pbank(sbk[1], n_q), AF.Exp, scale=SM_SCALE), reads=[bank[sbk[1]]], writes=[bp], partial=True)

                def stage_pv(i):
                    kb = kbs[i]
                    pbuf = Pb[i % 2]; bp = b_Pb[i % 2]
                    pa = pbuf[:, 0:n_q]; pb2 = pbuf[:, 512:512 + n_q]
                    st = (i == 0); sp = (i == nk - 1)
                    S.op("tensor", f_mms([
                        (pbank(ob, n_q), Vb[:, kb * 192: kb * 192 + 128], pa, st, sp),
                        (pbank(db, n_q), Vb[:, kb * 192 + 64: kb * 192 + 192], pb2, st, sp)]),
                        reads=[bp, b_Vb, b_const], writes=[bank[ob], bank[db]], partial=(i > 0))

                stage_qk(0)
                for i in range(nk):
                    if i + 1 < nk:
                        stage_qk(i + 1)
                    stage_pv(i)
                    if hooks and i in hooks:
                        for fn in hooks[i]:
                            fn()
                finish_attn(n_q, sink_col, ydst, ob=ob, db=db, dbc=dbc)

            def finish_attn(n_q, sink_col, ydst, col0=0, ob=6, db=7, dbc=5):
                S.op("vector", f_copy(Dsb[64:65, 0:n_q], pbank(ob, n_q)[64:65]), reads=[bank[ob]], writes=[b_Dsb], partial=True)
                S.op("vector", f_copy(Dsb[0:1, 0:n_q], pbank(db, n_q)[0:1]), reads=[bank[db]], writes=[b_Dsb], partial=True)
                S.op("tensor", f_mms([(pbank(dbc, n_q), selT[:], Dsb[:, 0:n_q], True, True)]), reads=[b_Dsb, b_const], writes=[bank[dbc]])
                if sink_col is not None:
                    S.op("vector", f_ts(rcp[:, 0:n_q], pbank(dbc, n_q), sinkT[:, sink_col:sink_col + 1], None, ALU.add), reads=[bank[dbc], b_const], writes=[b_rcp])
                    S.op("vector", f_recip(rcp[:, 0:n_q], rcp[:, 0:n_q]), reads=[b_rcp], writes=[b_rcp])
                else:
                    S.op("vector", f_recip(rcp[:, 0:n_q], pbank(dbc, n_q)), reads=[bank[dbc]], writes=[b_rcp])
                S.op("vector", f_tt(ydst[0:64], pbank(ob, n_q)[0:64], rcp[0:64, 0:n_q], ALU.mult), reads=[bank[ob], b_rcp], writes=[b_yst], partial=True)
                S.op("vector", f_tt(ydst[64:128], pbank(db, n_q)[64:128], rcp[64:128, 0:n_q], ALU.mult), reads=[bank[db], b_rcp], writes=[b_yst], partial=True)

            def window_attention(jq, t0, sink_col, ydst):
                info = {}

                def stage1(qb):
                    gq = t0 // 128 + qb
                    kbs = []
                    if gq >= 1:
                        kbs.append((gq - 1, "L"))
                    kbs.append((gq, "C"))
                    if gq + 1 < NB:
                        kbs.append((gq + 1, "R"))
                    kbs.append((NB, "X"))
                    kbs.append((NB + 1, "X"))
                    info[qb] = kbs
                    npc = len(kbs)
                    sel = qb % 2
                    sb3 = (0, 1, 2) if sel == 0 else (2, 3, 4)
                    scol0 = 0 if sel == 0 else 1280
                    pbuf = Pb[sel]; bp = b_Pb[sel]
                    qa = QA[:, jq * 512 + qb * 128: jq * 512 + (qb + 1) * 128]
                    qb_ = QB[:, jq * 512 + qb * 128: jq * 512 + (qb + 1) * 128]
                    mm = []
                    for i, (kb, kind) in enumerate(kbs):
                        kcol = KTw[:, kb * 128:(kb + 1) * 128]
                        pa_ = ps[:, scol0 + (2 * i) * 128: scol0 + (2 * i + 1) * 128]
                        pb_ = ps[:, scol0 + (2 * i + 1) * 128: scol0 + (2 * i + 2) * 128]
                        if kind in ("L", "R"):
                            mk = maskLR[:, 0:128] if kind == "L" else maskLR[:, 256:384]
                            mm.append((pa_, kcol, qa, True, False))
                            mm.append((pa_, identb, mk, False, True))
                            mm.append((pb_, kcol, qb_, True, False))
                            mm.append((pb_, identb, mk, False, True))
                        else:
                            mm.append((pa_, kcol, qa, True, True))
                            mm.append((pb_, kcol, qb_, True, True))
                    bks = [bank[b] for b in sb3]
                    S.op("tensor", f_mms(mm), reads=[b_KTw, b_QAc[jq], b_QBc[jq], b_const], writes=bks)
                    S.op("scalar", f_act(pbuf[:, 0:npc * 256], ps[:, scol0: scol0 + npc * 256], AF.Exp, scale=SM_SCALE), reads=bks, writes=[bp])

                def stage2(qb):
                    kbs = info[qb]
                    npc = len(kbs)
                    sel = qb % 2
                    pbuf = Pb[sel]; bp = b_Pb[sel]
                    mm = []
                    oc = qb * 128
                    for i, (kb, kind) in enumerate(kbs):
                        st = (i == 0); sp = (i == npc - 1)
                        pa = pbuf[:, (2 * i) * 128:(2 * i + 1) * 128]
                        pb2 = pbuf[:, (2 * i + 1) * 128:(2 * i + 2) * 128]
                        mm.append((pbank(6, 128, oc), Vw[:, kb * 192: kb * 192 + 128], pa, st, sp))
                        mm.append((pbank(7, 128, oc), Vw[:, kb * 192 + 64: kb * 192 + 192], pb2, st, sp))
                    S.op("tensor", f_mms(mm), reads=[bp, b_Vw, b_const], writes=[bank[6], bank[7]], partial=(qb > 0))

                stage1(0)
                for qb in range(4):
                    if qb + 1 < 4:
                        stage1(qb + 1)
                    stage2(qb)
                finish_attn(512, sink_col, ydst, ob=6, db=7, dbc=5)

            SG = [(0, 1), (2, 3)]

            if stop == "A0":
                return finalize()
            for s in range(NSEQ):
                import os as _os
                _dbg = _os.environ.get("KDBG", "")
                if stop == "A0b":
                    return finalize()
                tiles = [("ctx", 0, CTX)] + [("lat", t * 512, 512) for t in range(NT)]
                blkc = 0
                for (kind, t0, n) in tiles:
                    is_ctx = (kind == "ctx")
                    v = 2 if is_ctx else s
                    src = src_c[s] if is_ctx else src_x[s]
                    nb = n // 128
                    if not is_ctx:
                        load_rope(t0, n)
                    for b in range(nb):
                        si = blkc % 2
                        norm_block(src[t0 + b * 128: t0 + (b + 1) * 128, :], xs_[si][:], b_xs[si], junk[si][:], b_junk[si],
                                   hT, b_hT, b * 128, l, v, 0, 1, si, si)
                        blkc += 1
                        if stop == "A1":
                            return finalize()
                    kbase = (L + t0) if is_ctx else t0
                    if not (is_ctx and last):
                        for g in range(4):
                            pb = 2 + g % 2
                            S.op("tensor", f_mms([(pbank(pb, n), WINc(c, g * 128, 128), hT[:, c * 512: c * 512 + n], c == 0, c == KC - 1) for c in range(KC)]),
                                 reads=[b_hT, b_WIN], writes=[bank[pb]])
                            if is_ctx:
                                dst = pTc[:, g * PWc + 8: g * PWc + 8 + n]; bd_ = b_pTc
                            else:
                                dst = pT[:, g * PW + 8 + t0: g * PW + 8 + t0 + n]; bd_ = b_pT
                            S.op("scalar", f_act(dst, pbank(pb, n), AF.Identity), reads=[bank[pb]], writes=[bd_], partial=True)
                    if stop == "A2":
                        return finalize()
                    for a, (wcol, KT, bKT, normed) in enumerate(((512, KTw, b_KTw, False), (640, KTg, b_KTg, True))):
                        pb = 2 + a
                        S.op("tensor", f_mms([(pbank(pb, n), WINc(c, wcol, 128), hT[:, c * 512: c * 512 + n], c == 0, c == KC - 1) for c in range(KC)]),
                             reads=[b_hT, b_WIN], writes=[bank[pb]])
                        rope_chunk(pbank(pb, n), bank[pb], n, t0, [(KT[:, kbase:kbase + n], bKT, 0, 128)], (4, 5), a, normed, 2 * l + 1, is_ctx=is_ctx)
                    if stop == "A3" or (stop == "A3b" and not is_ctx):
                        return finalize()
                    for b in range(nb):
                        pb = 6 + b % 2
                        S.op("tensor", f_mms([(pbank(pb, 256), hT[:, c * 512 + b * 128: c * 512 + (b + 1) * 128], WINc(c, 768, 256), c == 0, c == KC - 1) for c in range(KC)]),
                             reads=[b_hT, b_WIN], writes=[bank[pb]])
                        kblk = kbase // 128 + b
                        S.op("scalar", f_act(Vw[:, kblk * 192: kblk * 192 + 64], pbank(pb, 64, 0), AF.Identity), reads=[bank[pb]], writes=[b_Vw], partial=True)
                        S.op("scalar", f_act(Vw[:, kblk * 192 + 128: kblk * 192 + 192], pbank(pb, 64, 64), AF.Identity), reads=[bank[pb]], writes=[b_Vw], partial=True)
                        S.op("scalar", f_act(Vg[:, kblk * 192: kblk * 192 + 64], pbank(pb, 64, 128), AF.Identity), reads=[bank[pb]], writes=[b_Vg], partial=True)
                        S.op("scalar", f_act(Vg[:, kblk * 192 + 128: kblk * 192 + 192], pbank(pb, 64, 192), AF.Identity), reads=[bank[pb]], writes=[b_Vg], partial=True)
                        if stop is not None and stop.startswith("Vb") and not is_ctx and int(stop[2:]) == b:
                            return finalize()
                    if stop is not None and stop.startswith("At") and int(stop[2:]) * 512 == t0 + (0 if is_ctx else 512):
                        return finalize()

                if stop == "A":
                    return finalize()
                tilesB = ([] if last else [("ctx", 0, CTX)]) + [("lat", t * 512, 512) for t in range(NT)]

                def prepB(tile):
                    nonlocal blkc
                    (kind, t0, n) = tile
                    is_ctx = (kind == "ctx")
                    v = 2 if is_ctx else s
                    src = src_c[s] if is_ctx else src_x[s]
                    for b in range(n // 128):
                        si = blkc % 2
                        norm_block(src[t0 + b * 128: t0 + (b + 1) * 128, :], xs_[si][:], b_xs[si], junk[si][:], b_junk[si],
                                   hT, b_hT, b * 128, l, v, 0, 1, si, si)
                        blkc += 1

                def qchunk(tile, jq):
                    (kind_, t0_, n_) = tile
                    st = {}

                    def a():
                        S.op("tensor", f_mms([(pbank(7, n_), WINc(c, 1024 + jq * 128, 128), hT[:, c * 512: c * 512 + n_], c == 0, c == KC - 1) for c in range(KC)]),
                             reads=[b_hT, b_WIN], writes=[bank[7]])
                        outs = [(QA[:, jq * 512: jq * 512 + n_], b_QAc[jq], 0, 64), (QB[:, jq * 512: jq * 512 + n_], b_QBc[jq], 64, 128)]
                        st["s"] = rope_stages(pbank(7, n_), bank[7], n_, t0_, outs, (7, 7), 0 if jq < 4 else 1, jq >= 4, 2 * l, is_ctx=False, dve_copy=True)
                        st["s"][0]()
                    return [a, lambda: st["s"][1](), lambda: st["s"][2]()]

                prepB(tilesB[0])
                q0_done = False
                for ti, (kind, t0, n) in enumerate(tilesB):
                    is_ctx = (kind == "ctx")
                    nxt = tilesB[ti + 1] if ti + 1 < len(tilesB) else None
                    ycol = (L + t0) if is_ctx else t0
                    if is_ctx:
                        stg = {}
                        for step in range(8 + 2):
                            if step - 2 >= 0:
                                stg[step - 2][2]()
                            if 0 <= step - 1 < 8:
                                stg[step - 1][1]()
                            if step < 8:
                                jq = step
                                pb = 2 + jq % 2
                                S.op("tensor", f_mms([(pbank(pb, n), WINc(c, 1024 + jq * 128, 128), hT[:, c * 512: c * 512 + n], c == 0, c == KC - 1) for c in range(KC)]),
                                     reads=[b_hT, b_WIN], writes=[bank[pb]])
                                outs = [(QA[:, jq * 512: jq * 512 + n], b_QAc[jq], 0, 64), (QB[:, jq * 512: jq * 512 + n], b_QBc[jq], 64, 128)]
                                stg[jq] = rope_stages(pbank(pb, n), bank[pb], n, t0, outs, (4, 5), jq % 2, jq >= 4, 2 * l, is_ctx=True)
                                stg[jq][0]()
                        pool_tile(pTc, b_pTc, PWc, n, 0, CTX, 0)
                        for j in range(4):
                            attention(n, KTw, b_KTw, Vw, b_Vw, [NB, NB + 1], j, l * 4 + j, yst[:, (4 + j) * 512:(4 + j) * 512 + n], SG, ob=6, db=7, dbc=5)
                            attention(n, KTg, b_KTg, Vg, b_Vg, [NB, NB + 1], 4 + j, None, yst[:, (8 + j) * 512:(8 + j) * 512 + n], SG, ob=4, db=5, dbc=6)
                            if j == 0 and nxt is not None:
                                prepB(nxt)
                        S.dma("sync", f_dma(yb[s].rearrange("k p t -> p k t")[:, :, ycol:ycol + n],
                                            yst[:].rearrange("p (k t) -> p k t", k=12)[:, :, 0:n]), b_yst, reads=[b_yst])
                        continue

                    if not q0_done:
                        load_rope(t0, n)
                        for jq in (0, 4):
                            for fn in qchunk((kind, t0, n), jq):
                                fn()
                    q0_done = False
                    for j in range(4):
                        window_attention(j, t0, l * 4 + j, yst[:, (4 + j) * 512:(4 + j) * 512 + 512])
                        hooks = {}

                        def add(k, fn, hooks=hooks):
                            hooks.setdefault(min(NKB - 1, (k * NKB) // 34), []).append(fn)
                        if j < 3:
                            qw = qchunk((kind, t0, n), j + 1)
                            qg = qchunk((kind, t0, n), 4 + j + 1)
                            for k, fn in zip((1, 4, 7), qw):
                                add(k, fn)
                            for k, fn in zip((10, 14, 18), qg):
                                add(k, fn)
                            if j == 0:
                                add(20, lambda t0=t0, n=n: pool_tile(pT, b_pT, PW, n, t0, L, 0, mode="compute"))
                                for g in range(4):
                                    add(26 + 2 * g, lambda g=g, t0=t0, n=n: pool_tile(pT, b_pT, PW, n, t0, L, 0, mode="project", groups=(g,), pbk=7))
                        elif nxt is not None:
                            (kindn, t0n, nn) = nxt

                            def nb_fn(b, mode, t0n=t0n):
                                si = b % 2
                                norm_block(src_x[s][t0n + b * 128: t0n + (b + 1) * 128, :], xs_[si][:], b_xs[si], junk[si][:], b_junk[si],
                                           hT, b_hT, b * 128, l, s, 0, 1, 7, si, mode=mode)
                            for b in range(4):
                                add(2 * b, lambda b=b: nb_fn(b, "load"))
                                add(1 + 2 * b, lambda b=b: nb_fn(b, "pre"))
                                add(4 + 2 * b, lambda b=b: nb_fn(b, "post"))
                            add(12, lambda t0n=t0n, nn=nn: load_rope(t0n, nn))
                            qw = qchunk(nxt, 0)
                            qg = qchunk(nxt, 4)
                            for k, fn in zip((13, 16, 19), qw):
                                add(k, fn)
                            for k, fn in zip((22, 26, 30), qg):
                                add(k, fn)
                            q0_done = True
                        attention(512, KTg, b_KTg, Vg, b_Vg, list(range(NKB)), 4 + j, None, yst[:, (8 + j) * 512:(8 + j) * 512 + 512], SG, ob=4, db=5, dbc=6, hooks=hooks)
                    S.dma("sync", f_dma(yb[s].rearrange("k p t -> p k t")[:, :, ycol:ycol + n],
                                        yst[:].rearrange("p (k t) -> p k t", k=12)[:, :, 0:n]), b_yst, reads=[b_yst])
            S.barrier()
            if stop == "B":
                return finalize()

            state["top"] = P_END
            WBR = alloc("WBR", 3 * 4 * D, BF16); b_WBR = Buf("WBR")
            WGT = alloc("WGT", 3 * KC * D, BF16); b_WGT = Buf("WGT")
            WOU = alloc("WOU", KC * D, BF16); b_WOU = Buf("WOU")
            xq = [alloc("xq%d" % i, D, F32) for i in range(8)]; b_xq = [Buf("xq%d" % i) for i in range(8)]
            junkm = [alloc("junkm%d" % i, D, BF16) for i in range(2)]; b_junkm = [Buf("jm0"), Buf("jm1")]
            hTm = alloc("hTm", KC * 512, BF16); b_hTm = Buf("hTm")
            yT = [alloc("yT%d" % i, 12 * 512, BF16) for i in range(2)]; b_yT = [Buf("yT0"), Buf("yT1")]
            mT = alloc("mT", KC * 512, BF16); b_mT = Buf("mT")
            sg = [alloc("sg%d" % i, 512, F32) for i in range(2)]; b_sg = [Buf("sg0"), Buf("sg1")]
            macc = alloc("macc", 512, F32); b_macc = Buf("macc")
            mtmp = alloc("mtmp", 512, F32); b_mtmp = Buf("mtmp")
            gt1 = [alloc("gt1_%d" % i, D, F32) for i in range(2)]; b_gt1 = [Buf("gt1s"), Buf("gt1c")]
            otmp = [alloc("otmp%d" % i, 512, F32) for i in range(2)]; b_otmp = [Buf("ot0"), Buf("ot1")]

            S.dma("gpsimd", f_dma(WBR[:, 0:4 * D].rearrange("p (c n) -> p c n", c=4), w_branch[l, 0].rearrange("(c p) n -> p c n", p=128)),
                  b_WBR, writes=[b_WBR], partial=True)
            for i in (1, 2):
                for j in range(4):
                    for h in range(2):
                        r0 = (h * 4 + j) * 64
                        S.dma("gpsimd", f_dma(WBR[h * 64:(h + 1) * 64, (i * 4 + j) * D:(i * 4 + j + 1) * D], w_branch[l, i, r0:r0 + 64, :]),
                              b_WBR, writes=[b_WBR], partial=True)
            for i in range(3):
                S.dma("gpsimd", f_dma(WGT[:, i * KC * D:(i + 1) * KC * D].rearrange("p (c n) -> p c n", c=KC), w_gate[l, i].rearrange("(c p) n -> p c n", p=128)),
                      b_WGT, writes=[b_WGT], partial=True)
            S.dma("gpsimd", f_dma(WOU[:].rearrange("p (c n) -> p c n", c=KC), w_out[l].rearrange("(c p) n -> p c n", p=128)), b_WOU, writes=[b_WOU])

            dst_x = xsB
            dst_c = csB
            for s in range(NSEQ):
                load_gate_tiles([(gt1[0], b_gt1[0], l, s, 2)] + ([] if last else [(gt1[1], b_gt1[1], l, 2, 2)]))
                tilesM = ([] if last else [("ctx", 0, CTX)]) + [("lat", t * 512, 512) for t in range(NT)]

                def prepM_y(ti):
                    (kind, t0, n) = tilesM[ti]
                    ycol = (L + t0) if kind == "ctx" else t0
                    yt = yT[ti % 2]; byt = b_yT[ti % 2]
                    S.dma("sync", f_dma(yt[:].rearrange("p (k t) -> p k t", k=12)[:, :, 0:n], yb[s].rearrange("k p t -> p k t")[:, :, ycol:ycol + n]),
                          byt, writes=[byt])

                def prepM_blk(ti, b, mode="all"):
                    (kind, t0, n) = tilesM[ti]
                    is_ctx = (kind == "ctx")
                    v = 2 if is_ctx else s
                    src = src_c[s] if is_ctx else src_x[s]
                    q = (ti % 2) * 4 + b
                    norm_block(src[t0 + b * 128: t0 + (b + 1) * 128, :], xq[q][:], b_xq[q], junkm[b % 2][:], b_junkm[b % 2],
                               hTm, b_hTm, b * 128, l, v, 0, 1, b % 2, b % 2, mode=mode)

                prepM_y(0)
                for b in range(tilesM[0][2] // 128):
                    prepM_blk(0, b)
                for ti, (kind, t0, n) in enumerate(tilesM):
                    is_ctx = (kind == "ctx")
                    dstd = dst_c[s] if is_ctx else dst_x[s]
                    gtile = gt1[1] if is_ctx else gt1[0]
                    bgt = b_gt1[1] if is_ctx else b_gt1[0]
                    nb = n // 128
                    yt = yT[ti % 2]; byt = b_yT[ti % 2]
                    for oc in range(KC):
                        for i in range(3):
                            pg = 2 + i % 2
                            S.op("tensor", f_mms([(pbank(pg, n), WGT[:, (i * KC + c) * D + oc * 128:(i * KC + c) * D + (oc + 1) * 128], hTm[:, c * 512: c * 512 + n], c == 0, c == KC - 1)
                                                  for c in range(KC)]), reads=[b_hTm, b_WGT], writes=[bank[pg]])
                            S.op("scalar", f_act(sg[i % 2][:, 0:n], pbank(pg, n), AF.Sigmoid, bias=bgT[:, (l * 3 + i) * 8 + oc:(l * 3 + i) * 8 + oc + 1], scale=1.0),
                                 reads=[bank[pg], b_const], writes=[b_sg[i % 2]])
                            pbx = 4 + i % 2
                            S.op("tensor", f_mms([(pbank(pbx, n), WBR[:, (i * 4 + c) * D + oc * 128:(i * 4 + c) * D + (oc + 1) * 128], yt[:, (i * 4 + c) * 512:(i * 4 + c) * 512 + n], c == 0, c == 3)
                                                  for c in range(4)]), reads=[byt, b_WBR], writes=[bank[pbx]])
                            if i == 0:
                                S.op("vector", f_tt(macc[:, 0:n], pbank(pbx, n), sg[i % 2][:, 0:n], ALU.mult), reads=[bank[pbx], b_sg[i % 2]], writes=[b_macc])
                            elif i == 1:
                                S.op("vector", f_tt(mtmp[:, 0:n], pbank(pbx, n), sg[i % 2][:, 0:n], ALU.mult), reads=[bank[pbx], b_sg[i % 2]], writes=[b_mtmp])
                                S.op("gpsimd", f_tt(macc[:, 0:n], macc[:, 0:n], mtmp[:, 0:n], ALU.add), reads=[b_macc, b_mtmp], writes=[b_macc])
                            else:
                                S.op("vector", f_tt(mtmp[:, 0:n], pbank(pbx, n), sg[i % 2][:, 0:n], ALU.mult), reads=[bank[pbx], b_sg[i % 2]], writes=[b_mtmp])
                                S.op("gpsimd", f_tt(mT[:, oc * 512: oc * 512 + n], macc[:, 0:n], mtmp[:, 0:n], ALU.add), reads=[b_macc, b_mtmp], writes=[b_mT], partial=True)
                    has_next = ti + 1 < len(tilesM)
                    pending = list(range(tilesM[ti + 1][2] // 128)) if has_next else []
                    if has_next:
                        prepM_y(ti + 1)
                        for b2 in pending:
                            prepM_blk(ti + 1, b2, mode="load")
                    for b in range(nb):
                        q = (ti % 2) * 4 + b
                        for hf in range(2):
                            po = 6 + hf
                            S.op("tensor", f_mms([(pbank(po), mT[:, c * 512 + b * 128: c * 512 + (b + 1) * 128], WOU[:, c * D + hf * 512: c * D + (hf + 1) * 512], c == 0, c == KC - 1)
                                                  for c in range(KC)]), reads=[b_mT, b_WOU], writes=[bank[po]])
                            S.op("vector", f_tt(otmp[hf][:], pbank(po), gtile[:, hf * 512:(hf + 1) * 512], ALU.mult), reads=[bank[po], bgt], writes=[b_otmp[hf]])
                            S.op("vector", f_tt(xq[q][:, hf * 512:(hf + 1) * 512], xq[q][:, hf * 512:(hf + 1) * 512], otmp[hf][:], ALU.add),
                                 reads=[b_xq[q], b_otmp[hf]], writes=[b_xq[q]])
                        S.dma("sync", f_dma(dstd[t0 + b * 128: t0 + (b + 1) * 128, :], xq[q][:]), b_xq[q], reads=[b_xq[q]])
                        if pending:
                            prepM_blk(ti + 1, pending.pop(0), mode="compute")
                    while pending:
                        prepM_blk(ti + 1, pending.pop(0), mode="compute")
            S.barrier()
            if stop == "M":
                return finalize()

            state["top"] = P_END
            WG = alloc("WG", KC * DFF, BF16); b_WG = Buf("WG")
            WV = alloc("WV", KC * DFF, BF16); b_WV = Buf("WV")
            WD = alloc("WD", FC * D, BF16); b_WD = Buf("WD")
            xc = [alloc("xc%d" % i, D, F32) for i in range(4)]; b_xc = [Buf("xc%d" % i) for i in range(4)]
            xh = alloc("xh", D, F32); b_xh = Buf("xh")
            junkc = [alloc("junkc%d" % i, D, BF16) for i in range(2)]; b_junkc = [Buf("jc0"), Buf("jc1")]
            hTc = alloc("hTc", KC * 512, BF16); b_hTc = Buf("hTc")
            hTh = alloc("hTh", KC * 2, BF16); b_hTh = Buf("hTh")
            ghs = alloc("ghs", 2 * FC, F32); b_ghs = Buf("ghs")
            junkh = alloc("junkh", D, BF16); b_junkh = Buf("junkh")
            S.op("gpsimd", f_memset(junkh[:], 0.0), writes=[b_junkh])
            uT = alloc("uT", FC * 512, BF16); b_uT = Buf("uT")
            av = [alloc("av%d" % i, 512, F32) for i in range(2)]; b_av = [Buf("av0"), Buf("av1")]
            sa = [alloc("sa%d" % i, 512, BF16) for i in range(2)]; b_sa = [Buf("sa0"), Buf("sa1")]
            gt2 = [alloc("gt2_%d" % i, D, F32) for i in range(2)]; b_gt2 = [Buf("gt2s"), Buf("gt2c")]
            fgt = gt2[1]; b_fgt = b_gt2[1]
            oc2 = av; b_oc2 = b_av

            S.dma("gpsimd", f_dma(WG[:].rearrange("p (c n) -> p c n", c=KC), w_ffg[l].rearrange("(c p) n -> p c n", p=128)), b_WG, writes=[b_WG])
            S.dma("gpsimd", f_dma(WV[:].rearrange("p (c n) -> p c n", c=KC), w_ffv[l].rearrange("(c p) n -> p c n", p=128)), b_WV, writes=[b_WV])
            S.dma("gpsimd", f_dma(WD[:].rearrange("p (c n) -> p c n", c=FC), w_ffd[l].rearrange("(c p) n -> p c n", p=128)), b_WD, writes=[b_WD])
            if last:
                S.dma("sync", f_dma(fgt[:], final_g.rearrange("(o n) -> o n", o=1).broadcast_to([128, D])), b_fgt, writes=[b_fgt])

            def cv(k, f):
                i = (l * 4 + k) * FC + f
                return cvT[:, i:i + 1]

            srcC_x = xsB
            srcC_c = csB
            dstC_x = y_out if last else xsA
            dstC_c = csA
            for s in range(NSEQ):
                load_gate_tiles([(gt2[0], b_gt2[0], l, s, 5)] + ([] if last else [(gt2[1], b_gt2[1], l, 2, 5)]))
                tilesC = ([] if last else [("ctx", 0, CTX)]) + [("lat", t * 512, 512) for t in range(NT)]

                def tinfo(ti):
                    (kind, t0, n) = tilesC[ti]
                    is_ctx = (kind == "ctx")
                    Lseq = CTX if is_ctx else L
                    src = srcC_c[s] if is_ctx else srcC_x[s]
                    return is_ctx, t0, n, (2 if is_ctx else s), src, (t0 > 0), (t0 + n < Lseq)

                def prepC_blk(ti, b, mode="all"):
                    is_ctx, t0, n, v, src, has_l, has_r = tinfo(ti)
                    norm_block(src[t0 + b * 128: t0 + (b + 1) * 128, :], xc[b][:], b_xc[b], junkc[b % 2][:], b_junkc[b % 2],
                               hTc, b_hTc, b * 128, l, v, 2, 3, b % 2, b % 2, mode=mode)

                def prepC_halo(ti):
                    is_ctx, t0, n, v, src, has_l, has_r = tinfo(ti)
                    if not (has_l or has_r):
                        return
                    if has_l:
                        S.dma("sync", f_dma(xh[0:1, :], src[t0 - 1:t0, :]), b_xh, writes=[b_xh], partial=True)
                    if has_r:
                        S.dma("sync", f_dma(xh[1:2, :], src[t0 + n:t0 + n + 1, :]), b_xh, writes=[b_xh], partial=True)
                    if not (has_l and has_r):
                        if has_l:
                            S.dma("sync", f_dma(xh[1:2, :], src[t0 - 1:t0, :]), b_xh, writes=[b_xh], partial=True)
                        else:
                            S.dma("sync", f_dma(xh[0:1, :], src[t0 + n:t0 + n + 1, :]), b_xh, writes=[b_xh], partial=True)
                    msh = stat[0:2, 8:9]; rsh = stat[0:2, 9:10]
                    S.op("scalar", f_act(junkh[0:2, :], xh[0:2, :], AF.Square, scale=1.0 / 32.0, accum=msh), reads=[b_xh], writes=[b_junkh, b_stat[4]])
                    S.op("scalar", f_act(rsh, msh, AF.Sqrt, bias=epst[0:2, 0:1], scale=1.0), reads=[b_stat[4], b_const], writes=[b_stat[4]])
                    S.op("vector", f_recip(rsh, rsh), reads=[b_stat[4]], writes=[b_stat[4]])
                    S.op("vector", f_ts(junkh[0:2, :], xh[0:2, :], rsh, None, ALU.mult), reads=[b_xh, b_stat[4]], writes=[b_junkh])
                    pt = pbank(0).bitcast(BF16)
                    S.op("tensor", f_transposes([(pt[:, c * 128:(c + 1) * 128], junkh[:, c * 128:(c + 1) * 128], identb) for c in range(KC)]),
                         reads=[b_junkh, b_const], writes=[bank[0]])
                    for c in range(KC):
                        S.op("vector", f_ts(hTh[:, 2 * c:2 * c + 2], pt[:, c * 128:c * 128 + 2], modT_ap(l, v, 3, c), modT_ap(l, v, 2, c), ALU.mult, ALU.add),
                             reads=[bank[0], b_modT], writes=[b_hTh], partial=True)

                def mainC1(ti):
                    is_ctx, t0, n, v, src, has_l, has_r = tinfo(ti)
                    if has_l or has_r:
                        for f in range(FC):
                            S.op("tensor", f_mms([(pbank(5, 2, 2 * f), WG[:, c * DFF + f * 128: c * DFF + (f + 1) * 128], hTh[:, 2 * c:2 * c + 2], c == 0, c == KC - 1) for c in range(KC)]),
                                 reads=[b_hTh, b_WG], writes=[bank[5]], partial=(f > 0))
                        S.op("vector", f_copy(ghs[:], pbank(5, 2 * FC)), reads=[bank[5]], writes=[b_ghs])
                    for f in range(FC):
                        pg = 1 + f % 2
                        pv = 3 + f % 2
                        S.op("tensor", f_mms([(pbank(pg, n), WG[:, c * DFF + f * 128: c * DFF + (f + 1) * 128], hTc[:, c * 512: c * 512 + n], c == 0, c == KC - 1) for c in range(KC)]),
                             reads=[b_hTc, b_WG], writes=[bank[pg]])
                        S.op("tensor", f_mms([(pbank(pv, n), WV[:, c * DFF + f * 128: c * DFF + (f + 1) * 128], hTc[:, c * 512: c * 512 + n], c == 0, c == KC - 1) for c in range(KC)]),
                             reads=[b_hTc, b_WV], writes=[bank[pv]])
                        a_ = av[f % 2]; ba = b_av[f % 2]
                        S.op("vector", f_ts(a_[:, 0:n], pbank(pg, n), cv(1, f), cv(3, f), ALU.mult, ALU.add), reads=[bank[pg], b_const], writes=[ba])
                        S.op("vector", f_stt(a_[:, 1:n], pbank(pg, n - 1), cv(0, f), a_[:, 1:n], ALU.mult, ALU.add), reads=[bank[pg], ba, b_const], writes=[ba])
                        S.op("vector", f_stt(a_[:, 0:n - 1], pbank(pg, n - 1, 1), cv(2, f), a_[:, 0:n - 1], ALU.mult, ALU.add), reads=[bank[pg], ba, b_const], writes=[ba])
                        if has_l:
                            S.op("vector", f_stt(a_[:, 0:1], ghs[:, 2 * f:2 * f + 1], cv(0, f), a_[:, 0:1], ALU.mult, ALU.add), reads=[b_ghs, ba, b_const], writes=[ba])
                        if has_r:
                            S.op("vector", f_stt(a_[:, n - 1:n], ghs[:, 2 * f + 1:2 * f + 2], cv(2, f), a_[:, n - 1:n], ALU.mult, ALU.add), reads=[b_ghs, ba, b_const], writes=[ba])
                        s_ = sa[f % 2]; bs = b_sa[f % 2]
                        S.op("scalar", f_act(s_[:, 0:n], a_[:, 0:n], AF.Silu), reads=[ba], writes=[bs])
                        S.op("vector", f_tt(uT[:, f * 512: f * 512 + n], pbank(pv, n), s_[:, 0:n], ALU.mult), reads=[bank[pv], bs], writes=[b_uT], partial=True)

                def mainC2_blk(ti, b):
                    is_ctx, t0, n, v, src, has_l, has_r = tinfo(ti)
                    dstd = dstC_c[s] if is_ctx else dstC_x[s]
                    gtile = gt2[1] if is_ctx else gt2[0]
                    bgt = b_gt2[1] if is_ctx else b_gt2[0]
                    for hf in range(2):
                        po = 6 + hf
                        S.op("tensor", f_mms([(pbank(po), uT[:, f * 512 + b * 128: f * 512 + (b + 1) * 128], WD[:, f * D + hf * 512: f * D + (hf + 1) * 512], f == 0, f == FC - 1)
                                              for f in range(FC)]), reads=[b_uT, b_WD], writes=[bank[po]])
                        S.op("vector", f_tt(oc2[hf][:], pbank(po), gtile[:, hf * 512:(hf + 1) * 512], ALU.mult), reads=[bank[po], bgt], writes=[b_oc2[hf]])
                        S.op("gpsimd", f_tt(xc[b][:, hf * 512:(hf + 1) * 512], xc[b][:, hf * 512:(hf + 1) * 512], oc2[hf][:], ALU.add),
                             reads=[b_xc[b], b_oc2[hf]], writes=[b_xc[b]])
                    if last and not is_ctx:
                        msf = stat[:, 12:13]; rsf = stat[:, 13:14]
                        S.op("scalar", f_act(junkc[b % 2][:], xc[b][:], AF.Square, scale=1.0 / 32.0, accum=msf), reads=[b_xc[b]], writes=[b_junkc[b % 2], b_stat[6]])
                        S.op("scalar", f_act(rsf, msf, AF.Sqrt, bias=epst[:, 0:1], scale=1.0), reads=[b_stat[6], b_const], writes=[b_stat[6]])
                        S.op("vector", f_recip(rsf, rsf), reads=[b_stat[6]], writes=[b_stat[6]])
                        S.op("vector", f_stt(xc[b][:], xc[b][:], rsf, fgt[:], ALU.mult, ALU.mult), reads=[b_xc[b], b_stat[6], b_fgt], writes=[b_xc[b]])
                    S.dma("sync", f_dma(dstd[t0 + b * 128: t0 + (b + 1) * 128, :], xc[b][:]), b_xc[b], reads=[b_xc[b]])

                for b in range(tilesC[0][2] // 128):
                    prepC_blk(0, b)
                prepC_halo(0)
                for ti in range(len(tilesC)):
                    nb = tilesC[ti][2] // 128
                    mainC1(ti)
                    has_next = ti + 1 < len(tilesC)
                    nbn = tilesC[ti + 1][2] // 128 if has_next else 0
                    pending = list(range(nbn))
                    loaded = set()
                    for b in range(nb):
                        mainC2_blk(ti, b)
                        if b < nbn:
                            prepC_blk(ti + 1, b, mode="load")
                            loaded.add(b)
                        if b >= 2 and pending and pending[0] in loaded:
                            prepC_blk(ti + 1, pending.pop(0), mode="compute")
                    while pending:
                        b2 = pending.pop(0)
                        if b2 not in loaded:
                            prepC_blk(ti + 1, b2, mode="load")
                            loaded.add(b2)
                        prepC_blk(ti + 1, b2, mode="compute")
                    if has_next:
                        prepC_halo(ti + 1)
            S.barrier()

        S.barrier()
        sems = {k: es.enter_context(nc.semaphore(k)) for k in S.semkeys}
        S.run(sems)
    return nc


_WEIGHT_NAMES = ["w_mod", "b_mod", "norm1_g", "norm2_g", "w_in", "w_pool_grp", "pool_scale", "win_sink",
                 "q_norm_g", "k_norm_g", "w_branch", "w_gate", "b_gate", "w_out", "w_ff_gate", "w_ff_val",
                 "conv_w", "conv_b", "w_ff_down", "final_g"]

_NC_CACHE = {}


def make_in_maps(inputs, L, n_cores):
    consts = host_consts(L)
    x = np.ascontiguousarray(np.asarray(inputs["x"], dtype=np.float32))
    c = np.asarray(inputs["c"], dtype=np.float32)
    ctx = np.ascontiguousarray(np.asarray(inputs["ctx"], dtype=np.float32))
    c_ctx = np.asarray(inputs["c_ctx"], dtype=np.float32)
    shared = {k: np.ascontiguousarray(np.asarray(inputs[k], dtype=np.float32)) for k in _WEIGHT_NAMES}
    shared.update(consts)
    maps = []
    for i in range(n_cores):
        m = dict(shared)
        m["x"] = x[NSEQ * i: NSEQ * (i + 1)]
        m["ctx"] = ctx[NSEQ * i: NSEQ * (i + 1)]
        m["c3"] = np.ascontiguousarray(np.concatenate([c[NSEQ * i: NSEQ * (i + 1)], c_ctx[None, :]], axis=0))
        maps.append(m)
    return maps


def kernel(**inputs):
    x = inputs["x"]
    B, L, _ = x.shape
    n_cores = B // NSEQ
    if L not in _NC_CACHE:
        _NC_CACHE[L] = build_nc(L)
    nc = _NC_CACHE[L]
    in_maps = make_in_maps(inputs, L, n_cores)
    res = run_bass_kernel_spmd(nc, in_maps, core_ids=list(range(n_cores)))
    out = np.concatenate([np.asarray(r["y"]) for r in res.results], axis=0)
    return out.astype(np.float32)
```
